# Optimizing a Trainium2 kernel written in Bass

```python
import jax, jax.numpy as jnp
from jax import lax
import numpy as np

D_MODEL = 1024
BATCH = 8
SEQ = 2048
DEPTH = 4
DEC_BATCH = 128
DEC_SEQ = 4
PAST_LEN = 2048
PAGE_SIZE = 128

N_A_LAYERS = DEPTH // 2
N_B_LAYERS = DEPTH - N_A_LAYERS
HEAD_DIM = 64
RWKV_HEADS = D_MODEL // HEAD_DIM
DECAY_LORA = 64
AAA_LORA = 64
MV_LORA = 32
GN_EPS = 64e-5
RMS_EPS = 1e-6
ATT_HEADS = D_MODEL // HEAD_DIM
ATT_WIDTH = ATT_HEADS * HEAD_DIM
WINDOWS = (128, 512, 2048)
DILATIONS = (1, 4, 16)
N_GROUPS = len(WINDOWS)
ATT_SCALE = HEAD_DIM ** -0.5
ROPE_THETA = 10000.0
NEG_INF = -1e30

kernel_name = 'yoco_rwkv7_dilated_swa_step'


def rms_norm(x, g):
    xf = x.astype(jnp.float32)
    y = xf * lax.rsqrt(jnp.mean(xf * xf, axis=-1, keepdims=True) + RMS_EPS)
    return (y * g.astype(jnp.float32)).astype(x.dtype)


def rope(x, pos):
    half = HEAD_DIM // 2
    inv = ROPE_THETA ** (-jnp.arange(half, dtype=jnp.float32) / half)
    ang = pos.astype(jnp.float32)[:, None] * inv[None, :]
    cos = jnp.cos(ang)[:, None, :]
    sin = jnp.sin(ang)[:, None, :]
    xf = x.astype(jnp.float32)
    x1, x2 = xf[..., :half], xf[..., half:]
    return jnp.concatenate([x1 * cos - x2 * sin, x2 * cos + x1 * sin], axis=-1).astype(x.dtype)


def wkv_step(state, inp):
    r, w, k, v, kk, b = inp
    sa = jnp.einsum('bhij,bhj->bhi', state, kk)
    state = state * w[:, :, None, :] - sa[..., None] * b[:, :, None, :] + v[..., None] * k[:, :, None, :]
    return state, jnp.einsum('bhij,bhj->bhi', state, r)


def rwkv7_layer(x, shift_prev, wkv_prev, v_first, ln_g, mu, w_in, w0, w_la, w_lb, a0, a_la, a_lb,
                k_k, k_a, r_k, gn_w, gn_b, w_out, v_res):
    B, T, D = x.shape
    H, C = RWKV_HEADS, HEAD_DIM
    f32 = jnp.float32
    xn = rms_norm(x, ln_g)
    prev = jnp.concatenate([shift_prev[:, None].astype(xn.dtype), xn[:, :-1]], axis=1)
    mixed = xn[None] + mu[:, None, None, :] * (prev - xn)[None]
    rkvz = jnp.einsum('pbtd,pde->pbte', mixed[:4], w_in)
    r = rkvz[0].astype(f32)
    k = rkvz[1].astype(f32)
    v = rkvz[2].astype(f32)
    z = rkvz[3]
    xv = mixed[2].astype(f32)
    xw = mixed[4].astype(f32)
    xa = mixed[5].astype(f32)
    w_log = -jax.nn.softplus(-(w0 + jnp.tanh(xw @ w_la) @ w_lb)) - 0.5
    decay = jnp.exp(-jnp.exp(w_log))
    if v_res is not None:
        v0, v_la, v_lb = v_res
        v = v + (v_first - v) * jax.nn.sigmoid(v0 + (xv @ v_la) @ v_lb)
    a = jax.nn.sigmoid(a0 + (xa @ a_la) @ a_lb)
    kk = (k * k_k).reshape(B, T, H, C)
    kk = kk * lax.rsqrt(jnp.maximum(jnp.sum(kk * kk, axis=-1, keepdims=True), 1e-24))
    k = k * (1.0 + (a - 1.0) * k_a)
    rh = r.reshape(B, T, H, C)
    wh = decay.reshape(B, T, H, C)
    kh = k.reshape(B, T, H, C)
    vh = v.reshape(B, T, H, C)
    ah = a.reshape(B, T, H, C)
    seq = tuple(jnp.swapaxes(t, 0, 1) for t in (rh, wh, kh, vh, kk, kk * ah))
    wkv, ys = lax.scan(wkv_step, wkv_prev.astype(f32), seq)
    y = jnp.swapaxes(ys, 0, 1)
    y_mean = jnp.mean(y, axis=-1, keepdims=True)
    y_var = jnp.mean(jnp.square(y - y_mean), axis=-1, keepdims=True)
    y = ((y - y_mean) * lax.rsqrt(y_var + GN_EPS)).reshape(B, T, D) * gn_w + gn_b
    y = y + (jnp.sum(rh * kh * r_k, axis=-1, keepdims=True) * vh).reshape(B, T, D)
    out = (y.astype(x.dtype) * jax.nn.silu(z)) @ w_out
    return x + out, xn[:, -1], wkv, v


def rwkv7_stack(x, shift_init, wkv_init, ln, mu, w_in, w0, w_la, w_lb, a0, a_la, a_lb,
                v0, v_la, v_lb, k_k, k_a, r_k, gn_w, gn_b, w_out):
    shifts, states = [], []
    v_first = None
    for l in range(N_A_LAYERS):
        v_res = None if l == 0 else (v0[l - 1], v_la[l - 1], v_lb[l - 1])
        x, sh, st, v = rwkv7_layer(x, shift_init[l], wkv_init[l], v_first, ln[l], mu[l], w_in[l], w0[l],
                                   w_la[l], w_lb[l], a0[l], a_la[l], a_lb[l], k_k[l], k_a[l], r_k[l],
                                   gn_w[l], gn_b[l], w_out[l], v_res)
        if l == 0:
            v_first = v
        shifts.append(sh)
        states.append(st)
    return x, jnp.stack(shifts), jnp.stack(states)


def shared_kv(h, pos, ln_g, w_kv):
    B, T, _ = h.shape
    kv = (rms_norm(h, ln_g) @ w_kv).reshape(B, T, 2, N_GROUPS * ATT_HEADS, HEAD_DIM)
    k = rope(kv[:, :, 0], pos).reshape(B, T, N_GROUPS, ATT_HEADS, HEAD_DIM)
    v = kv[:, :, 1].reshape(B, T, N_GROUPS, ATT_HEADS, HEAD_DIM)
    return k, v


def dilated_band_attention(q, k, v, window, dilation):
    B, S, H, C = q.shape
    n = window // dilation
    L = S // dilation
    nb = -(-L // n)
    Lp = nb * n

    def strided(t):
        t = jnp.swapaxes(t.reshape(B, L, dilation, H, C), 1, 2)
        t = jnp.pad(t, ((0, 0), (0, 0), (0, Lp - L), (0, 0), (0, 0)))
        return t.reshape(B, dilation, nb, n, H, C)

    def with_prev(t):
        prev = jnp.pad(t, ((0, 0), (0, 0), (1, 0), (0, 0), (0, 0), (0, 0)))[:, :, :nb]
        return jnp.concatenate([prev, t], axis=3)

    qs = strided(q)
    kb = with_prev(strided(k))
    vb = with_prev(strided(v))
    s = jnp.einsum('bdnqhc,bdnkhc->bdnhqk', qs, kb, preferred_element_type=jnp.float32) * ATT_SCALE
    qi = jnp.arange(n)[:, None]
    ki = jnp.arange(2 * n)[None, :]
    rel = n + qi - ki
    blk = jnp.arange(nb)[:, None, None]
    valid = (rel >= 0) & (rel < n) & (blk * n + ki >= n)
    s = jnp.where(valid[None, None, :, None], s, NEG_INF)
    m = jnp.max(s, axis=-1, keepdims=True)
    e = jnp.exp(s - m)
    den = jnp.sum(e, axis=-1)
    o = jnp.einsum('bdnhqk,bdnkhc->bdnqhc', e, vb.astype(jnp.float32)) / jnp.swapaxes(den, 3, 4)[..., None]
    lse = jnp.swapaxes(m[..., 0] + jnp.log(den), 3, 4)

    def unstrided(t):
        t = t.reshape((B, dilation, Lp) + t.shape[4:])[:, :, :L]
        return jnp.swapaxes(t, 1, 2).reshape((B, S) + t.shape[3:])

    return unstrided(o), unstrided(lse)


def gather_strided(k_all, v_all, n_past, n_new, window, dilation):
    n = window // dilation
    idx = n_past + jnp.arange(n_new)[:, None] - dilation * jnp.arange(n)[None, :]
    valid = idx >= 0
    idx = jnp.maximum(idx, 0)
    return k_all[:, idx], v_all[:, idx], valid


def dilated_gathered_attention(q, kg, vg, valid):
    s = jnp.einsum('bthc,btkhc->bthk', q, kg, preferred_element_type=jnp.float32) * ATT_SCALE
    s = jnp.where(valid[None, :, None, :], s, NEG_INF)
    m = jnp.max(s, axis=-1, keepdims=True)
    e = jnp.exp(s - m)
    den = jnp.sum(e, axis=-1)
    o = jnp.einsum('bthk,btkhc->bthc', e, vg.astype(jnp.float32)) / den[..., None]
    return o, m[..., 0] + jnp.log(den)


def dilated_layer(x, pos, ln_g, w_in, w_out, attend):
    B, T, _ = x.shape
    xn = rms_norm(x, ln_g)
    proj = xn @ w_in
    q_all = proj[..., :N_GROUPS * ATT_WIDTH].reshape(B, T, N_GROUPS * ATT_HEADS, HEAD_DIM)
    q_all = rope(q_all, pos).reshape(B, T, N_GROUPS, ATT_HEADS, HEAD_DIM)
    z = proj[..., N_GROUPS * ATT_WIDTH:]
    outs, lses = [], []
    for g in range(N_GROUPS):
        o, l = attend(g, q_all[:, :, g])
        outs.append(o)
        lses.append(l)
    alpha = jax.nn.softmax(jnp.stack(lses), axis=0)
    o = jnp.einsum('gbth,gbthc->bthc', alpha, jnp.stack(outs)).reshape(B, T, ATT_WIDTH)
    return x + (o.astype(x.dtype) * jax.nn.silu(z)) @ w_out


def setup_inputs(seed: int = 0) -> dict:
    key = jax.random.key(seed)
    keys = iter(jax.random.split(key, 40))
    f32 = jnp.float32
    D, H, C = D_MODEL, RWKV_HEADS, HEAD_DIM
    NA, NB, G = N_A_LAYERS, N_B_LAYERS, N_GROUPS

    def normal(shape, scale):
        return scale * jax.random.normal(next(keys), shape, f32)

    def gain(shape):
        return 1.0 + 0.02 * jax.random.normal(next(keys), shape, f32)

    inp = {}
    inp['x_prompt'] = normal((BATCH, SEQ, D), 1.0)
    inp['x_sample'] = normal((DEC_BATCH, DEC_SEQ, D), 1.0)
    inp['state_wkv'] = normal((NA, DEC_BATCH, H, C, C), 0.3)
    inp['state_shift'] = normal((NA, DEC_BATCH, D), 1.0)
    inp['cache_kv_g0'] = normal((DEC_BATCH, min(WINDOWS[0], PAST_LEN), 2, ATT_HEADS, C), 1.0)
    inp['cache_kv_g1'] = normal((DEC_BATCH, min(WINDOWS[1], PAST_LEN), 2, ATT_HEADS, C), 1.0)
    inp['cache_kv_g2'] = normal((DEC_BATCH, min(WINDOWS[2], PAST_LEN), 2, ATT_HEADS, C), 1.0)
    inp['a_ln'] = gain((NA, D))
    inp['a_mu'] = jax.random.uniform(next(keys), (NA, 6, D), f32)
    inp['a_w_in'] = normal((NA, 4, D, D), D ** -0.5)
    inp['a_w0'] = jax.random.uniform(next(keys), (NA, D), f32, -3.0, 0.0)
    inp['a_w_lora_a'] = normal((NA, D, DECAY_LORA), D ** -0.5)
    inp['a_w_lora_b'] = normal((NA, DECAY_LORA, D), 0.1 * DECAY_LORA ** -0.5)
    inp['a_a0'] = normal((NA, D), 0.1)
    inp['a_a_lora_a'] = normal((NA, D, AAA_LORA), D ** -0.5)
    inp['a_a_lora_b'] = normal((NA, AAA_LORA, D), 0.1 * AAA_LORA ** -0.5)
    inp['a_v0'] = normal((NA - 1, D), 0.1)
    inp['a_v_lora_a'] = normal((NA - 1, D, MV_LORA), D ** -0.5)
    inp['a_v_lora_b'] = normal((NA - 1, MV_LORA, D), 0.1 * MV_LORA ** -0.5)
    inp['a_k_k'] = 0.85 + 0.02 * jax.random.normal(next(keys), (NA, D), f32)
    inp['a_k_a'] = gain((NA, D))
    inp['a_r_k'] = normal((NA, H, C), 0.1)
    inp['a_gn_w'] = gain((NA, D))
    inp['a_gn_b'] = normal((NA, D), 0.02)
    inp['a_w_out'] = normal((NA, D, D), D ** -0.5)
    inp['kv_ln'] = gain((D,))
    inp['w_kv'] = normal((D, 2 * G * ATT_WIDTH), D ** -0.5)
    inp['b_ln'] = gain((NB, D))
    inp['b_w_in'] = normal((NB, D, (G + 1) * ATT_WIDTH), D ** -0.5)
    inp['b_w_out'] = normal((NB, ATT_WIDTH, D), ATT_WIDTH ** -0.5)
    inp['final_ln'] = gain((D,))
    return inp


def reference(x_prompt, x_sample, state_wkv, state_shift, cache_kv_g0, cache_kv_g1, cache_kv_g2,
              a_ln, a_mu, a_w_in, a_w0, a_w_lora_a, a_w_lora_b, a_a0, a_a_lora_a, a_a_lora_b,
              a_v0, a_v_lora_a, a_v_lora_b, a_k_k, a_k_a, a_r_k, a_gn_w, a_gn_b, a_w_out,
              kv_ln, w_kv, b_ln, b_w_in, b_w_out, final_ln):
    a_weights = (a_ln, a_mu, a_w_in, a_w0, a_w_lora_a, a_w_lora_b, a_a0, a_a_lora_a, a_a_lora_b,
                 a_v0, a_v_lora_a, a_v_lora_b, a_k_k, a_k_a, a_r_k, a_gn_w, a_gn_b, a_w_out)
    Bp, S, _ = x_prompt.shape
    T = x_sample.shape[1]
    pos_p = jnp.arange(S, dtype=jnp.int32)
    pos_s = PAST_LEN + jnp.arange(T, dtype=jnp.int32)

    zero_shift = jnp.zeros((N_A_LAYERS, Bp, D_MODEL), x_prompt.dtype)
    zero_wkv = jnp.zeros((N_A_LAYERS, Bp, RWKV_HEADS, HEAD_DIM, HEAD_DIM), jnp.float32)
    hp, shift_p, wkv_p = rwkv7_stack(x_prompt, zero_shift, zero_wkv, *a_weights)
    hs, shift_s, wkv_s = rwkv7_stack(x_sample, state_shift, state_wkv, *a_weights)

    kp, vp = shared_kv(hp, pos_p, kv_ln, w_kv)
    ks, vs = shared_kv(hs, pos_s, kv_ln, w_kv)

    caches = (cache_kv_g0, cache_kv_g1, cache_kv_g2)
    gathered = []
    for g in range(N_GROUPS):
        cache = caches[g]
        k_all = jnp.concatenate([cache[:, :, 0].astype(ks.dtype), ks[:, :, g]], axis=1)
        v_all = jnp.concatenate([cache[:, :, 1].astype(vs.dtype), vs[:, :, g]], axis=1)
        gathered.append(gather_strided(k_all, v_all, cache.shape[1], T, WINDOWS[g], DILATIONS[g]))

    def attend_prompt(g, q):
        return dilated_band_attention(q, kp[:, :, g], vp[:, :, g], WINDOWS[g], DILATIONS[g])

    def attend_sample(g, q):
        kg, vg, valid = gathered[g]
        return dilated_gathered_attention(q, kg, vg, valid)

    for l in range(N_B_LAYERS):
        hp = dilated_layer(hp, pos_p, b_ln[l], b_w_in[l], b_w_out[l], attend_prompt)
        hs = dilated_layer(hs, pos_s, b_ln[l], b_w_in[l], b_w_out[l], attend_sample)

    y_prompt = rms_norm(hp, final_ln)
    y_sample = rms_norm(hs, final_ln)

    kv_p = []
    kv_s = []
    for g in range(N_GROUPS):
        wc = min(WINDOWS[g], S)
        kv_p.append(jnp.stack([kp[:, S - wc:, g], vp[:, S - wc:, g]], axis=2))
        kv_s.append(jnp.stack([ks[:, :, g], vs[:, :, g]], axis=2))
    return (y_prompt, y_sample, wkv_p, wkv_s, shift_p, shift_s,
            kv_p[0], kv_s[0], kv_p[1], kv_s[1], kv_p[2], kv_s[2])
```

```python
import contextlib
import math
import numpy as np
import concourse.bass as bass
import concourse.mybir as mybir
from concourse.bass_utils import run_bass_kernel_spmd

F32 = mybir.dt.float32
BF16 = mybir.dt.bfloat16
AF = mybir.ActivationFunctionType
ALU = mybir.AluOpType
AX = mybir.AxisListType

NCORES = 8
D = 1024
NCH = 8
SEQ = 2048
NTILE = 16
SB = 16
ST_ = 4
NS = SB * ST_
RMS_EPS = 1e-6
GN_EPS = 64e-5
CDEC = math.exp(-0.5)
SEM_LIMIT = 12000
import os as _os
TRACE = bool(_os.environ.get("K_TRACE"))


class Buf:
    __slots__ = ("name", "w", "r")

    def __init__(self, name):
        self.name = name
        self.w = None
        self.r = []


class Op:
    __slots__ = ("eng", "fn", "deps", "is_dma", "ms", "sem", "val", "line")

    def __init__(self, eng, fn, is_dma):
        self.eng = eng
        self.fn = fn
        self.is_dma = is_dma
        self.deps = []
        self.ms = False
        self.sem = None
        self.val = 0


class Sched:
    ENGS = ("pe", "act", "dve", "pool", "sp")

    def __init__(self, nc):
        self.nc = nc
        self.ops = {e: [] for e in self.ENGS}
        self.dma_cnt = {}
        self.all_dma = []

    def add(self, eng, fn, reads=(), writes=(), dma=False, key=None, extra=()):
        op = Op(eng, fn, dma)
        if TRACE:
            import sys as _sys
            f = _sys._getframe(1)
            ls = []
            while f is not None and len(ls) < 4:
                ls.append(f.f_lineno)
                f = f.f_back
            op.line = ls
        for d in extra:
            op.deps.append(d)
            d.ms = True
        deps = {}
        for b in reads:
            if b.w is not None:
                deps[id(b.w)] = (b.w, True)
        for b in writes:
            if b.w is not None and id(b.w) not in deps:
                deps[id(b.w)] = (b.w, False)
            for r in b.r:
                if id(r) not in deps:
                    deps[id(r)] = (r, False)
        for d, raw in deps.values():
            if d.is_dma:
                op.deps.append(d)
                d.ms = True
            elif d.eng == eng and not dma:
                if raw and eng != "pe":
                    op.deps.append(d)
                    d.ms = True
            else:
                op.deps.append(d)
                d.ms = True
        for b in reads:
            b.r.append(op)
        for b in writes:
            b.w = op
            b.r = []
        if dma:
            c = self.dma_cnt.get(id(key), (key, 0))[1] + 1
            self.dma_cnt[id(key)] = (key, c)
            op.sem = ("dma", id(key))
            op.val = 16 * c
            self.all_dma.append(op)
        self.ops[eng].append(op)
        return op

    def fence(self):
        lasts = []
        for e in self.ENGS:
            for op in reversed(self.ops[e]):
                if not op.is_dma:
                    lasts.append(op)
                    break
        dl = {}
        for op in self.all_dma:
            dl[op.sem] = op
        for e in self.ENGS:
            extra = [o for o in lasts if o.eng != e] + list(dl.values())
            self.add(e, lambda eng: eng.nop(), extra=extra)

    def emit(self):
        nc = self.nc
        nsem_eng = {}
        for e in self.ENGS:
            cnt = 0
            epoch = 0
            for op in self.ops[e]:
                if op.is_dma or not op.ms:
                    continue
                cnt += 1
                if cnt > SEM_LIMIT:
                    epoch += 1
                    cnt = 1
                op.sem = ("eng", e, epoch)
                op.val = cnt
            nsem_eng[e] = epoch + 1
        sems = {}
        with contextlib.ExitStack() as es:
            for e in self.ENGS:
                for k in range(nsem_eng[e]):
                    sems[("eng", e, k)] = es.enter_context(nc.semaphore(f"s_{e}{k}"))
            for kid, (key, c) in self.dma_cnt.items():
                sems[("dma", kid)] = es.enter_context(nc.semaphore(f"d_{key.name}"))
            last_dma = {}
            for op in self.all_dma:
                last_dma[op.sem] = op
            block = es.enter_context(nc.Block())

            def run(ename, eng):
                waited = {}
                nops = len(self.ops[ename])
                for io, op in enumerate(self.ops[ename]):
                    if TRACE and io >= nops - 25:
                        print("TR", ename, io, op.line, "inc", op.sem[1:] if op.sem else None, op.val, "ms", op.ms,
                              "waits", [(d.eng, d.sem[1:], d.val, d.line) for d in op.deps if waited.get(d.sem, 0) < d.val])
                    for d in op.deps:
                        if waited.get(d.sem, 0) >= d.val:
                            continue
                        waited[d.sem] = d.val
                        eng.wait_ge(sems[d.sem], d.val)
                    ins = op.fn(eng)
                    if op.is_dma:
                        ins.then_inc(sems[op.sem], 16)
                    elif op.ms:
                        ins.then_inc(sems[op.sem], 1)
                if ename == "sp":
                    for s, op in last_dma.items():
                        if waited.get(s, 0) < op.val:
                            eng.wait_ge(sems[s], op.val)

            @block.tensor
            def _(eng):
                run("pe", eng)

            @block.scalar
            def _(eng):
                run("act", eng)

            @block.vector
            def _(eng):
                run("dve", eng)

            @block.gpsimd
            def _(eng):
                run("pool", eng)

            @block.sync
            def _(eng):
                run("sp", eng)


class T:
    def __init__(self, h, name):
        self.h = h
        self.b = Buf(name)

    def __getitem__(self, k):
        return self.h[k]


class Vw:
    def __init__(self, t, ap):
        self.b = t.b
        self.ap = ap

    def __getitem__(self, k):
        return self.ap[k]


class Al:
    def __init__(self, name, ap):
        self.b = Buf(name)
        self.ap = ap

    def __getitem__(self, k):
        return self.ap[k]


def bc(ap, shape):
    return ap.broadcast_to(shape)


V_LN, V_MU, V_W0, V_A0, V_V0, V_KK, V_KA, V_RK, V_GNW, V_GNB = 0, 1, 7, 8, 9, 10, 11, 12, 13, 14
V_NW0, V_NA0, V_NV0, V_OMK = 15, 16, 17, 18
NV = 20


class _Stop(Exception):
    pass


class Builder:
    def __init__(self, stage="full"):
        import os
        self.stop = int(os.environ.get("K_STOP", "-1"))
        self.stop_at = tuple(int(v) for v in os.environ.get("K_AT", "0,0").split(","))
        self.cur = (0, 0)
        self.ckcnt = 0
        self.stop_cnt = int(os.environ.get("K_CNT", "1"))
        self.stage = stage
        self.nc = bass.Bass("TRN2", target_bir_lowering=False)
        self.s = Sched(self.nc)
        self.es = contextlib.ExitStack()
        self.gflip = 0
        self.aflip = 0
        self.nflip = 0
        self.uid = 0
        self.bank_last = {}
        self.bank_round = {}

    def sb(self, name, shape, dt):
        return T(self.es.enter_context(self.nc.sbuf_tensor(name, list(shape), dt)), name)

    def din(self, name, shape, dt=F32):
        return self.nc.dram_tensor(name, list(shape), dt, kind="ExternalInput").ap()

    def dout(self, name, shape, dt=F32):
        return self.nc.dram_tensor(name, list(shape), dt, kind="ExternalOutput").ap()

    def dint(self, name, shape, dt=F32):
        return self.nc.dram_tensor(name, list(shape), dt, kind="Internal").ap()

    def ck(self, n):
        if self.stop == n and self.cur == self.stop_at:
            self.ckcnt += 1
            if self.ckcnt == self.stop_cnt:
                raise _Stop()

    def A(self, eng, fn, r=(), w=()):
        w = list(w) + [x for x in r if getattr(x, "psum", False) and x not in w]
        self.s.add(eng, fn, reads=[x.b if isinstance(x, (T, Vw, Al)) else x for x in r],
                   writes=[x.b if isinstance(x, (T, Vw, Al)) else x for x in w])

    def DMA(self, eng, out, in_, key, r=(), w=()):
        self.s.add(eng, lambda e: e.dma_start(out=out, in_=in_),
                   reads=[x.b if isinstance(x, (T, Vw, Al)) else x for x in r],
                   writes=[x.b if isinstance(x, (T, Vw, Al)) else x for x in w], dma=True,
                   key=key.b if isinstance(key, (T, Vw, Al)) else key)

    def new_round(self, bank):
        self.bank_round[id(bank)] = set()

    def MM(self, bank, out, lhsT, rhs, kbase, ksize, qbase, qsize, r=()):
        rows = set(range(kbase // 32, (kbase + ksize + 31) // 32))
        quads = set(range(qbase // 32, (qbase + qsize + 31) // 32))
        cleared = self.bank_round.setdefault(id(bank), set())
        if quads <= cleared:
            start = False
        else:
            assert not (quads & cleared), (quads, cleared)
            start = True
            cleared |= quads
        last = self.bank_last.get(id(bank))
        extra = []
        if last is not None and not (last[1] & rows):
            extra.append(last[0])
        fn = lambda e: e.matmul(out=out, lhsT=lhsT, rhs=rhs, start=start, stop=True, skip_group_check=True)
        op = self.s.add("pe", fn, reads=[x.b for x in r], writes=[bank.b], extra=extra)
        self.bank_last[id(bank)] = (op, rows)
        return op

    def nextG(self):
        self.gflip ^= 1
        return self.psG[self.gflip]

    def nextA(self):
        self.aflip ^= 1
        return self.psA[self.aflip]

    def nextN(self):
        self.nflip ^= 1
        return self.psN[self.nflip]

    def build(self):
        nc = self.nc
        with self.es:
            self._declare_io()
            self._alloc()
            self._load_consts()
            try:
                for l in range(2):
                    self.layer_a(l)
                if self.stage != "A":
                    self.kv_phase()
                if self.stage not in ("A", "KV"):
                    for lb in range(2):
                        self.layer_b(lb)
            except _Stop:
                import os
                names = [n for n in os.environ.get("K_DUMP", "").split(",") if n]
                for i, n in enumerate(names):
                    t = getattr(self, n)
                    ap = t.h[:] if isinstance(t, T) else t.ap
                    shp = list(ap.shape)
                    dst = self.dout(f"dbg{i}", shp)
                    self.DMA("pool", dst, ap, t, r=[t])
            self.s.emit()
        return nc

    def _declare_io(self):
        I = self.din
        self.xp = I("xp", [SEQ, D])
        self.xs = I("xs", [NS, D])
        self.swkv = I("swkv", [2, SB, 16, 64, 64])
        self.sshift = I("sshift", [2, SB, D])
        self.a_vec = I("a_vec", [2, 128, NV, 8])
        self.a_w_in = I("a_w_in", [2, 4, D, D])
        self.a_w_out = I("a_w_out", [2, D, D])
        self.a_wla = I("a_w_lora_a", [2, D, 64])
        self.a_wlb = I("a_w_lora_b", [2, 64, D])
        self.a_ala = I("a_a_lora_a", [2, D, 64])
        self.a_alb = I("a_a_lora_b", [2, 64, D])
        self.a_vla = I("a_v_lora_a", [1, D, 32])
        self.a_vlb = I("a_v_lora_b", [1, 32, D])
        self.b_vec = I("b_vec", [128, 4, NCH])
        self.w_kv = I("w_kv", [D, 6 * D])
        self.b_w_in = I("b_w_in", [2, D, 4 * D])
        self.b_w_out = I("b_w_out", [2, D, D])
        self.final_ln = I("final_ln", [D])
        self.c_ropeP = I("c_ropeP", [128, NTILE + 1, 2, 32])
        self.c_ropeT = I("c_ropeT", [128, 2, SEQ + NS])
        self.c_maskP = I("c_maskP", [128, 2, 128])
        self.c_prot = I("c_prot", [128, 128])
        self.c_maskS = I("c_maskS", [128, 12, ST_])
        self.ckv = [I(f"ckv{g}", [SB, w, 2, D]) for g, w in enumerate((128, 512, 2048))]
        self.c_ident = I("c_ident", [128, 128])
        self.c_maskA = I("c_maskA", [128, 4, 128])
        self.c_maskAT = I("c_maskAT", [128, 128])
        self.c_blk = I("c_blk", [128, 128])
        self.c_scan = I("c_scan", [128, 2, 128])
        O = self.dout
        self.o_yp = O("y_prompt", [SEQ, D])
        self.o_ys = O("y_sample", [NS, D])
        self.o_wkvp = O("wkv_p", [2, 16, 64, 64])
        self.o_wkvs = O("wkv_s", [2, SB, 16, 64, 64])
        self.o_shp = O("shift_p", [2, D])
        self.o_shs = O("shift_s", [2, SB, D])
        self.o_kvp = [O(f"kv{g}p", [w, 2, D]) for g, w in enumerate((128, 512, 2048))]
        self.o_kvs = [O(f"kv{g}s", [NS, 2, D]) for g in range(3)]
        self.d_kT = self.dint("d_kT", [3, NCH, 128, SEQ + NS], BF16)
        self.d_v = self.dint("d_v", [3, SEQ + NS, D], BF16)
        self.kTbuf = [Buf(f"kTd{g}") for g in range(3)]
        self.vbuf = [Buf(f"vd{g}") for g in range(3)]
        self.d_vf = self.dint("d_vf", [NTILE + 1, 128, NCH, 128])
        self.xbuf = [Buf(f"xd{i}") for i in range(NTILE + 1)]
        self.vfbuf = [Buf(f"vfd{i}") for i in range(NTILE + 1)]

    def x_dram(self, ti):
        if ti < NTILE:
            return self.o_yp[ti * 128:(ti + 1) * 128, :]
        return self.o_ys[:, :]

    def x_in(self, ti):
        if ti < NTILE:
            return self.xp[ti * 128:(ti + 1) * 128, :]
        return self.xs[:, :]

    def _alloc(self):
        sb = self.sb
        nc = self.nc
        banks = [T(self.es.enter_context(nc.psum_tensor(f"ps{i}", [128, 512], F32)), f"ps{i}") for i in range(8)]
        for bk_ in banks:
            bk_.psum = True
        self.psG = banks[0:2]
        self.psA = banks[2:4]
        self.psN = banks[4:6]
        self.psX = banks[6]
        self.psY = banks[7]
        self.identf = sb("identf", [128, 128], F32)
        self.identb = sb("identb", [128, 128], BF16)
        self.maskA = sb("maskA", [128, 4, 128], BF16)
        self.maskAT = sb("maskAT", [128, 128], BF16)
        self.blkf = sb("blkf", [128, 128], F32)
        self.blkb = sb("blkb", [128, 128], BF16)
        self.scanm = sb("scanm", [128, 2, 128], F32)
        self.W = [sb(f"W{p}", [128, NCH, D], BF16) for p in range(5)]
        self.Wla = sb("Wla", [128, NCH, 64], BF16)
        self.Ala = sb("Ala", [128, NCH, 64], BF16)
        self.Vla = sb("Vla", [128, NCH, 32], BF16)
        self.Wlb = sb("Wlb", [64, D], BF16)
        self.Alb = sb("Alb", [64, D], BF16)
        self.Vlb = sb("Vlb", [32, D], BF16)
        self.VEC = sb("VEC", [128, NV, NCH], F32)
        self.BV = sb("BV", [128, 4, NCH], F32)
        self.ropeS = sb("ropeS", [128, 2, 32], F32)
        self.maskP = sb("maskP", [128, 2, 128], BF16)
        self.ones_b = sb("ones_b", [128, 64], BF16)
        self.prot = sb("prot", [128, 128], BF16)
        self.XNs = sb("XNs", [128, NCH, NS], BF16)
        self.OGs = sb("OGs", [128, NCH, NS], BF16)
        self.QTs = sb("QTs", [128, NCH, 3, NS], BF16)
        self.zTs = sb("zTs", [128, NCH, NS], BF16)
        self.KTs = sb("KTs", [128, 3, NCH, NS], BF16)
        self.ropeTs = sb("ropeTs", [128, 2, NS], BF16)
        self.Xt = sb("Xt", [128, D], F32)
        self.junk = sb("junk", [128, D], BF16)
        self.st1 = sb("st1", [128, 4], F32)
        FM = lambda n, dt=F32: sb(n, [128, NCH, 128], dt)
        self.F = [FM(f"F{i}") for i in range(8)]
        F = self.F
        flat = lambda t: Vw(t, t.h[:, :, :].rearrange("p a b -> p (a b)"))
        st4 = lambda t: Vw(t, t.h[0:64, :, :].rearrange("p a (b c) -> p a b c", c=64))
        self.xr = flat(F[7])
        self.xnT = F[4]
        self.dxT = F[5]
        self.carry = sb("carry", [128, NCH, 1], F32)
        self.shiftT = sb("shiftT", [128, NCH, SB], F32)
        self.shtok = flat(F[0])
        self.rowo = flat(F[0])
        self.MX = [FM(f"MX{i}", BF16) for i in range(2)]
        self.rT = FM("rT")
        self.kT = FM("kT")
        self.vT = FM("vT")
        self.vfT = F[6]
        self.szT = FM("szT", BF16)
        self.hid = sb("hid", [64, 128], F32)
        self.hidb = sb("hidb", [64, 128], BF16)
        self.hids = [(self.hid, self.hidb), (sb("hid2", [64, 128], F32), sb("hidb2", [64, 128], BF16))]
        self.AR = sb("AR", [128, NCH, 2, 128], BF16)
        self.BK = sb("BK", [128, NCH, 2, 128], BF16)
        self.vTb = FM("vTb", BF16)
        self.BhT = FM("BhT", BF16)
        self.KhT = FM("KhT", BF16)
        self.Vt = sb("Vt", [128, D], BF16)
        self.Bh = sb("Bh", [128, D], BF16)
        self.Kh = sb("Kh", [128, D], BF16)
        self.WL = sb("WL", [128, NCH, SB], F32)
        self.Asb = sb("Asb", [128, 8, 4, 128], BF16)
        self.Nsb = [sb(f"Nsb{i}", [128, 8, 2, 128], BF16) for i in range(2)]
        self.Xb = sb("Xb", [128, 8, 64], BF16)
        self.STf = sb("STf", [128, NCH, 64], F32)
        self.STb = sb("STb", [128, NCH, 64], BF16)
        self.STfG = [Al(f"STfG{i}", self.STf.h[:, 2 * i:2 * i + 2, :]) for i in range(4)]
        self.STbG = [Al(f"STbG{i}", self.STb.h[:, 2 * i:2 * i + 2, :]) for i in range(4)]
        self.SI = st4(F[3])
        self.SO = st4(F[1])
        self.YT = F[6]
        self.OT = FM("OT", BF16)

    def _load_consts(self):
        D_ = self.DMA
        D_("sp", self.identf[:], self.c_ident[:, :], self.identf, w=[self.identf])
        D_("pool", self.identb[:], self.c_ident[:, :], self.identb, w=[self.identb])
        D_("pool", self.maskA[:], self.c_maskA[:, :, :], self.maskA, w=[self.maskA])
        D_("pool", self.maskAT[:], self.c_maskAT[:, :], self.maskAT, w=[self.maskAT])
        D_("sp", self.blkf[:], self.c_blk[:, :], self.blkf, w=[self.blkf])
        D_("pool", self.blkb[:], self.c_blk[:, :], self.blkb, w=[self.blkb])
        D_("sp", self.scanm[:], self.c_scan[:, :, :], self.scanm, w=[self.scanm])
        D_("sp", self.BV[:], self.b_vec[:, :, :], self.BV, w=[self.BV])
        D_("sp", self.ropeS[:], self.c_ropeP[:, NTILE, :, :], self.ropeS, w=[self.ropeS])
        D_("pool", self.maskP[:], self.c_maskP[:, :, :], self.maskP, w=[self.maskP])
        D_("pool", self.prot[:], self.c_prot[:, :], self.prot, w=[self.prot])
        self.A("pool", lambda e: e.memset(self.ones_b[:], 1.0), w=[self.ones_b])

    def load_layer_a_weights(self, l):
        D_ = self.DMA
        for p in range(5):
            src = self.a_w_in[l, p] if p < 4 else self.a_w_out[l]
            srcv = src.rearrange("(c p) e -> p c e", p=128)
            for c in range(NCH):
                D_("pool", self.W[p][:, c, :], srcv[:, c, :], self.W[p], w=[self.W[p]])
        D_("pool", self.Wla[:], self.a_wla[l].rearrange("(c p) k -> p c k", p=128), self.Wla, w=[self.Wla])
        D_("pool", self.Ala[:], self.a_ala[l].rearrange("(c p) k -> p c k", p=128), self.Ala, w=[self.Ala])
        D_("pool", self.Wlb[:], self.a_wlb[l], self.Wlb, w=[self.Wlb])
        D_("pool", self.Alb[:], self.a_alb[l], self.Alb, w=[self.Alb])
        if l == 1:
            D_("pool", self.Vla[:], self.a_vla[0].rearrange("(c p) k -> p c k", p=128), self.Vla, w=[self.Vla])
            D_("pool", self.Vlb[:], self.a_vlb[0], self.Vlb, w=[self.Vlb])
        D_("sp", self.VEC[:], self.a_vec[l], self.VEC, w=[self.VEC])
        V = self.VEC
        A = self.A
        A("dve", lambda e: e.tensor_scalar(out=V[:, V_NW0:V_NW0 + 3, :], in0=V[:, V_W0:V_W0 + 3, :], scalar1=-1.0,
                                           scalar2=None, op0=ALU.mult), r=[V], w=[V])
        A("dve", lambda e: e.tensor_scalar(out=V[:, V_OMK, :], in0=V[:, V_KA, :], scalar1=-1.0, scalar2=1.0,
                                           op0=ALU.mult, op1=ALU.add), r=[V], w=[V])

    def layer_a(self, l):
        A = self.A
        self.load_layer_a_weights(l)
        A("pool", lambda e: e.memset(self.STf[:], 0.0), w=self.STfG)
        A("pool", lambda e: e.memset(self.STb[:], 0.0), w=self.STbG)
        A("pool", lambda e: e.memset(self.carry[:], 0.0), w=[self.carry])
        for ti in range(NTILE):
            self.tile_a(l, ti, 128, sample=False)
        self.store_state(self.o_wkvp[l])
        self.store_shift_prompt(l)
        self.load_shift_sample(l)
        self.tile_a(l, NTILE, NS, sample=True)

    def store_state(self, dst):
        A = self.A
        for half in range(2):
            ps = self.nextG()
            psv = ps[:, :].rearrange("p (a b) -> p a b", b=128)
            for q in range(4):
                hp = half * 4 + q
                A("pe", lambda e, psv=psv, q=q, hp=hp: e.transpose(out=psv[0:64, q, :], in_=self.STf[:, hp, :],
                                                                   identity=self.identf[:, :]),
                  r=self.STfG + [self.identf], w=[ps])
            A("act", lambda e, psv=psv, half=half: e.activation(
                out=self.SO[:, half * 4:(half + 1) * 4, :, :].rearrange("p a b c -> p a (b c)"),
                in_=psv[0:64, :, :], func=AF.Copy), r=[ps], w=[self.SO])
        self.DMA("sp", dst.rearrange("(hp hh) i j -> i hp hh j", hh=2), self.SO[:, :, :, :], self.SO, r=[self.SO])

    def load_state(self, src):
        A = self.A
        self.DMA("sp", self.SI[:, :, :, :], src.rearrange("(hp hh) i j -> i hp hh j", hh=2), self.SI, w=[self.SI])
        self.ck(111)
        ps = self.nextG()
        psv = ps[:, :].rearrange("p (a b) -> p a b", b=64)
        for hp in range(NCH):
            A("pe", lambda e, psv=psv, hp=hp: e.transpose(
                out=psv[:, hp, :], in_=self.SI[:, hp, :, :].rearrange("p a b -> p (a b)"),
                identity=self.identf[0:64, 0:64]), r=[self.SI, self.identf], w=[ps])
        self.ck(112)
        A("act", lambda e, psv=psv: e.activation(out=self.STf[:], in_=psv[:, :, :], func=AF.Copy), r=[ps], w=self.STfG)
        self.ck(113)
        A("dve", lambda e, psv=psv: e.tensor_copy(out=self.STb[:], in_=psv[:, :, :]), r=[ps], w=self.STbG)

    def store_shift_prompt(self, l):
        A = self.A
        ps = self.nextG()
        for half in range(2):
            if half == 1:
                ps2 = self.nextG()
            else:
                ps2 = ps
            for q in range(4):
                c = half * 4 + q
                A("pe", lambda e, ps2=ps2, q=q, c=c: e.transpose(out=ps2[0:1, q * 128:(q + 1) * 128],
                                                                  in_=self.carry[:, c, 0:1], identity=self.identf[:, :]),
                  r=[self.carry, self.identf], w=[ps2])
            A("act", lambda e, ps2=ps2, half=half: e.activation(out=self.rowo[0:1, half * 512:(half + 1) * 512],
                                                                in_=ps2[0:1, :], func=AF.Copy), r=[ps2], w=[self.rowo])
        self.DMA("sp", self.o_shp[l:l + 1, :], self.rowo[0:1, :], self.rowo, r=[self.rowo])

    def load_shift_sample(self, l):
        A = self.A
        self.DMA("sp", self.shtok[0:SB, :], self.sshift[l], self.shtok, w=[self.shtok])
        ps = self.nextG()
        psv = ps[:, 0:NCH * SB].rearrange("p (a b) -> p a b", b=SB)
        for c in range(NCH):
            A("pe", lambda e, psv=psv, c=c: e.transpose(out=psv[:, c, :], in_=self.shtok[0:SB, c * 128:(c + 1) * 128],
                                                        identity=self.identf[0:SB, 0:SB]),
              r=[self.shtok, self.identf], w=[ps])
        A("act", lambda e, psv=psv: e.activation(out=self.shiftT[:], in_=psv[:, :, :], func=AF.Copy), r=[ps], w=[self.shiftT])

    def tile_a(self, l, ti, nt, sample):
        self.cur = (l, ti)
        A = self.A
        V = self.VEC
        Xt, xr, xnT, dxT = self.Xt, self.xr, self.xnT, self.dxT
        F = self.F
        if l == 0:
            self.DMA("sp", Xt[:nt, :], self.x_in(ti), Xt, w=[Xt])
        else:
            self.DMA("sp", Xt[:nt, :], self.x_dram(ti), Xt, r=[self.xbuf[ti]], w=[Xt])
        st1 = self.st1
        A("pool", lambda e: e.memset(st1[:], 0.0), w=[st1])
        A("act", lambda e: e.activation(out=self.junk[:nt, :], in_=Xt[:nt, :], func=AF.Square, accum_out=st1[:nt, 0:1]),
          r=[Xt, st1], w=[self.junk, st1])
        A("act", lambda e: e.activation(out=st1[:nt, 1:2], in_=st1[:nt, 0:1], func=AF.Ln, scale=1.0 / D, bias=RMS_EPS),
          r=[st1], w=[st1])
        A("act", lambda e: e.activation(out=st1[:nt, 2:3], in_=st1[:nt, 1:2], func=AF.Exp, scale=-0.5), r=[st1], w=[st1])
        A("act", lambda e: e.activation(out=xr[:nt, :], in_=Xt[:nt, :], func=AF.Identity, scale=st1[:nt, 2:3]),
          r=[Xt, st1], w=[xr])
        self.ck(1)
        for half in range(2):
            ps = self.nextG()
            psv = ps[:, :].rearrange("p (a b) -> p a b", b=128)
            for q in range(4):
                c = half * 4 + q
                A("pe", lambda e, psv=psv, q=q, c=c: e.transpose(out=psv[:, q, :nt], in_=xr[:nt, c * 128:(c + 1) * 128],
                                                                 identity=self.identf[:nt, :nt]),
                  r=[xr, self.identf], w=[ps])
            A("dve", lambda e, psv=psv, half=half: e.tensor_tensor(
                out=xnT[:, half * 4:(half + 1) * 4, :nt], in0=psv[:, :, :nt],
                in1=bc(V[:, V_LN, half * 4:(half + 1) * 4].unsqueeze(2), [128, 4, nt]), op=ALU.mult),
              r=[ps, V], w=[xnT])
        self.ck(2)
        if not sample:
            A("dve", lambda e: e.tensor_tensor(out=dxT[:, :, 1:nt], in0=xnT[:, :, 0:nt - 1], in1=xnT[:, :, 1:nt],
                                               op=ALU.subtract), r=[xnT], w=[dxT])
            A("dve", lambda e: e.tensor_tensor(out=dxT[:, :, 0:1], in0=self.carry[:, :, 0:1], in1=xnT[:, :, 0:1],
                                               op=ALU.subtract), r=[xnT, self.carry], w=[dxT])
            A("dve", lambda e: e.tensor_copy(out=self.carry[:, :, 0:1], in_=xnT[:, :, nt - 1:nt]), r=[xnT], w=[self.carry])
        else:
            x4 = xnT[:, :, 0:NS].rearrange("p c (b t) -> p c b t", t=ST_)
            d4 = dxT[:, :, 0:NS].rearrange("p c (b t) -> p c b t", t=ST_)
            A("dve", lambda e: e.tensor_tensor(out=d4[:, :, :, 1:ST_], in0=x4[:, :, :, 0:ST_ - 1], in1=x4[:, :, :, 1:ST_],
                                               op=ALU.subtract), r=[xnT], w=[dxT])
            A("dve", lambda e: e.tensor_tensor(out=d4[:, :, :, 0:1], in0=self.shiftT[:].unsqueeze(3), in1=x4[:, :, :, 0:1],
                                               op=ALU.subtract), r=[xnT, self.shiftT], w=[dxT])
            for half in range(2):
                ps = self.nextG()
                for q in range(4):
                    c = half * 4 + q
                    A("pe", lambda e, ps=ps, q=q, c=c: e.transpose(out=ps[0:SB, q * 128:(q + 1) * 128],
                                                                    in_=x4[:, c, :, ST_ - 1], identity=self.identf[:, :]),
                      r=[xnT, self.identf], w=[ps])
                A("act", lambda e, ps=ps, half=half: e.activation(out=self.rowo[0:SB, half * 512:(half + 1) * 512],
                                                                  in_=ps[0:SB, :], func=AF.Copy), r=[ps], w=[self.rowo])
            self.DMA("sp", self.o_shs[l], self.rowo[0:SB, :], self.rowo, r=[self.rowo])

        def mix(p, dst):
            for c in range(NCH):
                eng = "dve"
                A(eng, lambda e, c=c: e.scalar_tensor_tensor(out=dst[:, c, :nt], in0=dxT[:, c, :nt],
                                                             scalar=V[:, V_MU + p, c:c + 1], in1=xnT[:, c, :nt],
                                                             op0=ALU.mult, op1=ALU.add), r=[dxT, xnT, V], w=[dst])

        def proj(Wt, src, evac):
            for half in range(2):
                ps = self.nextG()
                psv = ps[:, :].rearrange("p (a b) -> p a b", b=128)
                for q in range(4):
                    eo = half * 4 + q
                    for c in range(NCH):
                        A("pe", lambda e, psv=psv, q=q, eo=eo, c=c: e.matmul(
                            out=psv[:, q, :nt], lhsT=Wt[:, c, eo * 128:(eo + 1) * 128], rhs=src[:, c, :nt],
                            start=(c == 0), stop=(c == NCH - 1)), r=[Wt, src], w=[ps])
                evac(half, ps, psv)

        def evac_copy(dst, eng="act"):
            def f(half, ps, psv):
                if eng == "act":
                    A("act", lambda e: e.activation(out=dst[:, half * 4:(half + 1) * 4, :nt], in_=psv[:, :, :nt], func=AF.Copy),
                      r=[ps], w=[dst])
                else:
                    A("dve", lambda e: e.tensor_copy(out=dst[:, half * 4:(half + 1) * 4, :nt], in_=psv[:, :, :nt]),
                      r=[ps], w=[dst])
            return f

        def sigmoid_from_psum(dst, negbias_slot):
            def f(half, ps, psv):
                for q in range(4):
                    c = half * 4 + q
                    A("act", lambda e, q=q, c=c: e.activation(out=dst[:, c, :nt], in_=psv[:, q, :nt], func=AF.Exp,
                                                              scale=-1.0, bias=V[:, negbias_slot, c:c + 1]),
                      r=[ps, V], w=[dst])
                if half == 1:
                    A("act", lambda e: e.activation(out=dst[:, :, :nt], in_=dst[:, :, :nt], func=AF.Ln, bias=1.0), r=[dst], w=[dst])
                    A("act", lambda e: e.activation(out=dst[:, :, :nt], in_=dst[:, :, :nt], func=AF.Exp, scale=-1.0), r=[dst], w=[dst])
            return f

        def lora_hidden(Wa, src, nh, tanh, hset):
            ps = self.nextG()
            for c in range(NCH):
                A("pe", lambda e, ps=ps, c=c: e.matmul(out=ps[0:nh, 0:nt], lhsT=Wa[:, c, :], rhs=src[:, c, :nt],
                                                       start=(c == 0), stop=(c == NCH - 1)), r=[Wa, src], w=[ps])
            hid, hidb = self.hids[hset]
            if tanh:
                A("act", lambda e: e.activation(out=hid[0:nh, :nt], in_=ps[0:nh, 0:nt], func=AF.Exp, scale=2.0), r=[ps], w=[hid])
                A("dve", lambda e: e.tensor_scalar(out=hid[0:nh, :nt], in0=hid[0:nh, :nt], scalar1=1.0, scalar2=None, op0=ALU.add),
                  r=[hid], w=[hid])
                A("dve", lambda e: e.reciprocal(out=hid[0:nh, :nt], in_=hid[0:nh, :nt]), r=[hid], w=[hid])
                A("dve", lambda e: e.tensor_scalar(out=hidb[0:nh, :nt], in0=hid[0:nh, :nt], scalar1=-2.0, scalar2=1.0,
                                                   op0=ALU.mult, op1=ALU.add), r=[hid], w=[hidb])
            else:
                A("act", lambda e: e.activation(out=hidb[0:nh, :nt], in_=ps[0:nh, 0:nt], func=AF.Copy), r=[ps], w=[hidb])

        def lora_out(Wb, nh, evac, hset):
            hid, hidb = self.hids[hset]
            for half in range(2):
                ps2 = self.nextG()
                psv = ps2[:, :].rearrange("p (a b) -> p a b", b=128)
                for q in range(4):
                    eo = half * 4 + q
                    A("pe", lambda e, psv=psv, q=q, eo=eo: e.matmul(out=psv[:, q, :nt], lhsT=Wb[0:nh, eo * 128:(eo + 1) * 128],
                                                                    rhs=hidb[0:nh, :nt], start=True, stop=True),
                      r=[Wb, hidb], w=[ps2])
                evac(half, ps2, psv)

        def lora(Wa, Wb, src, nh, tanh, evac):
            lora_hidden(Wa, src, nh, tanh, 0)
            lora_out(Wb, nh, evac, 0)

        rT, kT, vT, szT = self.rT, self.kT, self.vT, self.szT
        MX = self.MX
        recw = F[1]
        aT = F[2]
        ez = F[0]
        mix(4, MX[0]); mix(5, MX[1])
        lora_hidden(self.Wla, MX[0], 64, True, 0)
        lora_hidden(self.Ala, MX[1], 64, False, 1)
        mix(0, MX[0]); proj(self.W[0], MX[0], evac_copy(rT, "act"))
        self.ck(4)
        lora_out(self.Wlb, 64, sigmoid_from_psum(recw, V_NW0), 0)
        mix(1, MX[1]); proj(self.W[1], MX[1], evac_copy(kT, "dve"))
        lora_out(self.Alb, 64, sigmoid_from_psum(aT, V_NA0), 1)
        mix(2, MX[0]); proj(self.W[2], MX[0], evac_copy(vT, "act"))
        if l == 1:
            gv = F[0]
            lora(self.Vla, self.Vlb, MX[0], 32, False, sigmoid_from_psum(gv, V_NV0))
            vf = self.vfT
            self.DMA("sp", vf[:, :, :nt], self.d_vf[ti, :, :, 0:nt], vf, r=[self.vfbuf[ti]], w=[vf])
            A("dve", lambda e: e.tensor_tensor(out=vf[:, :, :nt], in0=vf[:, :, :nt], in1=vT[:, :, :nt], op=ALU.subtract),
              r=[vf, vT], w=[vf])
            A("dve", lambda e: e.tensor_tensor(out=vf[:, :, :nt], in0=vf[:, :, :nt], in1=gv[:, :, :nt], op=ALU.mult),
              r=[vf, gv], w=[vf])
            A("dve", lambda e: e.tensor_tensor(out=vT[:, :, :nt], in0=vT[:, :, :nt], in1=vf[:, :, :nt], op=ALU.add),
              r=[vf, vT], w=[vT])
        else:
            self.DMA("sp", self.d_vf[ti, :, :, 0:nt], vT[:, :, :nt], vT, r=[vT], w=[self.vfbuf[ti]])
        self.ck(5)

        def evac_z(half, ps, psv):
            sl = slice(half * 4, (half + 1) * 4)
            A("act", lambda e: e.activation(out=ez[:, sl, :nt], in_=psv[:, :, :nt], func=AF.Exp, scale=-1.0), r=[ps], w=[ez])
            A("act", lambda e: e.activation(out=ez[:, sl, :nt], in_=ez[:, sl, :nt], func=AF.Ln, bias=1.0), r=[ez], w=[ez])
            A("act", lambda e: e.activation(out=ez[:, sl, :nt], in_=ez[:, sl, :nt], func=AF.Exp, scale=-1.0), r=[ez], w=[ez])
            A("dve", lambda e: e.tensor_tensor(out=szT[:, sl, :nt], in0=psv[:, :, :nt], in1=ez[:, sl, :nt], op=ALU.mult),
              r=[ps, ez], w=[szT])
        mix(3, MX[1]); proj(self.W[3], MX[1], evac_z)
        self.ck(6)
        self.ck(7)
        cum = F[3]
        sm = 1 if sample else 0
        for c in range(NCH):
            A("dve", lambda e, c=c: e.tensor_tensor_scan(out=cum[:, c, :nt], data0=self.scanm[:, sm, :nt], data1=recw[:, c, :nt],
                                                         initial=0.0, op0=ALU.mult, op1=ALU.add), r=[recw, self.scanm], w=[cum])
        A("pool", lambda e: e.tensor_tensor(out=recw[:, :, :nt], in0=cum[:, :, :nt], in1=recw[:, :, :nt], op=ALU.subtract),
          r=[cum, recw], w=[recw])
        Epos, Eneg, Eprev = F[4], F[5], F[6]
        A("act", lambda e: e.activation(out=Epos[:, :, :nt], in_=cum[:, :, :nt], func=AF.Exp, scale=-CDEC), r=[cum], w=[Epos])
        A("act", lambda e: e.activation(out=Eneg[:, :, :nt], in_=cum[:, :, :nt], func=AF.Exp, scale=CDEC), r=[cum], w=[Eneg])
        A("act", lambda e: e.activation(out=Eprev[:, :, :nt], in_=recw[:, :, :nt], func=AF.Exp, scale=-CDEC), r=[recw], w=[Eprev])
        nb = SB if sample else 1
        L = ST_ if sample else 128
        WL = self.WL
        cum4 = cum[:, :, 0:nb * L].rearrange("p c (b t) -> p c b t", t=L)
        A("act", lambda e: e.activation(out=WL[:, :, 0:nb], in_=cum4[:, :, :, L - 1], func=AF.Exp, scale=-CDEC), r=[cum], w=[WL])
        self.ck(8)
        kk = F[7]
        sq = self.junk
        sqv = sq[:, :].rearrange("p (a b) -> p a b", b=128)
        for c in range(NCH):
            A("act", lambda e, c=c: e.activation(out=sqv[:, c, :nt], in_=kT[:, c, :nt], func=AF.Square, scale=V[:, V_KK, c:c + 1]),
              r=[kT, V], w=[sq])
        rs = F[0]
        for half in range(2):
            ps = self.nextG()
            psv = ps[:, :].rearrange("p (a b) -> p a b", b=128)
            for q in range(4):
                c = half * 4 + q
                A("pe", lambda e, psv=psv, q=q, c=c: e.matmul(out=psv[:, q, :nt], lhsT=self.blkb[:, :], rhs=sqv[:, c, :nt],
                                                              start=True, stop=True), r=[self.blkb, sq], w=[ps])
            sl = slice(half * 4, (half + 1) * 4)
            A("act", lambda e, psv=psv, sl=sl: e.activation(out=rs[:, sl, :nt], in_=psv[:, :, :nt], func=AF.Ln, bias=1e-24),
              r=[ps], w=[rs])
        A("act", lambda e: e.activation(out=rs[:, :, :nt], in_=rs[:, :, :nt], func=AF.Exp, scale=-0.5), r=[rs], w=[rs])
        for c in range(NCH):
            eng = "dve"
            A(eng, lambda e, c=c: e.scalar_tensor_tensor(out=kk[:, c, :nt], in0=kT[:, c, :nt], scalar=V[:, V_KK, c:c + 1],
                                                         in1=rs[:, c, :nt], op0=ALU.mult, op1=ALU.mult), r=[kT, rs, V], w=[kk])
        tmp = F[0]
        for c in range(NCH):
            A("act", lambda e, c=c: e.activation(out=tmp[:, c, :nt], in_=aT[:, c, :nt], func=AF.Identity,
                                                 scale=V[:, V_KA, c:c + 1], bias=V[:, V_OMK, c:c + 1]), r=[aT, V], w=[tmp])
        A("dve", lambda e: e.tensor_tensor(out=kT[:, :, :nt], in0=kT[:, :, :nt], in1=tmp[:, :, :nt], op=ALU.mult),
          r=[kT, tmp], w=[kT])
        self.ck(9)
        AR, BK = self.AR, self.BK
        A("dve", lambda e: e.scalar_tensor_tensor(out=AR[:, :, 0, :nt], in0=kk[:, :, :nt], scalar=-1.0, in1=Eprev[:, :, :nt],
                                                  op0=ALU.mult, op1=ALU.mult), r=[kk, Eprev], w=[AR])
        A("pool", lambda e: e.tensor_tensor(out=AR[:, :, 1, :nt], in0=rT[:, :, :nt], in1=Epos[:, :, :nt], op=ALU.mult),
          r=[rT, Epos], w=[AR])
        A("dve", lambda e: e.tensor_tensor(out=aT[:, :, :nt], in0=aT[:, :, :nt], in1=kk[:, :, :nt], op=ALU.mult), r=[aT, kk], w=[aT])
        A("dve", lambda e: e.tensor_tensor(out=aT[:, :, :nt], in0=aT[:, :, :nt], in1=Eneg[:, :, :nt], op=ALU.mult), r=[aT, Eneg], w=[aT])
        A("pool", lambda e: e.tensor_tensor(out=Eneg[:, :, :nt], in0=Eneg[:, :, :nt], in1=kT[:, :, :nt], op=ALU.mult),
          r=[kT, Eneg], w=[Eneg])
        A("act", lambda e: e.activation(out=BK[:, :, 0, :nt], in_=aT[:, :, :nt], func=AF.Copy), r=[aT], w=[BK])
        A("act", lambda e: e.activation(out=BK[:, :, 1, :nt], in_=Eneg[:, :, :nt], func=AF.Copy), r=[Eneg], w=[BK])
        WLb = bc(WL[:, :, 0:nb].unsqueeze(3), [128, NCH, nb, L])
        v4 = lambda t: t[:, :, 0:nb * L].rearrange("p c (b t) -> p c b t", t=L)
        A("dve", lambda e: e.tensor_tensor(out=v4(self.BhT), in0=v4(aT), in1=WLb, op=ALU.mult), r=[aT, WL], w=[self.BhT])
        A("pool", lambda e: e.tensor_tensor(out=v4(self.KhT), in0=v4(Eneg), in1=WLb, op=ALU.mult), r=[Eneg, WL], w=[self.KhT])
        A("act", lambda e: e.activation(out=self.vTb[:, :, :nt], in_=vT[:, :, :nt], func=AF.Copy), r=[vT], w=[self.vTb])
        self.ck(10)
        YT = self.YT
        for b in range(nb):
            c0 = b * L
            if sample and not (_os.environ.get("K_NOLOAD") and b > 0):
                self.load_state(self.swkv[l, b])
            self.ck(110)
            for (srcT, dstt) in ((self.vTb, self.Vt), (self.BhT, self.Bh), (self.KhT, self.Kh)):
                ps = self.nextG()
                psb = ps.h.bitcast(BF16)
                for c in range(NCH):
                    A("pe", lambda e, psb=psb, c=c, srcT=srcT, c0=c0: e.transpose(out=psb[0:L, c * 128:(c + 1) * 128],
                                                                            in_=srcT[:, c, c0:c0 + L], identity=self.identb[:, :]),
                      r=[srcT, self.identb], w=[ps])
                A("act", lambda e, psb=psb, dstt=dstt: e.activation(out=dstt[0:L, :], in_=psb[0:L, :], func=AF.Copy),
                  r=[ps], w=[dstt])
            self.ck(11)
            self.wkv_chunk(L, c0, b)
            self.ck(12)
            if sample and not _os.environ.get("K_NOSTORE"):
                self.store_state(self.o_wkvs[l, b])
            self.ck(100 + b)
        self.ck(13)
        yc = F[1]
        for half in range(2):
            ps = self.nextG()
            psv = ps[:, :].rearrange("p (a b) -> p a b", b=128)
            sl = slice(half * 4, (half + 1) * 4)
            for q in range(4):
                c = half * 4 + q
                A("pe", lambda e, psv=psv, q=q, c=c: e.matmul(out=psv[:, q, :nt], lhsT=self.blkf[:, :], rhs=YT[:, c, :nt],
                                                              start=True, stop=True), r=[self.blkf, YT], w=[ps])
            A("dve", lambda e, psv=psv, sl=sl: e.scalar_tensor_tensor(out=yc[:, sl, :nt], in0=psv[:, :, :nt], scalar=-1.0 / 64,
                                                                      in1=YT[:, sl, :nt], op0=ALU.mult, op1=ALU.add),
              r=[ps, YT], w=[yc])
        ysq = F[3]
        A("act", lambda e: e.activation(out=ysq[:, :, :nt], in_=yc[:, :, :nt], func=AF.Square), r=[yc], w=[ysq])
        rstd = F[4]
        for half in range(2):
            ps = self.nextG()
            psv = ps[:, :].rearrange("p (a b) -> p a b", b=128)
            sl = slice(half * 4, (half + 1) * 4)
            for q in range(4):
                c = half * 4 + q
                A("pe", lambda e, psv=psv, q=q, c=c: e.matmul(out=psv[:, q, :nt], lhsT=self.blkf[:, :], rhs=ysq[:, c, :nt],
                                                              start=True, stop=True), r=[self.blkf, ysq], w=[ps])
            A("act", lambda e, psv=psv, sl=sl: e.activation(out=rstd[:, sl, :nt], in_=psv[:, :, :nt], func=AF.Ln,
                                                            scale=1.0 / 64, bias=GN_EPS), r=[ps], w=[rstd])
        A("act", lambda e: e.activation(out=rstd[:, :, :nt], in_=rstd[:, :, :nt], func=AF.Exp, scale=-0.5), r=[rstd], w=[rstd])
        A("dve", lambda e: e.tensor_tensor(out=yc[:, :, :nt], in0=yc[:, :, :nt], in1=rstd[:, :, :nt], op=ALU.mult),
          r=[yc, rstd], w=[yc])
        for c in range(NCH):
            A("act", lambda e, c=c: e.activation(out=yc[:, c, :nt], in_=yc[:, c, :nt], func=AF.Identity,
                                                 scale=V[:, V_GNW, c:c + 1], bias=V[:, V_GNB, c:c + 1]), r=[yc, V], w=[yc])
        rk = F[5]
        for c in range(NCH):
            eng = "dve"
            A(eng, lambda e, c=c: e.scalar_tensor_tensor(out=rk[:, c, :nt], in0=rT[:, c, :nt], scalar=V[:, V_RK, c:c + 1],
                                                         in1=kT[:, c, :nt], op0=ALU.mult, op1=ALU.mult), r=[rT, kT, V], w=[rk])
        for half in range(2):
            ps = self.nextG()
            psv = ps[:, :].rearrange("p (a b) -> p a b", b=128)
            sl = slice(half * 4, (half + 1) * 4)
            for q in range(4):
                c = half * 4 + q
                A("pe", lambda e, psv=psv, q=q, c=c: e.matmul(out=psv[:, q, :nt], lhsT=self.blkf[:, :], rhs=rk[:, c, :nt],
                                                              start=True, stop=True), r=[self.blkf, rk], w=[ps])
            A("dve", lambda e, psv=psv, sl=sl: e.tensor_tensor(out=ysq[:, sl, :nt], in0=psv[:, :, :nt], in1=vT[:, sl, :nt],
                                                               op=ALU.mult), r=[ps, vT], w=[ysq])
        A("dve", lambda e: e.tensor_tensor(out=yc[:, :, :nt], in0=yc[:, :, :nt], in1=ysq[:, :, :nt], op=ALU.add), r=[yc, ysq], w=[yc])
        OT = self.OT
        A("dve", lambda e: e.tensor_tensor(out=OT[:, :, :nt], in0=yc[:, :, :nt], in1=szT[:, :, :nt], op=ALU.mult), r=[yc, szT], w=[OT])
        self.ck(14)
        Wo = self.W[4]
        for half in range(2):
            ps = self.nextG()
            for c in range(NCH):
                A("pe", lambda e, ps=ps, c=c, half=half: e.matmul(out=ps[:nt, :], lhsT=OT[:, c, :nt],
                                                                  rhs=Wo[:, c, half * 512:(half + 1) * 512],
                                                                  start=(c == 0), stop=(c == NCH - 1)), r=[OT, Wo], w=[ps])
            A("dve", lambda e, ps=ps, half=half: e.tensor_tensor(out=Xt[:nt, half * 512:(half + 1) * 512], in0=ps[:nt, :],
                                                                 in1=Xt[:nt, half * 512:(half + 1) * 512], op=ALU.add),
              r=[ps, Xt], w=[Xt])
        self.DMA("sp", self.x_dram(ti), Xt[:nt, :], Xt, r=[Xt], w=[self.xbuf[ti]])
        self.ck(15)

    def wkv_chunk(self, L, c0, b):
        if not hasattr(self, "AsbG"):
            self.AsbG = [Al(f"AsbG{i}", self.Asb.h[:, 4 * i:4 * i + 4, :, :]) for i in range(2)]
            self.NsbG = [[Al(f"NsbG{j}{i}", self.Nsb[j].h[:, 4 * i:4 * i + 4, :, :]) for i in range(2)] for j in range(2)]
            self.XbG = [Al(f"XbG{i}", self.Xb.h[:, 4 * i:4 * i + 4, :]) for i in range(2)]
        A = self.A
        MM = self.MM
        AR, BK = self.AR, self.BK
        Vt, Bh, Kh, YT, WL = self.Vt, self.Bh, self.Kh, self.YT, self.WL
        nst = max(1, int(math.log2(L)))
        cs = slice(c0, c0 + L)
        psXs = [self.psX, self.psY]
        for pi in range(2):
            grps = [2 * pi, 2 * pi + 1]
            Ncur = {}
            for gi in grps:
                Asb = self.AsbG[gi % 2]
                for hl in range(4):
                    h = 4 * gi + hl
                    hp, pb = h // 2, 64 * (h % 2)
                    ps = self.psA[h % 2]
                    self.new_round(ps)
                    pv = ps[:, :].rearrange("p (a b) -> p a b", b=128)
                    for bk in range(2):
                        MM(ps, pv[0:L, 2 * bk:2 * bk + 2, 0:L], BK[pb:pb + 64, hp, bk, cs], AR[pb:pb + 64, hp, :, cs], pb, 64, 0, L,
                           r=[BK, AR])
                    A("dve", lambda e, pv=pv, hl=hl, Asb=Asb: e.tensor_tensor(out=Asb[0:L, hl, :, 0:L], in0=pv[0:L, :, 0:L],
                                                                             in1=self.maskA[0:L, :, 0:L], op=ALU.mult),
                      r=[ps, self.maskA], w=[Asb])
            for k2 in range(2):
                ps = self.psN[k2]
                self.new_round(ps)
                pv = ps[:, :].rearrange("p (a b) -> p a b", b=128)
                for gi in grps:
                    for j in range(2):
                        h = 4 * gi + 2 * j + k2
                        hp, pb = h // 2, 64 * (h % 2)
                        MM(ps, pv[0:L, 2 * (gi % 2) + j, 0:L], AR[pb:pb + 64, hp, 0, cs], BK[pb:pb + 64, hp, 0, cs], pb, 64, 0, L,
                           r=[BK, AR])
                for gi in grps:
                    N0 = self.NsbG[0][gi % 2]
                    sl0 = 2 * (gi % 2)
                    A("dve" if gi % 2 == 0 else "act", (lambda e, pv=pv, N0=N0, sl0=sl0, k2=k2: e.tensor_tensor(
                        out=N0[0:L, k2:4:2, 1, 0:L], in0=pv[0:L, sl0:sl0 + 2, 0:L],
                        in1=bc(self.maskAT[0:L, 0:L].unsqueeze(1), [L, 2, L]), op=ALU.mult)) if gi % 2 == 0 else
                      (lambda e, pv=pv, N0=N0, sl0=sl0, k2=k2: e.activation(out=N0[0:L, k2:4:2, 1, 0:L], in_=pv[0:L, sl0:sl0 + 2, 0:L],
                                                                            func=AF.Copy)),
                      r=[ps, self.maskAT], w=[N0])
            for gi in grps:
                N0 = self.NsbG[0][gi % 2]
                Asb = self.AsbG[gi % 2]
                if gi % 2 == 1:
                    A("pool", lambda e, N0=N0: e.tensor_tensor(out=N0[0:L, :, 1, 0:L], in0=N0[0:L, :, 1, 0:L],
                                                               in1=bc(self.maskAT[0:L, 0:L].unsqueeze(1), [L, 4, L]), op=ALU.mult),
                      r=[N0, self.maskAT], w=[N0])
                A("pool", lambda e, N0=N0, Asb=Asb: e.tensor_copy(out=N0[0:L, :, 0, 0:L], in_=Asb[0:L, :, 0, 0:L]), r=[Asb], w=[N0])
                Ncur[gi] = N0
            for gi in grps:
                Asb, Xb = self.AsbG[gi % 2], self.XbG[gi % 2]
                psX = psXs[gi % 2]
                pxv = psX[:, 0:256].rearrange("p (a b) -> p a b", b=64)
                self.new_round(psX)
                for hl in range(4):
                    h = 4 * gi + hl
                    hp, pb = h // 2, 64 * (h % 2)
                    STb = self.STbG[gi]
                    MM(psX, pxv[0:L, hl, :], AR[pb:pb + 64, hp, 0, cs], STb[pb:pb + 64, hp % 2, :], pb, 64, 0, L, r=[AR, STb])
                    MM(psX, pxv[0:L, hl, :], Asb[0:L, hl, 2, 0:L], Vt[0:L, h * 64:(h + 1) * 64], 0, L, 0, L, r=[Asb, Vt])
                A("act", lambda e, Xb=Xb, pxv=pxv: e.activation(out=Xb[0:L, :, :], in_=pxv[0:L, :, :], func=AF.Copy), r=[psX], w=[Xb])
            for k in range(nst):
                for gi in grps:
                    Xb = self.XbG[gi % 2]
                    psX = psXs[gi % 2]
                    pxv = psX[:, 0:256].rearrange("p (a b) -> p a b", b=64)
                    Nc = Ncur[gi]
                    for hl in range(4):
                        MM(psX, pxv[0:L, hl, :], Nc[0:L, hl, 0, 0:L], Xb[0:L, hl, :], 0, L, 0, L, r=[Nc, Xb])
                    (A("dve", lambda e, Xb=Xb, pxv=pxv: e.tensor_copy(out=Xb[0:L, :, :], in_=pxv[0:L, :, :]), r=[psX], w=[Xb])
                     if gi % 2 == 0 else
                     A("act", lambda e, Xb=Xb, pxv=pxv: e.activation(out=Xb[0:L, :, :], in_=pxv[0:L, :, :], func=AF.Copy), r=[psX], w=[Xb]))
                if k < nst - 1:
                    for gi in grps:
                        Nc = Ncur[gi]
                        Nn = self.NsbG[(k + 1) % 2][gi % 2]
                        for pr in range(2):
                            ps = self.psN[(2 * gi + pr) % 2]
                            self.new_round(ps)
                            pv = ps[:, :].rearrange("p (a b) -> p a b", b=128)
                            for k2 in range(2):
                                hl = 2 * pr + k2
                                MM(ps, pv[0:L, 2 * k2, 0:L], Nc[0:L, hl, 1, 0:L], Nc[0:L, hl, 0, 0:L], 0, L, 0, L, r=[Nc])
                                MM(ps, pv[0:L, 2 * k2 + 1, 0:L], Nc[0:L, hl, 0, 0:L], Nc[0:L, hl, 1, 0:L], 0, L, 0, L, r=[Nc])
                            if pr == 0:
                                A("dve", lambda e, pv=pv, pr=pr, Nn=Nn: e.tensor_copy(
                                    out=Nn[0:L, 2 * pr:2 * pr + 2, :, 0:L].rearrange("p a b c -> p (a b) c"), in_=pv[0:L, :, 0:L]),
                                  r=[ps], w=[Nn])
                            else:
                                A("act", lambda e, pv=pv, pr=pr, Nn=Nn: e.activation(
                                    out=Nn[0:L, 2 * pr:2 * pr + 2, :, 0:L].rearrange("p a b c -> p (a b) c"), in_=pv[0:L, :, 0:L],
                                    func=AF.Copy), r=[ps], w=[Nn])
                        Ncur[gi] = Nn
            for gi in grps:
                Asb, Xb = self.AsbG[gi % 2], self.XbG[gi % 2]
                STb, STf = self.STbG[gi], self.STfG[gi]
                psY = self.nextG()
                pyv = psY[:, 0:256].rearrange("p (a b) -> p a b", b=128)
                self.new_round(psY)
                for hl in range(4):
                    h = 4 * gi + hl
                    hp, pb = h // 2, 64 * (h % 2)
                    q = hl // 2
                    MM(psY, pyv[pb:pb + 64, q, 0:L], STb[pb:pb + 64, hp % 2, :], AR[pb:pb + 64, hp, 1, cs], pb, 64, pb, 64, r=[STb, AR])
                    MM(psY, pyv[pb:pb + 64, q, 0:L], Xb[0:L, hl, :], Asb[0:L, hl, 1, 0:L], 0, L, pb, 64, r=[Xb, Asb])
                    MM(psY, pyv[pb:pb + 64, q, 0:L], Vt[0:L, h * 64:(h + 1) * 64], Asb[0:L, hl, 3, 0:L], 0, L, pb, 64, r=[Vt, Asb])
                A("act", lambda e, gi=gi, pyv=pyv: e.activation(out=YT[:, 2 * gi:2 * gi + 2, cs], in_=pyv[:, :, 0:L], func=AF.Copy),
                  r=[psY], w=[YT])
                ps = self.nextG()
                psv = ps[:, 0:128].rearrange("p (a b) -> p a b", b=64)
                self.new_round(ps)
                for hl in range(4):
                    h = 4 * gi + hl
                    hp, pb = h // 2, 64 * (h % 2)
                    q = hl // 2
                    MM(ps, psv[pb:pb + 64, q, :], Bh[0:L, h * 64:(h + 1) * 64], Xb[0:L, hl, :], 0, L, pb, 64, r=[Bh, Xb])
                    MM(ps, psv[pb:pb + 64, q, :], Kh[0:L, h * 64:(h + 1) * 64], Vt[0:L, h * 64:(h + 1) * 64], 0, L, pb, 64, r=[Kh, Vt])
                sl = slice(2 * gi, 2 * gi + 2)
                A("dve", lambda e, sl=sl, STf=STf: e.tensor_tensor(out=STf[:, :, :], in0=STf[:, :, :],
                                                                   in1=bc(WL[:, sl, b:b + 1], [128, 2, 64]), op=ALU.mult),
                  r=[STf, WL], w=[STf])
                A("dve", lambda e, STf=STf, psv=psv: e.tensor_tensor(out=STf[:, :, :], in0=STf[:, :, :], in1=psv[:, :, :], op=ALU.add),
                  r=[STf, ps], w=[STf])
                A("act", lambda e, STf=STf, STb=STb: e.activation(out=STb[:, :, :], in_=STf[:, :, :], func=AF.Copy), r=[STf], w=[STb])

    def norm_T(self, ti, nt, vec_ap, dst_fn, load_from):
        A = self.A
        Xt, xr, st1 = self.Xt, self.xr, self.st1
        self.DMA("sp", Xt[:nt, :], load_from, Xt, r=[self.xbuf[ti]], w=[Xt])
        A("pool", lambda e: e.memset(st1[:], 0.0), w=[st1])
        A("act", lambda e: e.activation(out=self.junk[:nt, :], in_=Xt[:nt, :], func=AF.Square, accum_out=st1[:nt, 0:1]),
          r=[Xt, st1], w=[self.junk, st1])
        A("act", lambda e: e.activation(out=st1[:nt, 1:2], in_=st1[:nt, 0:1], func=AF.Ln, scale=1.0 / D, bias=RMS_EPS),
          r=[st1], w=[st1])
        A("act", lambda e: e.activation(out=st1[:nt, 2:3], in_=st1[:nt, 1:2], func=AF.Exp, scale=-0.5), r=[st1], w=[st1])
        A("act", lambda e: e.activation(out=xr[:nt, :], in_=Xt[:nt, :], func=AF.Identity, scale=st1[:nt, 2:3]),
          r=[Xt, st1], w=[xr])
        for half in range(2):
            ps = self.nextG()
            psv = ps[:, :].rearrange("p (a b) -> p a b", b=128)
            for q in range(4):
                c = half * 4 + q
                A("pe", lambda e, psv=psv, q=q, c=c: e.transpose(out=psv[:, q, :nt], in_=xr[:nt, c * 128:(c + 1) * 128],
                                                                 identity=self.identf[:nt, :nt]),
                  r=[xr, self.identf], w=[ps])
            for q in range(4):
                c = half * 4 + q
                dap, dT = dst_fn(c)
                if q % 2 == 0:
                    A("act", lambda e, psv=psv, q=q, c=c, dap=dap: e.activation(out=dap, in_=psv[:, q, :nt], func=AF.Identity,
                                                                                scale=vec_ap[:, c:c + 1]), r=[ps, self.BV], w=[dT])
                else:
                    A("dve", lambda e, psv=psv, q=q, c=c, dap=dap: e.tensor_scalar(out=dap, in0=psv[:, q, :nt], scalar1=vec_ap[:, c:c + 1],
                                                                                   scalar2=None, op0=ALU.mult), r=[ps, self.BV], w=[dT])

    def kv_phase(self):
        A = self.A
        self.s.fence()
        self.ropeP = Al("ropeP", self.F[1].h[:, :, :].rearrange("p a b -> p (a b)").rearrange("p (t k d) -> p t k d", k=2, d=32))
        self.DMA("sp", self.ropeP[:, :, :, :], self.c_ropeP[:, 0:NTILE, :, :], self.ropeP, w=[self.ropeP])
        wkv = self.w_kv.rearrange("(c p) e -> p c e", p=128)
        hnT = self.MX[0]
        KTt = self.MX[1]
        Kf = self.xr
        Vf = Vw(self.F[6], self.F[6].h[:, :, :].rearrange("p a b -> p (a b)"))
        tmpr = Vw(self.F[0], self.F[0].h[:, :, :].rearrange("p a b -> p (a b)"))
        Vb = Vw(self.szT, self.szT.h[:, :, :].rearrange("p a b -> p (a b)"))
        for g in range(3):
            Wk, Wv = self.W[2 * (g % 2)], self.W[2 * (g % 2) + 1]
            for c in range(NCH):
                self.DMA("pool", Wk[:, c, :], wkv[:, c, g * D:(g + 1) * D], Wk, w=[Wk])
                self.DMA("pool", Wv[:, c, :], wkv[:, c, 3 * D + g * D:3 * D + (g + 1) * D], Wv, w=[Wv])
            for ti in range(NTILE + 1):
                self.kv_tile(g, ti, 128 if ti < NTILE else NS, Wk, Wv, hnT, KTt, Kf, Vf, tmpr, Vb)

    def kv_tile(self, g, ti, nt, Wk, Wv, hnT, KTt, Kf, Vf, tmpr, Vb):
        A = self.A
        if True:
            if True:
                self.norm_T(ti, nt, self.BV[:, 0, :], lambda c: (hnT[:, c, :nt], hnT), self.x_dram(ti))
                if ti < NTILE:
                    cosb = bc(self.ropeP[:nt, ti, 0, :].unsqueeze(1), [nt, 8, 32])
                    sinb = bc(self.ropeP[:nt, ti, 1, :].unsqueeze(1), [nt, 8, 32])
                    rpT = self.ropeP
                else:
                    cosb = bc(self.ropeS[:nt, 0, :].unsqueeze(1), [nt, 8, 32])
                    sinb = bc(self.ropeS[:nt, 1, :].unsqueeze(1), [nt, 8, 32])
                    rpT = self.ropeS
                for half in range(2):
                    ps = self.nextG()
                    for c in range(NCH):
                        A("pe", lambda e, ps=ps, c=c, half=half: e.matmul(out=ps[:nt, :], lhsT=hnT[:, c, :nt],
                                                                          rhs=Wk[:, c, half * 512:(half + 1) * 512],
                                                                          start=(c == 0), stop=(c == NCH - 1)), r=[hnT, Wk], w=[ps])
                    pv = ps[:nt, :].rearrange("p (h t d) -> p h t d", t=2, d=32)
                    ov = Kf[:nt, half * 512:(half + 1) * 512].rearrange("p (h t d) -> p h t d", t=2, d=32)
                    tv = tmpr[:nt, 0:512].rearrange("p (h t d) -> p h t d", t=2, d=32)
                    A("dve", lambda e, pv=pv, ov=ov: e.tensor_tensor(out=ov[:, :, 0, :], in0=pv[:, :, 0, :], in1=cosb, op=ALU.mult),
                      r=[ps, rpT], w=[Kf])
                    A("dve", lambda e, pv=pv, tv=tv: e.tensor_tensor(out=tv[:, :, 0, :], in0=pv[:, :, 1, :], in1=sinb, op=ALU.mult),
                      r=[ps, rpT], w=[tmpr])
                    A("dve", lambda e, pv=pv, ov=ov: e.tensor_tensor(out=ov[:, :, 1, :], in0=pv[:, :, 1, :], in1=cosb, op=ALU.mult),
                      r=[ps, rpT], w=[Kf])
                    A("dve", lambda e, pv=pv, tv=tv: e.tensor_tensor(out=tv[:, :, 1, :], in0=pv[:, :, 0, :], in1=sinb, op=ALU.mult),
                      r=[ps, rpT], w=[tmpr])
                    A("pool", lambda e, ov=ov, tv=tv: e.tensor_tensor(out=ov[:, :, 0, :], in0=ov[:, :, 0, :], in1=tv[:, :, 0, :],
                                                                      op=ALU.subtract), r=[Kf, tmpr], w=[Kf])
                    A("pool", lambda e, ov=ov, tv=tv: e.tensor_tensor(out=ov[:, :, 1, :], in0=ov[:, :, 1, :], in1=tv[:, :, 1, :],
                                                                      op=ALU.add), r=[Kf, tmpr], w=[Kf])
                for half in range(2):
                    ps = self.nextG()
                    for c in range(NCH):
                        A("pe", lambda e, ps=ps, c=c, half=half: e.matmul(out=ps[:nt, :], lhsT=hnT[:, c, :nt],
                                                                          rhs=Wv[:, c, half * 512:(half + 1) * 512],
                                                                          start=(c == 0), stop=(c == NCH - 1)), r=[hnT, Wv], w=[ps])
                    A("act", lambda e, ps=ps, half=half: e.activation(out=Vf[:nt, half * 512:(half + 1) * 512], in_=ps[:nt, :],
                                                                      func=AF.Copy), r=[ps], w=[Vf])
                if ti == NTILE:
                    dk, dv = self.o_kvs[g][:, 0, :], self.o_kvs[g][:, 1, :]
                else:
                    first = NTILE - (1, 4, 16)[g]
                    if ti >= first:
                        r0 = (ti - first) * 128
                        dk, dv = self.o_kvp[g][r0:r0 + 128, 0, :], self.o_kvp[g][r0:r0 + 128, 1, :]
                    else:
                        dk = dv = None
                if dk is not None:
                    self.DMA("sp", dk, Kf[:nt, :], Kf, r=[Kf])
                    self.DMA("sp", dv, Vf[:nt, :], Vf, r=[Vf])
                t0 = ti * 128
                A("pool", lambda e: e.tensor_copy(out=Vb[:nt, :], in_=Vf[:nt, :]), r=[Vf], w=[Vb])
                self.DMA("sp", self.d_v[g, t0:t0 + nt, :], Vb[:nt, :], Vb, r=[Vb], w=[self.vbuf[g]])
                A("act", lambda e: e.activation(out=self.junk[:nt, :], in_=Kf[:nt, :], func=AF.Copy), r=[Kf], w=[self.junk])
                ps = self.nextG()
                psb = ps.h.bitcast(BF16)
                for c in range(NCH):
                    A("pe", lambda e, psb=psb, c=c: e.transpose(out=psb[:, c * 128:c * 128 + nt], in_=self.junk[:nt, c * 128:(c + 1) * 128],
                                                                identity=self.identb[:nt, :nt]), r=[self.junk, self.identb], w=[ps])
                A("act", lambda e, psb=psb: e.activation(out=KTt[:, :, :nt], in_=psb[:, :].rearrange("p (a b) -> p a b", b=128)[:, :, :nt],
                                                         func=AF.Copy), r=[ps], w=[KTt])
                self.DMA("sp", self.d_kT[g, :, :, t0:t0 + nt].rearrange("hp p t -> p hp t"), KTt[:, :, :nt], KTt, r=[KTt],
                         w=[self.kTbuf[g]])
                if ti == NTILE:
                    A("pool", lambda e, g=g: e.tensor_copy(out=self.KTs[:, g, :, :], in_=KTt[:, :, :nt]), r=[KTt], w=[self.KTs])

    def setup_b(self):
        self.s.fence()
        bf = lambda ap: ap.bitcast(BF16)
        fl3 = lambda t: t.h[:, :, :].rearrange("p a b -> p (a b)")
        fl4 = lambda t: t.h[:, :, :, :].rearrange("p a b c -> p (a b c)")
        Wf = [fl3(t) for t in self.W]
        self.XNp = [Al(f"XNp{c}", Wf[c // 4][:, (c % 4) * 2048:(c % 4 + 1) * 2048]) for c in range(8)]
        self.OGp = [Al(f"OGp{c}", Wf[2 + c // 4][:, (c % 4) * 2048:(c % 4 + 1) * 2048]) for c in range(8)]
        self.Wout = Al("Wout", self.W[4].h[:, :, :])
        F = self.F
        self.QT = [Al(f"QT{g}", bf(fl3(F[g]))) for g in range(3)]
        self.KTp = [Al(f"KTp{g}", bf(fl3(F[3 + g]))) for g in range(3)]
        self.Vblk = [Al(f"Vblk{g}", bf(fl3(t)).rearrange("p (m e) -> p m e", e=128)) for g, t in enumerate((F[6], F[7], self.rT))]
        self.numT = Al("numT", fl4(self.Asb).bitcast(F32))
        self.denT = [Al(f"denT{i}", fl4(self.Nsb[i]).bitcast(F32)) for i in range(2)]
        self.Wqs = [Al(f"Wq{j}", t.h[:, :, :]) for j, t in enumerate((self.MX[0], self.MX[1], self.szT, self.vTb))]
        self.Wqr = Al("Wqr", self.BhT.h[:, :, :])
        self.zTp = Al("zTp", bf(fl3(self.kT)))
        self.cosT = Al("cosT", bf(fl3(self.vT)))
        self.sinT = Al("sinT", fl4(self.AR))
        self.tmpq = [Al("tmpq0", fl3(self.KhT).bitcast(F32)), Al("tmpq1", fl3(self.OT).bitcast(F32))]
        xb = self.Xb.h[:, :, :].rearrange("p a b -> p (a b)")
        self.Eb = [Al(f"Eb{i}", xb[:, i * 256:(i + 1) * 256].rearrange("p (a b) -> p a b", b=128)) for i in range(2)]
        self.Eb.append(Al("Eb2", self.WL.h[:, :, :].rearrange("p a b -> p (a b)").bitcast(BF16).rearrange("p (a b) -> p a b", b=128)))
        self.Eb.append(Al("Eb3", self.shiftT.h[:, :, :].rearrange("p a b -> p (a b)").bitcast(BF16).rearrange("p (a b) -> p a b", b=128)))
        self.blk_par = 0
        self.flnb = Al("flnb", fl4(self.BK).bitcast(F32))
        self.DMA("pool", self.cosT[:, :], self.c_ropeT[:, 0, 0:SEQ], self.cosT, w=[self.cosT])
        self.DMA("pool", self.sinT[:, :], self.c_ropeT[:, 1, 0:SEQ], self.sinT, w=[self.sinT])
        self.DMA("pool", self.ropeTs[:], self.c_ropeT[:, :, SEQ:SEQ + NS], self.ropeTs, w=[self.ropeTs])
        self.DMA("sp", self.flnb[:, :], self.final_ln.partition_broadcast(128), self.flnb, w=[self.flnb])

    def layer_b(self, lb):
        if lb == 0:
            self.setup_b()
        for ti in range(NTILE + 1):
            self.b_norm_tile(lb, ti, 128 if ti < NTILE else NS)
        wo = self.b_w_out[lb].rearrange("(c p) e -> p c e", p=128)
        for c in range(NCH):
            self.DMA("pool", self.Wout[:, c, :], wo[:, c, :], self.Wout, w=[self.Wout])
        for hp in range(NCH):
            self.b_hp(lb, hp)
        self.b_samples(lb)
        for ti in range(NTILE + 1):
            self.b_out_tile(lb, ti, 128 if ti < NTILE else NS)

    def b_norm_tile(self, lb, ti, nt):
        if ti < NTILE:
            dst = lambda c: (self.XNp[c][:, ti * 128:(ti + 1) * 128], self.XNp[c])
        else:
            dst = lambda c: (self.XNs[:, c, :], self.XNs)
        self.norm_T(ti, nt, self.BV[:, 1 + lb, :], dst, self.x_dram(ti))

    def b_proj(self, w, hp, ch, evac):
        A = self.A
        ps = self.nextG()
        n = 512 if ch < 4 else NS
        for c in range(NCH):
            rhs = self.XNp[c][:, ch * 512:(ch + 1) * 512] if ch < 4 else self.XNs[:, c, :]
            rt = self.XNp[c] if ch < 4 else self.XNs
            A("pe", lambda e, ps=ps, c=c, rhs=rhs: e.matmul(out=ps[:, 0:n], lhsT=w[:, c, :], rhs=rhs, start=(c == 0), stop=(c == NCH - 1)),
              r=[w, rt], w=[ps])
        evac(ps, n)
        return ps

    def b_hp(self, lb, hp):
        A = self.A
        src = self.b_w_in[lb].rearrange("(c p) e -> p c e", p=128)
        for j in range(4):
            col0 = j * D + hp * 128
            self.DMA("pool", self.Wqs[j][:, :, :], src[:, :, col0:col0 + 128], self.Wqs[j], w=[self.Wqs[j]])
        tq0, tq1 = self.tmpq
        for g in range(3):
            w, wr = self.Wqs[g], self.Wqr
            for ch in range(5):
                self.b_q_chunk(g, hp, ch, w, wr)
        for ch in range(5):
            self.b_z_chunk(hp, ch)
        for g in range(3):
            self.DMA("sp", self.KTp[g][:, :], self.d_kT[g, hp, :, 0:SEQ], self.KTp[g], r=[self.kTbuf[g]], w=[self.KTp[g]])
            dv = self.d_v[g, 0:SEQ, hp * 128:(hp + 1) * 128]
            if g == 0:
                self.DMA("sp", self.Vblk[0][:, :, :], dv.rearrange("(m i) e -> i m e", i=128), self.Vblk[0], r=[self.vbuf[0]], w=[self.Vblk[0]])
            elif g == 1:
                dvv = dv.rearrange("(m i r) e -> i r m e", i=128, r=4)
                for r_ in range(4):
                    self.DMA("sp", self.Vblk[1][:, 4 * r_:4 * r_ + 4, :], dvv[:, r_, :, :], self.Vblk[1], r=[self.vbuf[1]], w=[self.Vblk[1]])
            else:
                self.DMA("sp", self.Vblk[2][:, :, :], dv.rearrange("(i r) e -> i r e", r=16), self.Vblk[2], r=[self.vbuf[2]], w=[self.Vblk[2]])
        for g in range(3):
            if g == 0:
                blocks = [((m * 128, (m + 1) * 128, 1), m, (m - 1) if m > 0 else None) for m in range(16)]
            elif g == 1:
                blocks = [((r_ + 512 * m, 512 * (m + 1), 4), 4 * r_ + m, (4 * r_ + m - 1) if m > 0 else None)
                          for r_ in range(4) for m in range(4)]
            else:
                blocks = [((r_, SEQ, 16), r_, None) for r_ in range(16)]
            for bi, (qs, own, prev) in enumerate(blocks):
                pq = None
                if prev is not None:
                    pq = blocks[bi - 1][0]
                self.b_block(g, hp, qs, own, prev, pq)
        numT, zTp = self.numT, self.zTp
        for i in range(2):
            A("act", lambda e, i=i: e.activation(out=self.denT[i][:, :], in_=self.denT[i][:, :], func=AF.Ln), r=[self.denT[i]], w=[self.denT[i]])
            A("act", lambda e, i=i: e.activation(out=self.denT[i][:, :], in_=self.denT[i][:, :], func=AF.Exp, scale=-1.0),
              r=[self.denT[i]], w=[self.denT[i]])
            A("dve", lambda e, i=i: e.tensor_tensor(out=numT[:, i * 1024:(i + 1) * 1024], in0=numT[:, i * 1024:(i + 1) * 1024],
                                                    in1=self.denT[i][:, :], op=ALU.mult), r=[numT, self.denT[i]], w=[numT])
        A("dve", lambda e: e.tensor_tensor(out=self.OGp[hp][:, :], in0=numT[:, :], in1=zTp[:, :], op=ALU.mult),
          r=[numT, zTp], w=[self.OGp[hp]])

    def b_q_chunk(self, g, hp, ch, w, wr):
        A = self.A
        tq0, tq1 = self.tmpq
        n = 512 if ch < 4 else NS
        if ch < 4:
            cs_, sn_ = self.cosT[:, ch * 512:(ch + 1) * 512], self.sinT[:, ch * 512:(ch + 1) * 512]
            ct, st_ = self.cosT, self.sinT
            dst, dT = self.QT[g][:, ch * 512:(ch + 1) * 512], self.QT[g]
        else:
            cs_, sn_ = self.ropeTs[:, 0, :], self.ropeTs[:, 1, :]
            ct = st_ = self.ropeTs
            dst, dT = self.QTs[:, hp, g, :], self.QTs

        Qb = self.Wqr
        Qbf = Qb[:, :, :].rearrange("p a b -> p (a b)")

        def ev0(ps, n):
            A("act", lambda e: e.activation(out=Qbf[:, 0:n], in_=ps[:, 0:n], func=AF.Copy), r=[ps], w=[Qb])
            A("dve", lambda e: e.tensor_tensor(out=tq0[:, 0:n], in0=ps[:, 0:n], in1=cs_, op=ALU.mult), r=[ps, ct], w=[tq0])
        self.b_proj(w, hp, ch, ev0)
        ps2 = self.nextG()
        A("pe", lambda e: e.matmul(out=ps2[:, 0:n], lhsT=self.prot[:, :], rhs=Qbf[:, 0:n], start=True, stop=True),
          r=[self.prot, Qb], w=[ps2])
        A("dve", lambda e: e.tensor_tensor(out=tq1[:, 0:n], in0=ps2[:, 0:n], in1=sn_, op=ALU.mult), r=[ps2, st_], w=[tq1])
        A("pool", lambda e: e.tensor_tensor(out=dst, in0=tq0[:, 0:n], in1=tq1[:, 0:n], op=ALU.add), r=[tq0, tq1], w=[dT])

    def b_z_chunk(self, hp, ch):
        A = self.A
        tq0 = self.tmpq[0]
        if ch < 4:
            dst, dT = self.zTp[:, ch * 512:(ch + 1) * 512], self.zTp
        else:
            dst, dT = self.zTs[:, hp, :], self.zTs

        def ev(ps, n):
            A("act", lambda e: e.activation(out=tq0[:, 0:n], in_=ps[:, 0:n], func=AF.Exp, scale=-1.0), r=[ps], w=[tq0])
            A("act", lambda e: e.activation(out=tq0[:, 0:n], in_=tq0[:, 0:n], func=AF.Ln, bias=1.0), r=[tq0], w=[tq0])
            A("act", lambda e: e.activation(out=tq0[:, 0:n], in_=tq0[:, 0:n], func=AF.Exp, scale=-1.0), r=[tq0], w=[tq0])
            A("dve", lambda e: e.tensor_tensor(out=dst, in0=ps[:, 0:n], in1=tq0[:, 0:n], op=ALU.mult), r=[ps, tq0], w=[dT])
        self.b_proj(self.Wqs[3], hp, ch, ev)

    def b_block(self, g, hp, qs, own, prev, pqs):
        A = self.A
        MM = self.MM
        KT, QT, Vb = self.KTp[g], self.QT[g], self.Vblk[g]
        qsl = slice(qs[0], qs[1], qs[2])
        kb0 = 0 if prev is not None else 1
        self.blk_par ^= 1
        par = self.blk_par
        sbanks = (self.psA[0], self.psA[1]) if par == 0 else (self.psX, self.psY)
        Ebs = self.Eb[2 * par:2 * par + 2]
        for hh in range(2):
            pb = 64 * hh
            ps = sbanks[hh]
            pv = ps[:, 0:256].rearrange("p (a b) -> p a b", b=128)
            if prev is not None:
                psl = slice(pqs[0], pqs[1], pqs[2])
                A("pe", lambda e, pv=pv, pb=pb, psl=psl: e.matmul(out=pv[:, 0, :], lhsT=KT[pb:pb + 64, psl], rhs=QT[pb:pb + 64, qsl],
                                                                  start=True, stop=True), r=[KT, QT], w=[ps])
            A("pe", lambda e, pv=pv, pb=pb: e.matmul(out=pv[:, 1, :], lhsT=KT[pb:pb + 64, qsl], rhs=QT[pb:pb + 64, qsl],
                                                     start=True, stop=True), r=[KT, QT], w=[ps])
            Eb = Ebs[hh]
            A("act", lambda e, pv=pv, Eb=Eb: e.activation(out=Eb[:, kb0:2, :], in_=pv[:, kb0:2, :], func=AF.Exp, scale=0.125),
              r=[ps], w=[Eb])
            A("dve" if hh == 0 else "pool", lambda e, Eb=Eb: e.tensor_tensor(out=Eb[:, kb0:2, :], in0=Eb[:, kb0:2, :],
                                                                             in1=self.maskP[:, kb0:2, :], op=ALU.mult),
              r=[Eb, self.maskP], w=[Eb])
        psO = self.nextN()
        self.new_round(psO)
        po = psO[:, 0:256].rearrange("p (a b) -> p a b", b=128)
        for hh in range(2):
            pb = 64 * hh
            Eb = Ebs[hh]
            for kb in range(kb0, 2):
                bidx = prev if kb == 0 else own
                MM(psO, po[pb:pb + 64, 0, :], Vb[:, bidx, pb:pb + 64], Eb[:, kb, :], 0, 128, pb, 64, r=[Vb, Eb])
                MM(psO, po[pb:pb + 64, 1, :], self.ones_b[:, 0:64], Eb[:, kb, :], 0, 128, pb, 64, r=[self.ones_b, Eb])
        numT = self.numT
        if g == 0:
            A("act", lambda e: e.activation(out=numT[:, qsl], in_=po[:, 0, :], func=AF.Copy), r=[psO], w=[numT])
        else:
            A("dve", lambda e: e.tensor_tensor(out=numT[:, qsl], in0=po[:, 0, :], in1=numT[:, qsl], op=ALU.add), r=[psO, numT], w=[numT])
        pieces = []
        q0, q1, st = qs
        n_lo = len(range(q0, min(q1, 1024), st)) if q0 < 1024 else 0
        if n_lo > 0:
            pieces.append((0, slice(q0, min(q1, 1024), st), slice(0, n_lo)))
        if n_lo < 128:
            first_hi = q0 + n_lo * st
            pieces.append((1, slice(first_hi - 1024, q1 - 1024, st), slice(n_lo, 128)))
        for (hf, dsl, csl) in pieces:
            dT = self.denT[hf]
            if g == 0:
                A("act", lambda e, dT=dT, dsl=dsl, csl=csl: e.activation(out=dT[:, dsl], in_=po[:, 1, csl], func=AF.Copy), r=[psO], w=[dT])
            else:
                A("dve", lambda e, dT=dT, dsl=dsl, csl=csl: e.tensor_tensor(out=dT[:, dsl], in0=po[:, 1, csl], in1=dT[:, dsl], op=ALU.add),
                  r=[psO, dT], w=[dT])

    def setup_samples(self):
        bf = lambda ap: ap.bitcast(BF16)
        fl3 = lambda t: t.h[:, :, :].rearrange("p a b -> p (a b)")
        F = self.F
        f0, f1, f2, f3, f4, f5, f6 = [bf(fl3(F[k])) for k in range(7)]
        self.Kc = [Al("Kc0", f0[:, 0:1024]), Al("Kc1", f0[:, 1024:2048])]
        self.Vc = [Al("Vc0", f1[:, 0:1024]), Al("Vc1", f1[:, 1024:2048])]
        self.KTc = Al("KTc", f2[:, 0:1024].rearrange("p (a b) -> p a b", b=128))
        self.Vs = [Al("Vs0", f3[0:64, 0:1024]), Al("Vs1", f3[0:64, 1024:2048]), Al("Vs2", f4[0:64, 0:1024])]
        self.Vn = [Al("Vn0", f5[0:4, 0:1024]), Al("Vn1", f5[0:4, 1024:2048]), Al("Vn2", f4[0:4, 1024:2048])]
        self.Ec = Al("Ec", f6[:, 0:64].rearrange("p (a b c) -> p a b c", b=2, c=4))
        self.En = Al("En", f6[0:4, 64:256].rearrange("p (g a b c) -> p g a b c", a=8, b=2, c=4))
        self.Qbd = Al("Qbd", f6[:, 256:448].rearrange("p (a g b c) -> p a g b c", g=3, b=2, c=4))
        self.maskS = Al("maskS", f6[:, 448:496].rearrange("p (a c) -> p a c", c=4))
        f7 = fl3(F[7])
        self.numS = Al("numS", f7[:, 0:512].rearrange("p (a b) -> p a b", b=NS))
        self.denS = Al("denS", f7[:, 512:1024].rearrange("p (a b) -> p a b", b=NS))

    def b_samples(self, lb):
        A = self.A
        self.s.fence()
        if lb == 0:
            self.setup_samples()
        A("pool", lambda e: e.memset(self.Qbd[:, :, :, :, :], 0.0), w=[self.Qbd])
        self.DMA("pool", self.maskS[:, :, :], self.c_maskS[:, :, :], self.maskS, w=[self.maskS])
        for g in range(3):
            self.DMA("sp", self.Vs[g][:, :], self.d_v[g, SEQ:SEQ + NS, :], self.Vs[g], r=[self.vbuf[g]], w=[self.Vs[g]])
        for b in range(SB):
            self.b_sample_batch(lb, b)
        numS, denS = self.numS, self.denS
        A("dve", lambda e: e.reciprocal(out=denS[:, :, :], in_=denS[:, :, :]), r=[denS], w=[denS])
        A("dve", lambda e: e.tensor_tensor(out=numS[:, :, :], in0=numS[:, :, :], in1=denS[:, :, :], op=ALU.mult), r=[numS, denS], w=[numS])
        A("dve", lambda e: e.tensor_tensor(out=self.OGs[:], in0=numS[:, :, :], in1=self.zTs[:], op=ALU.mult), r=[numS, self.zTs], w=[self.OGs])
        self.s.fence()

    def b_sample_batch(self, lb, b):
        A = self.A
        MM = self.MM
        Qbd, QTs, Ec, En = self.Qbd, self.QTs, self.Ec, self.En
        bs = slice(ST_ * b, ST_ * b + ST_)
        for hh in range(2):
            pb = 64 * hh
            A("dve", lambda e, pb=pb, hh=hh: e.tensor_copy(out=Qbd[pb:pb + 64, :, :, hh, :], in_=QTs[pb:pb + 64, :, :, bs]),
              r=[QTs], w=[Qbd])
        for g in range(3):
            for half in range(2):
                ps = self.nextG()
                A("pe", lambda e, ps=ps, g=g, half=half: e.matmul(out=ps[0:ST_, :], lhsT=self.identb[0:NS, bs],
                                                                  rhs=self.Vs[g][0:NS, half * 512:(half + 1) * 512], start=True, stop=True),
                  r=[self.identb, self.Vs[g]], w=[ps])
                A("act", lambda e, ps=ps, g=g, half=half: e.activation(out=self.Vn[g][0:ST_, half * 512:(half + 1) * 512], in_=ps[0:ST_, :],
                                                                       func=AF.Copy), r=[ps], w=[self.Vn[g]])
        psX = self.psX
        self.new_round(psX)
        po = psX[:, 0:64].rearrange("p (s a c) -> p s a c", s=2, c=ST_)
        tiles = [(0, 0)] + [(1, r_) for r_ in range(4)] + [(2, r_) for r_ in range(4)]
        for tix, (g, r_) in enumerate(tiles):
            self.b_sample_tile(b, tix, g, r_, po)
        ps2 = self.psA[1]
        pv2 = ps2[:, 0:192].rearrange("p (g a c) -> p g a c", a=8, c=8)
        for g in range(3):
            for hp in range(NCH):
                A("pe", lambda e, g=g, hp=hp: e.matmul(out=pv2[0:ST_, g, hp, :], lhsT=self.KTs[:, g, hp, bs], rhs=Qbd[:, hp, g, :, :],
                                                       start=True, stop=True), r=[self.KTs, Qbd], w=[ps2])
        A("act", lambda e: e.activation(out=En[:, :, :, :, :].rearrange("p g a b c -> p g a (b c)"), in_=pv2[0:ST_, :, :, :], func=AF.Exp,
                                        scale=0.125), r=[ps2], w=[En])
        A("dve", lambda e: e.tensor_tensor(out=En[:, :, :, :, :].rearrange("p g a b c -> p g (a b) c"),
                                           in0=En[:, :, :, :, :].rearrange("p g a b c -> p g (a b) c"),
                                           in1=bc(self.maskS[0:ST_, 9:12, :].unsqueeze(2), [ST_, 3, 16, ST_]), op=ALU.mult),
          r=[En, self.maskS], w=[En])
        for g in range(3):
            for hp in range(NCH):
                for hh in range(2):
                    pb = 64 * hh
                    MM(psX, po[pb:pb + 64, 0, hp, :], self.Vn[g][0:ST_, hp * 128 + pb:hp * 128 + pb + 64], En[:, g, hp, hh, :], 0, ST_, pb, 64,
                       r=[self.Vn[g], En])
                    MM(psX, po[pb:pb + 64, 1, hp, :], self.ones_b[0:ST_, 0:64], En[:, g, hp, hh, :], 0, ST_, pb, 64, r=[self.ones_b, En])
        A("act", lambda e: e.activation(out=self.numS[:, :, bs], in_=po[:, 0, :, :], func=AF.Copy), r=[psX], w=[self.numS])
        A("dve", lambda e: e.tensor_copy(out=self.denS[:, :, bs], in_=po[:, 1, :, :]), r=[psX], w=[self.denS])

    def b_sample_tile(self, b, tix, g, r_, po):
        A = self.A
        MM = self.MM
        i = tix % 2
        Kc, Vc, KTc, Ec, Qbd = self.Kc[i], self.Vc[i], self.KTc, self.Ec, self.Qbd
        psX = self.psX
        if g == 0:
            src = self.ckv[0][b]
        elif g == 1:
            src = self.ckv[1][b].rearrange("(m r) k e -> m r k e", r=4)[:, r_]
        else:
            src = self.ckv[2][b].rearrange("(m r) k e -> m r k e", r=16)[:, r_]
        self.DMA("pool", Kc[:, :], src[:, 0, :], Kc, w=[Kc])
        self.DMA("pool", Vc[:, :], src[:, 1, :], Vc, w=[Vc])
        ps = self.nextG()
        psb = ps.h.bitcast(BF16)
        for c in range(NCH):
            A("pe", lambda e, psb=psb, c=c: e.transpose(out=psb[:, c * 128:(c + 1) * 128], in_=Kc[:, c * 128:(c + 1) * 128],
                                                        identity=self.identb[:, :]), r=[Kc, self.identb], w=[ps])
        A("act", lambda e, psb=psb: e.activation(out=KTc[:, :, :], in_=psb[:, :].rearrange("p (a b) -> p a b", b=128), func=AF.Copy),
          r=[ps], w=[KTc])
        ps1 = self.psA[0]
        pv = ps1[:, 0:64].rearrange("p (a c) -> p a c", c=8)
        for hp in range(NCH):
            A("pe", lambda e, hp=hp: e.matmul(out=pv[:, hp, :], lhsT=KTc[:, hp, :], rhs=Qbd[:, hp, g, :, :], start=True, stop=True),
              r=[KTc, Qbd], w=[ps1])
        A("act", lambda e: e.activation(out=Ec[:, :, :, :].rearrange("p a b c -> p a (b c)"), in_=pv[:, :, :], func=AF.Exp, scale=0.125),
          r=[ps1], w=[Ec])
        A("dve", lambda e: e.tensor_tensor(out=Ec[:, :, :, :].rearrange("p a b c -> p (a b) c"),
                                           in0=Ec[:, :, :, :].rearrange("p a b c -> p (a b) c"),
                                           in1=bc(self.maskS[:, tix, :].unsqueeze(1), [128, 16, ST_]), op=ALU.mult),
          r=[Ec, self.maskS], w=[Ec])
        for hp in range(NCH):
            for hh in range(2):
                pb = 64 * hh
                MM(psX, po[pb:pb + 64, 0, hp, :], Vc[:, hp * 128 + pb:hp * 128 + pb + 64], Ec[:, hp, hh, :], 0, 128, pb, 64, r=[Vc, Ec])
                MM(psX, po[pb:pb + 64, 1, hp, :], self.ones_b[:, 0:64], Ec[:, hp, hh, :], 0, 128, pb, 64, r=[self.ones_b, Ec])

    def b_out_tile(self, lb, ti, nt):
        A = self.A
        Xt = self.Xt
        self.DMA("sp", Xt[:nt, :], self.x_dram(ti), Xt, r=[self.xbuf[ti]], w=[Xt])
        Wo = self.Wout
        for half in range(2):
            ps = self.nextG()
            for c in range(NCH):
                if ti < NTILE:
                    lhsT, lt = self.OGp[c][:, ti * 128:(ti + 1) * 128], self.OGp[c]
                else:
                    lhsT, lt = self.OGs[:, c, :], self.OGs
                A("pe", lambda e, ps=ps, c=c, half=half, lhsT=lhsT: e.matmul(out=ps[:nt, :], lhsT=lhsT, rhs=Wo[:, c, half * 512:(half + 1) * 512],
                                                                             start=(c == 0), stop=(c == NCH - 1)), r=[lt, Wo], w=[ps])
            A("dve", lambda e, ps=ps, half=half: e.tensor_tensor(out=Xt[:nt, half * 512:(half + 1) * 512], in0=ps[:nt, :],
                                                                 in1=Xt[:nt, half * 512:(half + 1) * 512], op=ALU.add),
              r=[ps, Xt], w=[Xt])
        if lb == 1:
            st1 = self.st1
            A("pool", lambda e: e.memset(st1[:], 0.0), w=[st1])
            A("act", lambda e: e.activation(out=self.junk[:nt, :], in_=Xt[:nt, :], func=AF.Square, accum_out=st1[:nt, 0:1]),
              r=[Xt, st1], w=[self.junk, st1])
            A("act", lambda e: e.activation(out=st1[:nt, 1:2], in_=st1[:nt, 0:1], func=AF.Ln, scale=1.0 / D, bias=RMS_EPS), r=[st1], w=[st1])
            A("act", lambda e: e.activation(out=st1[:nt, 2:3], in_=st1[:nt, 1:2], func=AF.Exp, scale=-0.5), r=[st1], w=[st1])
            A("act", lambda e: e.activation(out=Xt[:nt, :], in_=Xt[:nt, :], func=AF.Identity, scale=st1[:nt, 2:3]), r=[Xt, st1], w=[Xt])
            A("dve", lambda e: e.tensor_tensor(out=Xt[:nt, :], in0=Xt[:nt, :], in1=self.flnb[:nt, :], op=ALU.mult), r=[Xt, self.flnb], w=[Xt])
        self.DMA("sp", self.x_dram(ti), Xt[:nt, :], Xt, r=[Xt], w=[self.xbuf[ti]])

    def _dump_x(self):
        pass


def _consts():
    c = {}
    c["c_ident"] = np.eye(128, dtype=np.float32)
    s = np.arange(128)[:, None]
    t = np.arange(128)[None, :]
    lt = (s < t).astype(np.float32)
    le = (s <= t).astype(np.float32)
    c["c_maskA"] = np.ascontiguousarray(np.stack([lt, le, lt, le], axis=1))
    c["c_maskAT"] = np.ascontiguousarray((t < s).astype(np.float32))
    blk = np.zeros((128, 128), np.float32)
    blk[:64, :64] = 1.0
    blk[64:, 64:] = 1.0
    c["c_blk"] = blk
    sc = np.ones((128, 2, 128), np.float32)
    sc[:, 0, 0] = 0.0
    sc[:, 1, 0::4] = 0.0
    c["c_scan"] = sc
    half = 32
    inv = (10000.0 ** (-np.arange(half, dtype=np.float32) / np.float32(half))).astype(np.float32)
    pos = np.concatenate([np.arange(SEQ, dtype=np.float32), np.tile(np.float32(2048) + np.arange(ST_, dtype=np.float32), SB)])
    ang = (pos[:, None] * inv[None, :]).astype(np.float32)
    cs = np.stack([np.cos(ang), np.sin(ang)], 1).astype(np.float32)
    rp = np.zeros((128, NTILE + 1, 2, 32), np.float32)
    rp[:, :NTILE] = cs[:SEQ].reshape(NTILE, 128, 2, 32).transpose(1, 0, 2, 3)
    rp[:NS, NTILE] = cs[SEQ:]
    c["c_ropeP"] = rp
    rt = np.zeros((128, 2, SEQ + NS), np.float32)
    pidx = np.arange(128) % 32
    rt[:, 0, :] = np.cos(ang).T[pidx]
    rt[:, 1, :] = np.sin(ang).T[pidx]
    c["c_ropeT"] = rt
    k = np.arange(128)[:, None]
    q = np.arange(128)[None, :]
    ms = np.zeros((128, 12, ST_), np.float32)
    kk_ = np.arange(128)[:, None]
    tt_ = np.arange(ST_)[None, :]
    ms[:, 0, :] = (kk_ > tt_)
    for r_ in range(4):
        ms[:, 1 + r_, :] = (tt_ == r_) & (kk_ >= 1)
        ms[:, 5 + r_, :] = (tt_ == r_) & (kk_ >= 1)
    ms[:ST_, 9, :] = (kk_[:ST_] <= tt_)
    ms[:ST_, 10, :] = (kk_[:ST_] == tt_)
    ms[:ST_, 11, :] = (kk_[:ST_] == tt_)
    c["c_maskS"] = ms
    pr_ = np.zeros((128, 128), np.float32)
    for m_ in range(128):
        if m_ % 64 < 32:
            pr_[m_ + 32, m_] = -1.0
        else:
            pr_[m_ - 32, m_] = 1.0
    c["c_prot"] = pr_
    c["c_maskP"] = np.ascontiguousarray(np.stack([(k > q), (k <= q)], 1).astype(np.float32))
    return c


def _fm(v):
    return np.ascontiguousarray(np.asarray(v, np.float32).reshape(NCH, 128).T)


def _prep_inputs(inp, ncores=NCORES):
    f = lambda k: np.ascontiguousarray(np.asarray(inp[k], dtype=np.float32))
    shared = {}
    a_vec = np.zeros((2, 128, NV, NCH), np.float32)
    for l in range(2):
        a_vec[l, :, V_LN] = _fm(inp["a_ln"][l])
        for p in range(6):
            a_vec[l, :, V_MU + p] = _fm(inp["a_mu"][l, p])
        a_vec[l, :, V_W0] = _fm(inp["a_w0"][l])
        a_vec[l, :, V_A0] = _fm(inp["a_a0"][l])
        if l == 1:
            a_vec[l, :, V_V0] = _fm(inp["a_v0"][0])
        a_vec[l, :, V_KK] = _fm(inp["a_k_k"][l])
        a_vec[l, :, V_KA] = _fm(inp["a_k_a"][l])
        a_vec[l, :, V_RK] = _fm(np.asarray(inp["a_r_k"][l]).reshape(-1))
        a_vec[l, :, V_GNW] = _fm(inp["a_gn_w"][l])
        a_vec[l, :, V_GNB] = _fm(inp["a_gn_b"][l])
    shared["a_vec"] = a_vec
    b_vec = np.zeros((128, 4, NCH), np.float32)
    b_vec[:, 0] = _fm(inp["kv_ln"])
    b_vec[:, 1] = _fm(inp["b_ln"][0])
    b_vec[:, 2] = _fm(inp["b_ln"][1])
    shared["b_vec"] = b_vec
    for k in ("w_kv", "b_w_in", "b_w_out", "final_ln"):
        shared[k] = f(k)
    for k in ("a_w_in", "a_w_out", "a_w_lora_a", "a_w_lora_b", "a_a_lora_a", "a_a_lora_b", "a_v_lora_a", "a_v_lora_b"):
        shared[k] = f(k)
    shared.update(_consts())
    xp = f("x_prompt")
    xs = f("x_sample")
    swkv = f("state_wkv")
    ssh = f("state_shift")
    maps = []
    for c in range(ncores):
        m = dict(shared)
        m["xp"] = xp[c]
        m["xs"] = np.ascontiguousarray(xs[c * SB:(c + 1) * SB].reshape(NS, D))
        m["swkv"] = np.ascontiguousarray(swkv[:, c * SB:(c + 1) * SB])
        m["sshift"] = np.ascontiguousarray(ssh[:, c * SB:(c + 1) * SB])
        for g in range(3):
            ck = np.asarray(inp[f"cache_kv_g{g}"])[c * SB:(c + 1) * SB]
            m[f"ckv{g}"] = np.ascontiguousarray(ck.reshape(SB, ck.shape[1], 2, D), dtype=np.float32)
        maps.append(m)
    return maps


_NC_CACHE = {}


def _get_nc(stage):
    if stage not in _NC_CACHE:
        _NC_CACHE[stage] = Builder(stage).build()
    return _NC_CACHE[stage]


def run_cores(inp, stage="full", cores=None):
    maps = _prep_inputs(inp, NCORES if cores is None else len(cores))
    nc = _get_nc(stage)
    res = run_bass_kernel_spmd(nc, maps, core_ids=list(range(len(maps))))
    return res.results


def kernel(**inputs):
    res = run_cores(inputs, "full")
    cat = lambda k, ax=0: np.concatenate([r[k] for r in res], axis=ax)
    y_prompt = np.stack([r["y_prompt"] for r in res], 0)
    y_sample = cat("y_sample").reshape(NCORES * SB, ST_, D)
    wkv_p = np.stack([r["wkv_p"] for r in res], 1)
    wkv_s = cat("wkv_s", 1)
    shift_p = np.stack([r["shift_p"] for r in res], 1)
    shift_s = cat("shift_s", 1)
    outs = [y_prompt, y_sample, wkv_p, wkv_s, shift_p, shift_s]
    for g, w in enumerate((128, 512, 2048)):
        outs.append(np.stack([r[f"kv{g}p"] for r in res], 0).reshape(NCORES, w, 2, 16, 64))
        outs.append(cat(f"kv{g}s").reshape(NCORES * SB, ST_, 2, 16, 64))
    return tuple(np.ascontiguousarray(o, dtype=np.float32) for o in outs)
```

```python
import contextlib
import math
import numpy as np
import concourse.bass as bass
import concourse.mybir as mybir
from concourse.bass_utils import run_bass_kernel_spmd

F32 = mybir.dt.float32
BF16 = mybir.dt.bfloat16
AF = mybir.ActivationFunctionType
ALU = mybir.AluOpType
AX = mybir.AxisListType

NCORES = 8
D = 1024
NCH = 8
SEQ = 2048
NTILE = 16
SB = 16
ST_ = 4
NS = SB * ST_
RMS_EPS = 1e-6
GN_EPS = 64e-5
CDEC = math.exp(-0.5)
SEM_LIMIT = 12000
import os as _os
TRACE = bool(_os.environ.get("K_TRACE"))


class Buf:
    __slots__ = ("name", "w", "r")

    def __init__(self, name):
        self.name = name
        self.w = None
        self.r = []


class Op:
    __slots__ = ("eng", "fn", "deps", "is_dma", "ms", "sem", "val", "line")

    def __init__(self, eng, fn, is_dma):
        self.eng = eng
        self.fn = fn
        self.is_dma = is_dma
        self.deps = []
        self.ms = False
        self.sem = None
        self.val = 0


class Sched:
    ENGS = ("pe", "act", "dve", "pool", "sp")

    def __init__(self, nc):
        self.nc = nc
        self.ops = {e: [] for e in self.ENGS}
        self.dma_cnt = {}
        self.all_dma = []

    def add(self, eng, fn, reads=(), writes=(), dma=False, key=None, extra=()):
        op = Op(eng, fn, dma)
        if TRACE:
            import sys as _sys
            f = _sys._getframe(1)
            ls = []
            while f is not None and len(ls) < 4:
                ls.append(f.f_lineno)
                f = f.f_back
            op.line = ls
        for d in extra:
            op.deps.append(d)
            d.ms = True
        deps = {}
        for b in reads:
            if b.w is not None:
                deps[id(b.w)] = (b.w, True)
        for b in writes:
            if b.w is not None and id(b.w) not in deps:
                deps[id(b.w)] = (b.w, False)
            for r in b.r:
                if id(r) not in deps:
                    deps[id(r)] = (r, False)
        for d, raw in deps.values():
            if d.is_dma:
                op.deps.append(d)
                d.ms = True
            elif d.eng == eng and not dma:
                if raw and eng != "pe":
                    op.deps.append(d)
                    d.ms = True
            else:
                op.deps.append(d)
                d.ms = True
        for b in reads:
            b.r.append(op)
        for b in writes:
            b.w = op
            b.r = []
        if dma:
            c = self.dma_cnt.get(id(key), (key, 0))[1] + 1
            self.dma_cnt[id(key)] = (key, c)
            op.sem = ("dma", id(key))
            op.val = 16 * c
            self.all_dma.append(op)
        self.ops[eng].append(op)
        return op

    def fence(self):
        lasts = []
        for e in self.ENGS:
            for op in reversed(self.ops[e]):
                if not op.is_dma:
                    lasts.append(op)
                    break
        dl = {}
        for op in self.all_dma:
            dl[op.sem] = op
        for e in self.ENGS:
            extra = [o for o in lasts if o.eng != e] + list(dl.values())
            self.add(e, lambda eng: eng.nop(), extra=extra)

    def emit(self):
        nc = self.nc
        nsem_eng = {}
        for e in self.ENGS:
            cnt = 0
            epoch = 0
            for op in self.ops[e]:
                if op.is_dma or not op.ms:
                    continue
                cnt += 1
                if cnt > SEM_LIMIT:
                    epoch += 1
                    cnt = 1
                op.sem = ("eng", e, epoch)
                op.val = cnt
            nsem_eng[e] = epoch + 1
        sems = {}
        with contextlib.ExitStack() as es:
            for e in self.ENGS:
                for k in range(nsem_eng[e]):
                    sems[("eng", e, k)] = es.enter_context(nc.semaphore(f"s_{e}{k}"))
            for kid, (key, c) in self.dma_cnt.items():
                sems[("dma", kid)] = es.enter_context(nc.semaphore(f"d_{key.name}"))
            last_dma = {}
            for op in self.all_dma:
                last_dma[op.sem] = op
            block = es.enter_context(nc.Block())

            def run(ename, eng):
                waited = {}
                nops = len(self.ops[ename])
                for io, op in enumerate(self.ops[ename]):
                    if TRACE and io >= nops - 25:
                        print("TR", ename, io, op.line, "inc", op.sem[1:] if op.sem else None, op.val, "ms", op.ms,
                              "waits", [(d.eng, d.sem[1:], d.val, d.line) for d in op.deps if waited.get(d.sem, 0) < d.val])
                    for d in op.deps:
                        if waited.get(d.sem, 0) >= d.val:
                            continue
                        waited[d.sem] = d.val
                        eng.wait_ge(sems[d.sem], d.val)
                    ins = op.fn(eng)
                    if op.is_dma:
                        ins.then_inc(sems[op.sem], 16)
                    elif op.ms:
                        ins.then_inc(sems[op.sem], 1)
                if ename == "sp":
                    for s, op in last_dma.items():
                        if waited.get(s, 0) < op.val:
                            eng.wait_ge(sems[s], op.val)

            @block.tensor
            def _(eng):
                run("pe", eng)

            @block.scalar
            def _(eng):
                run("act", eng)

            @block.vector
            def _(eng):
                run("dve", eng)

            @block.gpsimd
            def _(eng):
                run("pool", eng)

            @block.sync
            def _(eng):
                run("sp", eng)


class T:
    def __init__(self, h, name):
        self.h = h
        self.b = Buf(name)

    def __getitem__(self, k):
        return self.h[k]


class Vw:
    def __init__(self, t, ap):
        self.b = t.b
        self.ap = ap

    def __getitem__(self, k):
        return self.ap[k]


class Al:
    def __init__(self, name, ap):
        self.b = Buf(name)
        self.ap = ap

    def __getitem__(self, k):
        return self.ap[k]


def bc(ap, shape):
    return ap.broadcast_to(shape)


V_LN, V_MU, V_W0, V_A0, V_V0, V_KK, V_KA, V_RK, V_GNW, V_GNB = 0, 1, 7, 8, 9, 10, 11, 12, 13, 14
V_NW0, V_NA0, V_NV0, V_OMK = 15, 16, 17, 18
NV = 20


class _Stop(Exception):
    pass


class Builder:
    def __init__(self, stage="full"):
        import os
        self.stop = int(os.environ.get("K_STOP", "-1"))
        self.stop_at = tuple(int(v) for v in os.environ.get("K_AT", "0,0").split(","))
        self.cur = (0, 0)
        self.ckcnt = 0
        self.stop_cnt = int(os.environ.get("K_CNT", "1"))
        self.stage = stage
        self.nc = bass.Bass("TRN2", target_bir_lowering=False)
        self.s = Sched(self.nc)
        self.es = contextlib.ExitStack()
        self.gflip = 0
        self.aflip = 0
        self.nflip = 0
        self.uid = 0
        self.bank_last = {}
        self.bank_round = {}

    def sb(self, name, shape, dt):
        return T(self.es.enter_context(self.nc.sbuf_tensor(name, list(shape), dt)), name)

    def din(self, name, shape, dt=F32):
        return self.nc.dram_tensor(name, list(shape), dt, kind="ExternalInput").ap()

    def dout(self, name, shape, dt=F32):
        return self.nc.dram_tensor(name, list(shape), dt, kind="ExternalOutput").ap()

    def dint(self, name, shape, dt=F32):
        return self.nc.dram_tensor(name, list(shape), dt, kind="Internal").ap()

    def ck(self, n):
        if self.stop == n and self.cur == self.stop_at:
            self.ckcnt += 1
            if self.ckcnt == self.stop_cnt:
                raise _Stop()

    def A(self, eng, fn, r=(), w=()):
        w = list(w) + [x for x in r if getattr(x, "psum", False) and x not in w]
        self.s.add(eng, fn, reads=[x.b if isinstance(x, (T, Vw, Al)) else x for x in r],
                   writes=[x.b if isinstance(x, (T, Vw, Al)) else x for x in w])

    def DMA(self, eng, out, in_, key, r=(), w=()):
        self.s.add(eng, lambda e: e.dma_start(out=out, in_=in_),
                   reads=[x.b if isinstance(x, (T, Vw, Al)) else x for x in r],
                   writes=[x.b if isinstance(x, (T, Vw, Al)) else x for x in w], dma=True,
                   key=key.b if isinstance(key, (T, Vw, Al)) else key)

    def new_round(self, bank):
        self.bank_round[id(bank)] = set()

    def MM(self, bank, out, lhsT, rhs, kbase, ksize, qbase, qsize, r=()):
        rows = set(range(kbase // 32, (kbase + ksize + 31) // 32))
        quads = set(range(qbase // 32, (qbase + qsize + 31) // 32))
        cleared = self.bank_round.setdefault(id(bank), set())
        if quads <= cleared:
            start = False
        else:
            assert not (quads & cleared), (quads, cleared)
            start = True
            cleared |= quads
        last = self.bank_last.get(id(bank))
        extra = []
        if last is not None and not (last[1] & rows):
            extra.append(last[0])
        fn = lambda e: e.matmul(out=out, lhsT=lhsT, rhs=rhs, start=start, stop=True, skip_group_check=True)
        op = self.s.add("pe", fn, reads=[x.b for x in r], writes=[bank.b], extra=extra)
        self.bank_last[id(bank)] = (op, rows)
        return op

    def nextG(self):
        self.gflip ^= 1
        return self.psG[self.gflip]

    def nextA(self):
        self.aflip ^= 1
        return self.psA[self.aflip]

    def nextN(self):
        self.nflip ^= 1
        return self.psN[self.nflip]

    def build(self):
        nc = self.nc
        with self.es:
            self._declare_io()
            self._alloc()
            self._load_consts()
            try:
                for l in range(2):
                    self.layer_a(l)
                if self.stage != "A":
                    self.kv_phase()
                if self.stage not in ("A", "KV"):
                    for lb in range(2):
                        self.layer_b(lb)
            except _Stop:
                import os
                names = [n for n in os.environ.get("K_DUMP", "").split(",") if n]
                for i, n in enumerate(names):
                    t = getattr(self, n)
                    ap = t.h[:] if isinstance(t, T) else t.ap
                    shp = list(ap.shape)
                    dst = self.dout(f"dbg{i}", shp)
                    self.DMA("pool", dst, ap, t, r=[t])
            self.s.emit()
        return nc

    def _declare_io(self):
        I = self.din
        self.xp = I("xp", [SEQ, D])
        self.xs = I("xs", [NS, D])
        self.swkv = I("swkv", [2, SB, 16, 64, 64])
        self.sshift = I("sshift", [2, SB, D])
        self.a_vec = I("a_vec", [2, 128, NV, 8])
        self.a_w_in = I("a_w_in", [2, 4, D, D])
        self.a_w_out = I("a_w_out", [2, D, D])
        self.a_wla = I("a_w_lora_a", [2, D, 64])
        self.a_wlb = I("a_w_lora_b", [2, 64, D])
        self.a_ala = I("a_a_lora_a", [2, D, 64])
        self.a_alb = I("a_a_lora_b", [2, 64, D])
        self.a_vla = I("a_v_lora_a", [1, D, 32])
        self.a_vlb = I("a_v_lora_b", [1, 32, D])
        self.b_vec = I("b_vec", [128, 4, NCH])
        self.w_kv = I("w_kv", [D, 6 * D])
        self.b_w_in = I("b_w_in", [2, D, 4 * D])
        self.b_w_out = I("b_w_out", [2, D, D])
        self.final_ln = I("final_ln", [D])
        self.c_ropeP = I("c_ropeP", [128, NTILE + 1, 2, 32])
        self.c_ropeT = I("c_ropeT", [128, 2, SEQ + NS])
        self.c_maskP = I("c_maskP", [128, 2, 128])
        self.c_prot = I("c_prot", [128, 128])
        self.c_maskS = I("c_maskS", [128, 12, ST_])
        self.ckv = [I(f"ckv{g}", [SB, w, 2, D]) for g, w in enumerate((128, 512, 2048))]
        self.c_ident = I("c_ident", [128, 128])
        self.c_maskA = I("c_maskA", [128, 4, 128])
        self.c_maskAT = I("c_maskAT", [128, 128])
        self.c_blk = I("c_blk", [128, 128])
        self.c_scan = I("c_scan", [128, 2, 128])
        O = self.dout
        self.o_yp = O("y_prompt", [SEQ, D])
        self.o_ys = O("y_sample", [NS, D])
        self.o_wkvp = O("wkv_p", [2, 16, 64, 64])
        self.o_wkvs = O("wkv_s", [2, SB, 16, 64, 64])
        self.o_shp = O("shift_p", [2, D])
        self.o_shs = O("shift_s", [2, SB, D])
        self.o_kvp = [O(f"kv{g}p", [w, 2, D]) for g, w in enumerate((128, 512, 2048))]
        self.o_kvs = [O(f"kv{g}s", [NS, 2, D]) for g in range(3)]
        self.d_kT = self.dint("d_kT", [3, NCH, 128, SEQ + NS], BF16)
        self.d_v = self.dint("d_v", [3, SEQ + NS, D], BF16)
        self.kTbuf = [Buf(f"kTd{g}") for g in range(3)]
        self.vbuf = [Buf(f"vd{g}") for g in range(3)]
        self.d_vf = self.dint("d_vf", [NTILE + 1, 128, NCH, 128])
        self.xbuf = [Buf(f"xd{i}") for i in range(NTILE + 1)]
        self.vfbuf = [Buf(f"vfd{i}") for i in range(NTILE + 1)]

    def x_dram(self, ti):
        if ti < NTILE:
            return self.o_yp[ti * 128:(ti + 1) * 128, :]
        return self.o_ys[:, :]

    def x_in(self, ti):
        if ti < NTILE:
            return self.xp[ti * 128:(ti + 1) * 128, :]
        return self.xs[:, :]

    def _alloc(self):
        sb = self.sb
        nc = self.nc
        banks = [T(self.es.enter_context(nc.psum_tensor(f"ps{i}", [128, 512], F32)), f"ps{i}") for i in range(8)]
        for bk_ in banks:
            bk_.psum = True
        self.psG = banks[0:2]
        self.psA = banks[2:4]
        self.psN = banks[4:6]
        self.psX = banks[6]
        self.psY = banks[7]
        self.identf = sb("identf", [128, 128], F32)
        self.identb = sb("identb", [128, 128], BF16)
        self.maskA = sb("maskA", [128, 4, 128], BF16)
        self.maskAT = sb("maskAT", [128, 128], BF16)
        self.blkf = sb("blkf", [128, 128], F32)
        self.blkb = sb("blkb", [128, 128], BF16)
        self.scanm = sb("scanm", [128, 2, 128], F32)
        self.W = [sb(f"W{p}", [128, NCH, D], BF16) for p in range(5)]
        self.Wla = sb("Wla", [128, NCH, 64], BF16)
        self.Ala = sb("Ala", [128, NCH, 64], BF16)
        self.Vla = sb("Vla", [128, NCH, 32], BF16)
        self.Wlb = sb("Wlb", [64, D], BF16)
        self.Alb = sb("Alb", [64, D], BF16)
        self.Vlb = sb("Vlb", [32, D], BF16)
        self.VEC = sb("VEC", [128, NV, NCH], F32)
        self.BV = sb("BV", [128, 4, NCH], F32)
        self.ropeS = sb("ropeS", [128, 2, 32], F32)
        self.maskP = sb("maskP", [128, 2, 128], BF16)
        self.ones_b = sb("ones_b", [128, 64], BF16)
        self.prot = sb("prot", [128, 128], BF16)
        self.XNs = sb("XNs", [128, NCH, NS], BF16)
        self.OGs = sb("OGs", [128, NCH, NS], BF16)
        self.QTs = sb("QTs", [128, NCH, 3, NS], BF16)
        self.zTs = sb("zTs", [128, NCH, NS], BF16)
        self.KTs = sb("KTs", [128, 3, NCH, NS], BF16)
        self.ropeTs = sb("ropeTs", [128, 2, NS], BF16)
        self.Xt = sb("Xt", [128, D], F32)
        self.junk = sb("junk", [128, D], BF16)
        self.st1 = sb("st1", [128, 4], F32)
        FM = lambda n, dt=F32: sb(n, [128, NCH, 128], dt)
        self.F = [FM(f"F{i}") for i in range(8)]
        F = self.F
        flat = lambda t: Vw(t, t.h[:, :, :].rearrange("p a b -> p (a b)"))
        st4 = lambda t: Vw(t, t.h[0:64, :, :].rearrange("p a (b c) -> p a b c", c=64))
        self.xr = flat(F[7])
        self.xnT = F[4]
        self.dxT = F[5]
        self.carry = sb("carry", [128, NCH, 1], F32)
        self.shiftT = sb("shiftT", [128, NCH, SB], F32)
        self.shtok = flat(F[0])
        self.rowo = flat(F[0])
        self.MX = [FM(f"MX{i}", BF16) for i in range(2)]
        self.rT = FM("rT")
        self.kT = FM("kT")
        self.vT = FM("vT")
        self.vfT = F[6]
        self.szT = FM("szT", BF16)
        self.hid = sb("hid", [64, 128], F32)
        self.hidb = sb("hidb", [64, 128], BF16)
        self.hids = [(self.hid, self.hidb), (sb("hid2", [64, 128], F32), sb("hidb2", [64, 128], BF16))]
        self.AR = sb("AR", [128, NCH, 2, 128], BF16)
        self.BK = sb("BK", [128, NCH, 2, 128], BF16)
        self.vTb = FM("vTb", BF16)
        self.BhT = FM("BhT", BF16)
        self.KhT = FM("KhT", BF16)
        self.Vt = sb("Vt", [128, D], BF16)
        self.Bh = sb("Bh", [128, D], BF16)
        self.Kh = sb("Kh", [128, D], BF16)
        self.WL = sb("WL", [128, NCH, SB], F32)
        self.Asb = sb("Asb", [128, 8, 4, 128], BF16)
        self.Nsb = [sb(f"Nsb{i}", [128, 8, 2, 128], BF16) for i in range(2)]
        self.Xb = sb("Xb", [128, 8, 64], BF16)
        self.STf = sb("STf", [128, NCH, 64], F32)
        self.STb = sb("STb", [128, NCH, 64], BF16)
        self.STfG = [Al(f"STfG{i}", self.STf.h[:, 2 * i:2 * i + 2, :]) for i in range(4)]
        self.STbG = [Al(f"STbG{i}", self.STb.h[:, 2 * i:2 * i + 2, :]) for i in range(4)]
        self.SI = st4(F[3])
        self.SO = st4(F[1])
        self.YT = F[6]
        self.OT = FM("OT", BF16)

    def _load_consts(self):
        D_ = self.DMA
        D_("sp", self.identf[:], self.c_ident[:, :], self.identf, w=[self.identf])
        D_("pool", self.identb[:], self.c_ident[:, :], self.identb, w=[self.identb])
        D_("pool", self.maskA[:], self.c_maskA[:, :, :], self.maskA, w=[self.maskA])
        D_("pool", self.maskAT[:], self.c_maskAT[:, :], self.maskAT, w=[self.maskAT])
        D_("sp", self.blkf[:], self.c_blk[:, :], self.blkf, w=[self.blkf])
        D_("pool", self.blkb[:], self.c_blk[:, :], self.blkb, w=[self.blkb])
        D_("sp", self.scanm[:], self.c_scan[:, :, :], self.scanm, w=[self.scanm])
        D_("sp", self.BV[:], self.b_vec[:, :, :], self.BV, w=[self.BV])
        D_("sp", self.ropeS[:], self.c_ropeP[:, NTILE, :, :], self.ropeS, w=[self.ropeS])
        D_("pool", self.maskP[:], self.c_maskP[:, :, :], self.maskP, w=[self.maskP])
        D_("pool", self.prot[:], self.c_prot[:, :], self.prot, w=[self.prot])
        self.A("pool", lambda e: e.memset(self.ones_b[:], 1.0), w=[self.ones_b])

    def load_layer_a_weights(self, l):
        D_ = self.DMA
        for p in range(5):
            src = self.a_w_in[l, p] if p < 4 else self.a_w_out[l]
            srcv = src.rearrange("(c p) e -> p c e", p=128)
            for c in range(NCH):
                D_("pool", self.W[p][:, c, :], srcv[:, c, :], self.W[p], w=[self.W[p]])
        D_("pool", self.Wla[:], self.a_wla[l].rearrange("(c p) k -> p c k", p=128), self.Wla, w=[self.Wla])
        D_("pool", self.Ala[:], self.a_ala[l].rearrange("(c p) k -> p c k", p=128), self.Ala, w=[self.Ala])
        D_("pool", self.Wlb[:], self.a_wlb[l], self.Wlb, w=[self.Wlb])
        D_("pool", self.Alb[:], self.a_alb[l], self.Alb, w=[self.Alb])
        if l == 1:
            D_("pool", self.Vla[:], self.a_vla[0].rearrange("(c p) k -> p c k", p=128), self.Vla, w=[self.Vla])
            D_("pool", self.Vlb[:], self.a_vlb[0], self.Vlb, w=[self.Vlb])
        D_("sp", self.VEC[:], self.a_vec[l], self.VEC, w=[self.VEC])
        V = self.VEC
        A = self.A
        A("dve", lambda e: e.tensor_scalar(out=V[:, V_NW0:V_NW0 + 3, :], in0=V[:, V_W0:V_W0 + 3, :], scalar1=-1.0,
                                           scalar2=None, op0=ALU.mult), r=[V], w=[V])
        A("dve", lambda e: e.tensor_scalar(out=V[:, V_OMK, :], in0=V[:, V_KA, :], scalar1=-1.0, scalar2=1.0,
                                           op0=ALU.mult, op1=ALU.add), r=[V], w=[V])

    def layer_a(self, l):
        A = self.A
        self.load_layer_a_weights(l)
        A("pool", lambda e: e.memset(self.STf[:], 0.0), w=self.STfG)
        A("pool", lambda e: e.memset(self.STb[:], 0.0), w=self.STbG)
        A("pool", lambda e: e.memset(self.carry[:], 0.0), w=[self.carry])
        for ti in range(NTILE):
            self.tile_a(l, ti, 128, sample=False)
        self.store_state(self.o_wkvp[l])
        self.store_shift_prompt(l)
        self.load_shift_sample(l)
        self.tile_a(l, NTILE, NS, sample=True)

    def store_state(self, dst):
        A = self.A
        for half in range(2):
            ps = self.nextG()
            psv = ps[:, :].rearrange("p (a b) -> p a b", b=128)
            for q in range(4):
                hp = half * 4 + q
                A("pe", lambda e, psv=psv, q=q, hp=hp: e.transpose(out=psv[0:64, q, :], in_=self.STf[:, hp, :],
                                                                   identity=self.identf[:, :]),
                  r=self.STfG + [self.identf], w=[ps])
            A("act", lambda e, psv=psv, half=half: e.activation(
                out=self.SO[:, half * 4:(half + 1) * 4, :, :].rearrange("p a b c -> p a (b c)"),
                in_=psv[0:64, :, :], func=AF.Copy), r=[ps], w=[self.SO])
        self.DMA("sp", dst.rearrange("(hp hh) i j -> i hp hh j", hh=2), self.SO[:, :, :, :], self.SO, r=[self.SO])

    def load_state(self, src):
        A = self.A
        self.DMA("sp", self.SI[:, :, :, :], src.rearrange("(hp hh) i j -> i hp hh j", hh=2), self.SI, w=[self.SI])
        self.ck(111)
        ps = self.nextG()
        psv = ps[:, :].rearrange("p (a b) -> p a b", b=64)
        for hp in range(NCH):
            A("pe", lambda e, psv=psv, hp=hp: e.transpose(
                out=psv[:, hp, :], in_=self.SI[:, hp, :, :].rearrange("p a b -> p (a b)"),
                identity=self.identf[0:64, 0:64]), r=[self.SI, self.identf], w=[ps])
        self.ck(112)
        A("act", lambda e, psv=psv: e.activation(out=self.STf[:], in_=psv[:, :, :], func=AF.Copy), r=[ps], w=self.STfG)
        self.ck(113)
        A("dve", lambda e, psv=psv: e.tensor_copy(out=self.STb[:], in_=psv[:, :, :]), r=[ps], w=self.STbG)

    def store_shift_prompt(self, l):
        A = self.A
        ps = self.nextG()
        for half in range(2):
            if half == 1:
                ps2 = self.nextG()
            else:
                ps2 = ps
            for q in range(4):
                c = half * 4 + q
                A("pe", lambda e, ps2=ps2, q=q, c=c: e.transpose(out=ps2[0:1, q * 128:(q + 1) * 128],
                                                                  in_=self.carry[:, c, 0:1], identity=self.identf[:, :]),
                  r=[self.carry, self.identf], w=[ps2])
            A("act", lambda e, ps2=ps2, half=half: e.activation(out=self.rowo[0:1, half * 512:(half + 1) * 512],
                                                                in_=ps2[0:1, :], func=AF.Copy), r=[ps2], w=[self.rowo])
        self.DMA("sp", self.o_shp[l:l + 1, :], self.rowo[0:1, :], self.rowo, r=[self.rowo])

    def load_shift_sample(self, l):
        A = self.A
        self.DMA("sp", self.shtok[0:SB, :], self.sshift[l], self.shtok, w=[self.shtok])
        ps = self.nextG()
        psv = ps[:, 0:NCH * SB].rearrange("p (a b) -> p a b", b=SB)
        for c in range(NCH):
            A("pe", lambda e, psv=psv, c=c: e.transpose(out=psv[:, c, :], in_=self.shtok[0:SB, c * 128:(c + 1) * 128],
                                                        identity=self.identf[0:SB, 0:SB]),
              r=[self.shtok, self.identf], w=[ps])
        A("act", lambda e, psv=psv: e.activation(out=self.shiftT[:], in_=psv[:, :, :], func=AF.Copy), r=[ps], w=[self.shiftT])

    def tile_a(self, l, ti, nt, sample):
        self.cur = (l, ti)
        A = self.A
        V = self.VEC
        Xt, xr, xnT, dxT = self.Xt, self.xr, self.xnT, self.dxT
        F = self.F
        if l == 0:
            self.DMA("sp", Xt[:nt, :], self.x_in(ti), Xt, w=[Xt])
        else:
            self.DMA("sp", Xt[:nt, :], self.x_dram(ti), Xt, r=[self.xbuf[ti]], w=[Xt])
        st1 = self.st1
        A("pool", lambda e: e.memset(st1[:], 0.0), w=[st1])
        A("act", lambda e: e.activation(out=self.junk[:nt, :], in_=Xt[:nt, :], func=AF.Square, accum_out=st1[:nt, 0:1]),
          r=[Xt, st1], w=[self.junk, st1])
        A("act", lambda e: e.activation(out=st1[:nt, 1:2], in_=st1[:nt, 0:1], func=AF.Ln, scale=1.0 / D, bias=RMS_EPS),
          r=[st1], w=[st1])
        A("act", lambda e: e.activation(out=st1[:nt, 2:3], in_=st1[:nt, 1:2], func=AF.Exp, scale=-0.5), r=[st1], w=[st1])
        A("act", lambda e: e.activation(out=xr[:nt, :], in_=Xt[:nt, :], func=AF.Identity, scale=st1[:nt, 2:3]),
          r=[Xt, st1], w=[xr])
        self.ck(1)
        for half in range(2):
            ps = self.nextG()
            psv = ps[:, :].rearrange("p (a b) -> p a b", b=128)
            for q in range(4):
                c = half * 4 + q
                A("pe", lambda e, psv=psv, q=q, c=c: e.transpose(out=psv[:, q, :nt], in_=xr[:nt, c * 128:(c + 1) * 128],
                                                                 identity=self.identf[:nt, :nt]),
                  r=[xr, self.identf], w=[ps])
            A("dve", lambda e, psv=psv, half=half: e.tensor_tensor(
                out=xnT[:, half * 4:(half + 1) * 4, :nt], in0=psv[:, :, :nt],
                in1=bc(V[:, V_LN, half * 4:(half + 1) * 4].unsqueeze(2), [128, 4, nt]), op=ALU.mult),
              r=[ps, V], w=[xnT])
        self.ck(2)
        if not sample:
            A("dve", lambda e: e.tensor_tensor(out=dxT[:, :, 1:nt], in0=xnT[:, :, 0:nt - 1], in1=xnT[:, :, 1:nt],
                                               op=ALU.subtract), r=[xnT], w=[dxT])
            A("dve", lambda e: e.tensor_tensor(out=dxT[:, :, 0:1], in0=self.carry[:, :, 0:1], in1=xnT[:, :, 0:1],
                                               op=ALU.subtract), r=[xnT, self.carry], w=[dxT])
            A("dve", lambda e: e.tensor_copy(out=self.carry[:, :, 0:1], in_=xnT[:, :, nt - 1:nt]), r=[xnT], w=[self.carry])
        else:
            x4 = xnT[:, :, 0:NS].rearrange("p c (b t) -> p c b t", t=ST_)
            d4 = dxT[:, :, 0:NS].rearrange("p c (b t) -> p c b t", t=ST_)
            A("dve", lambda e: e.tensor_tensor(out=d4[:, :, :, 1:ST_], in0=x4[:, :, :, 0:ST_ - 1], in1=x4[:, :, :, 1:ST_],
                                               op=ALU.subtract), r=[xnT], w=[dxT])
            A("dve", lambda e: e.tensor_tensor(out=d4[:, :, :, 0:1], in0=self.shiftT[:].unsqueeze(3), in1=x4[:, :, :, 0:1],
                                               op=ALU.subtract), r=[xnT, self.shiftT], w=[dxT])
            for half in range(2):
                ps = self.nextG()
                for q in range(4):
                    c = half * 4 + q
                    A("pe", lambda e, ps=ps, q=q, c=c: e.transpose(out=ps[0:SB, q * 128:(q + 1) * 128],
                                                                    in_=x4[:, c, :, ST_ - 1], identity=self.identf[:, :]),
                      r=[xnT, self.identf], w=[ps])
                A("act", lambda e, ps=ps, half=half: e.activation(out=self.rowo[0:SB, half * 512:(half + 1) * 512],
                                                                  in_=ps[0:SB, :], func=AF.Copy), r=[ps], w=[self.rowo])
            self.DMA("sp", self.o_shs[l], self.rowo[0:SB, :], self.rowo, r=[self.rowo])

        def mix(p, dst):
            for c in range(NCH):
                eng = "dve"
                A(eng, lambda e, c=c: e.scalar_tensor_tensor(out=dst[:, c, :nt], in0=dxT[:, c, :nt],
                                                             scalar=V[:, V_MU + p, c:c + 1], in1=xnT[:, c, :nt],
                                                             op0=ALU.mult, op1=ALU.add), r=[dxT, xnT, V], w=[dst])

        def proj(Wt, src, evac):
            for half in range(2):
                ps = self.nextG()
                psv = ps[:, :].rearrange("p (a b) -> p a b", b=128)
                for q in range(4):
                    eo = half * 4 + q
                    for c in range(NCH):
                        A("pe", lambda e, psv=psv, q=q, eo=eo, c=c: e.matmul(
                            out=psv[:, q, :nt], lhsT=Wt[:, c, eo * 128:(eo + 1) * 128], rhs=src[:, c, :nt],
                            start=(c == 0), stop=(c == NCH - 1)), r=[Wt, src], w=[ps])
                evac(half, ps, psv)

        def evac_copy(dst, eng="act"):
            def f(half, ps, psv):
                if eng == "act":
                    A("act", lambda e: e.activation(out=dst[:, half * 4:(half + 1) * 4, :nt], in_=psv[:, :, :nt], func=AF.Copy),
                      r=[ps], w=[dst])
                else:
                    A("dve", lambda e: e.tensor_copy(out=dst[:, half * 4:(half + 1) * 4, :nt], in_=psv[:, :, :nt]),
                      r=[ps], w=[dst])
            return f

        def sigmoid_from_psum(dst, negbias_slot):
            def f(half, ps, psv):
                for q in range(4):
                    c = half * 4 + q
                    A("act", lambda e, q=q, c=c: e.activation(out=dst[:, c, :nt], in_=psv[:, q, :nt], func=AF.Exp,
                                                              scale=-1.0, bias=V[:, negbias_slot, c:c + 1]),
                      r=[ps, V], w=[dst])
                if half == 1:
                    A("act", lambda e: e.activation(out=dst[:, :, :nt], in_=dst[:, :, :nt], func=AF.Ln, bias=1.0), r=[dst], w=[dst])
                    A("act", lambda e: e.activation(out=dst[:, :, :nt], in_=dst[:, :, :nt], func=AF.Exp, scale=-1.0), r=[dst], w=[dst])
            return f

        def lora_hidden(Wa, src, nh, tanh, hset):
            ps = self.nextG()
            for c in range(NCH):
                A("pe", lambda e, ps=ps, c=c: e.matmul(out=ps[0:nh, 0:nt], lhsT=Wa[:, c, :], rhs=src[:, c, :nt],
                                                       start=(c == 0), stop=(c == NCH - 1)), r=[Wa, src], w=[ps])
            hid, hidb = self.hids[hset]
            if tanh:
                A("act", lambda e: e.activation(out=hid[0:nh, :nt], in_=ps[0:nh, 0:nt], func=AF.Exp, scale=2.0), r=[ps], w=[hid])
                A("dve", lambda e: e.tensor_scalar(out=hid[0:nh, :nt], in0=hid[0:nh, :nt], scalar1=1.0, scalar2=None, op0=ALU.add),
                  r=[hid], w=[hid])
                A("dve", lambda e: e.reciprocal(out=hid[0:nh, :nt], in_=hid[0:nh, :nt]), r=[hid], w=[hid])
                A("dve", lambda e: e.tensor_scalar(out=hidb[0:nh, :nt], in0=hid[0:nh, :nt], scalar1=-2.0, scalar2=1.0,
                                                   op0=ALU.mult, op1=ALU.add), r=[hid], w=[hidb])
            else:
                A("act", lambda e: e.activation(out=hidb[0:nh, :nt], in_=ps[0:nh, 0:nt], func=AF.Copy), r=[ps], w=[hidb])

        def lora_out(Wb, nh, evac, hset):
            hid, hidb = self.hids[hset]
            for half in range(2):
                ps2 = self.nextG()
                psv = ps2[:, :].rearrange("p (a b) -> p a b", b=128)
                for q in range(4):
                    eo = half * 4 + q
                    A("pe", lambda e, psv=psv, q=q, eo=eo: e.matmul(out=psv[:, q, :nt], lhsT=Wb[0:nh, eo * 128:(eo + 1) * 128],
                                                                    rhs=hidb[0:nh, :nt], start=True, stop=True),
                      r=[Wb, hidb], w=[ps2])
                evac(half, ps2, psv)

        def lora(Wa, Wb, src, nh, tanh, evac):
            lora_hidden(Wa, src, nh, tanh, 0)
            lora_out(Wb, nh, evac, 0)

        rT, kT, vT, szT = self.rT, self.kT, self.vT, self.szT
        MX = self.MX
        recw = F[1]
        aT = F[2]
        ez = F[0]
        mix(4, MX[0]); mix(5, MX[1])
        lora_hidden(self.Wla, MX[0], 64, True, 0)
        lora_hidden(self.Ala, MX[1], 64, False, 1)
        mix(0, MX[0]); proj(self.W[0], MX[0], evac_copy(rT, "act"))
        self.ck(4)
        lora_out(self.Wlb, 64, sigmoid_from_psum(recw, V_NW0), 0)
        mix(1, MX[1]); proj(self.W[1], MX[1], evac_copy(kT, "dve"))
        lora_out(self.Alb, 64, sigmoid_from_psum(aT, V_NA0), 1)
        mix(2, MX[0]); proj(self.W[2], MX[0], evac_copy(vT, "act"))
        if l == 1:
            gv = F[0]
            lora(self.Vla, self.Vlb, MX[0], 32, False, sigmoid_from_psum(gv, V_NV0))
            vf = self.vfT
            self.DMA("sp", vf[:, :, :nt], self.d_vf[ti, :, :, 0:nt], vf, r=[self.vfbuf[ti]], w=[vf])
            A("dve", lambda e: e.tensor_tensor(out=vf[:, :, :nt], in0=vf[:, :, :nt], in1=vT[:, :, :nt], op=ALU.subtract),
              r=[vf, vT], w=[vf])
            A("dve", lambda e: e.tensor_tensor(out=vf[:, :, :nt], in0=vf[:, :, :nt], in1=gv[:, :, :nt], op=ALU.mult),
              r=[vf, gv], w=[vf])
            A("dve", lambda e: e.tensor_tensor(out=vT[:, :, :nt], in0=vT[:, :, :nt], in1=vf[:, :, :nt], op=ALU.add),
              r=[vf, vT], w=[vT])
        else:
            self.DMA("sp", self.d_vf[ti, :, :, 0:nt], vT[:, :, :nt], vT, r=[vT], w=[self.vfbuf[ti]])
        self.ck(5)

        def evac_z(half, ps, psv):
            sl = slice(half * 4, (half + 1) * 4)
            A("act", lambda e: e.activation(out=ez[:, sl, :nt], in_=psv[:, :, :nt], func=AF.Exp, scale=-1.0), r=[ps], w=[ez])
            A("act", lambda e: e.activation(out=ez[:, sl, :nt], in_=ez[:, sl, :nt], func=AF.Ln, bias=1.0), r=[ez], w=[ez])
            A("act", lambda e: e.activation(out=ez[:, sl, :nt], in_=ez[:, sl, :nt], func=AF.Exp, scale=-1.0), r=[ez], w=[ez])
            A("dve", lambda e: e.tensor_tensor(out=szT[:, sl, :nt], in0=psv[:, :, :nt], in1=ez[:, sl, :nt], op=ALU.mult),
              r=[ps, ez], w=[szT])
        mix(3, MX[1]); proj(self.W[3], MX[1], evac_z)
        self.ck(6)
        self.ck(7)
        cum = F[3]
        sm = 1 if sample else 0
        for c in range(NCH):
            A("dve", lambda e, c=c: e.tensor_tensor_scan(out=cum[:, c, :nt], data0=self.scanm[:, sm, :nt], data1=recw[:, c, :nt],
                                                         initial=0.0, op0=ALU.mult, op1=ALU.add), r=[recw, self.scanm], w=[cum])
        A("pool", lambda e: e.tensor_tensor(out=recw[:, :, :nt], in0=cum[:, :, :nt], in1=recw[:, :, :nt], op=ALU.subtract),
          r=[cum, recw], w=[recw])
        Epos, Eneg, Eprev = F[4], F[5], F[6]
        A("act", lambda e: e.activation(out=Epos[:, :, :nt], in_=cum[:, :, :nt], func=AF.Exp, scale=-CDEC), r=[cum], w=[Epos])
        A("act", lambda e: e.activation(out=Eneg[:, :, :nt], in_=cum[:, :, :nt], func=AF.Exp, scale=CDEC), r=[cum], w=[Eneg])
        A("act", lambda e: e.activation(out=Eprev[:, :, :nt], in_=recw[:, :, :nt], func=AF.Exp, scale=-CDEC), r=[recw], w=[Eprev])
        nb = SB if sample else 1
        L = ST_ if sample else 128
        WL = self.WL
        cum4 = cum[:, :, 0:nb * L].rearrange("p c (b t) -> p c b t", t=L)
        A("act", lambda e: e.activation(out=WL[:, :, 0:nb], in_=cum4[:, :, :, L - 1], func=AF.Exp, scale=-CDEC), r=[cum], w=[WL])
        self.ck(8)
        kk = F[7]
        sq = self.junk
        sqv = sq[:, :].rearrange("p (a b) -> p a b", b=128)
        for c in range(NCH):
            A("act", lambda e, c=c: e.activation(out=sqv[:, c, :nt], in_=kT[:, c, :nt], func=AF.Square, scale=V[:, V_KK, c:c + 1]),
              r=[kT, V], w=[sq])
        rs = F[0]
        for half in range(2):
            ps = self.nextG()
            psv = ps[:, :].rearrange("p (a b) -> p a b", b=128)
            for q in range(4):
                c = half * 4 + q
                A("pe", lambda e, psv=psv, q=q, c=c: e.matmul(out=psv[:, q, :nt], lhsT=self.blkb[:, :], rhs=sqv[:, c, :nt],
                                                              start=True, stop=True), r=[self.blkb, sq], w=[ps])
            sl = slice(half * 4, (half + 1) * 4)
            A("act", lambda e, psv=psv, sl=sl: e.activation(out=rs[:, sl, :nt], in_=psv[:, :, :nt], func=AF.Ln, bias=1e-24),
              r=[ps], w=[rs])
        A("act", lambda e: e.activation(out=rs[:, :, :nt], in_=rs[:, :, :nt], func=AF.Exp, scale=-0.5), r=[rs], w=[rs])
        for c in range(NCH):
            eng = "dve"
            A(eng, lambda e, c=c: e.scalar_tensor_tensor(out=kk[:, c, :nt], in0=kT[:, c, :nt], scalar=V[:, V_KK, c:c + 1],
                                                         in1=rs[:, c, :nt], op0=ALU.mult, op1=ALU.mult), r=[kT, rs, V], w=[kk])
        tmp = F[0]
        for c in range(NCH):
            A("act", lambda e, c=c: e.activation(out=tmp[:, c, :nt], in_=aT[:, c, :nt], func=AF.Identity,
                                                 scale=V[:, V_KA, c:c + 1], bias=V[:, V_OMK, c:c + 1]), r=[aT, V], w=[tmp])
        A("dve", lambda e: e.tensor_tensor(out=kT[:, :, :nt], in0=kT[:, :, :nt], in1=tmp[:, :, :nt], op=ALU.mult),
          r=[kT, tmp], w=[kT])
        self.ck(9)
        AR, BK = self.AR, self.BK
        A("dve", lambda e: e.scalar_tensor_tensor(out=AR[:, :, 0, :nt], in0=kk[:, :, :nt], scalar=-1.0, in1=Eprev[:, :, :nt],
                                                  op0=ALU.mult, op1=ALU.mult), r=[kk, Eprev], w=[AR])
        A("pool", lambda e: e.tensor_tensor(out=AR[:, :, 1, :nt], in0=rT[:, :, :nt], in1=Epos[:, :, :nt], op=ALU.mult),
          r=[rT, Epos], w=[AR])
        A("dve", lambda e: e.tensor_tensor(out=aT[:, :, :nt], in0=aT[:, :, :nt], in1=kk[:, :, :nt], op=ALU.mult), r=[aT, kk], w=[aT])
        A("dve", lambda e: e.tensor_tensor(out=aT[:, :, :nt], in0=aT[:, :, :nt], in1=Eneg[:, :, :nt], op=ALU.mult), r=[aT, Eneg], w=[aT])
        A("pool", lambda e: e.tensor_tensor(out=Eneg[:, :, :nt], in0=Eneg[:, :, :nt], in1=kT[:, :, :nt], op=ALU.mult),
          r=[kT, Eneg], w=[Eneg])
        A("act", lambda e: e.activation(out=BK[:, :, 0, :nt], in_=aT[:, :, :nt], func=AF.Copy), r=[aT], w=[BK])
        A("act", lambda e: e.activation(out=BK[:, :, 1, :nt], in_=Eneg[:, :, :nt], func=AF.Copy), r=[Eneg], w=[BK])
        WLb = bc(WL[:, :, 0:nb].unsqueeze(3), [128, NCH, nb, L])
        v4 = lambda t: t[:, :, 0:nb * L].rearrange("p c (b t) -> p c b t", t=L)
        A("dve", lambda e: e.tensor_tensor(out=v4(self.BhT), in0=v4(aT), in1=WLb, op=ALU.mult), r=[aT, WL], w=[self.BhT])
        A("pool", lambda e: e.tensor_tensor(out=v4(self.KhT), in0=v4(Eneg), in1=WLb, op=ALU.mult), r=[Eneg, WL], w=[self.KhT])
        A("act", lambda e: e.activation(out=self.vTb[:, :, :nt], in_=vT[:, :, :nt], func=AF.Copy), r=[vT], w=[self.vTb])
        self.ck(10)
        YT = self.YT
        for b in range(nb):
            c0 = b * L
            if sample and not (_os.environ.get("K_NOLOAD") and b > 0):
                self.load_state(self.swkv[l, b])
            self.ck(110)
            for (srcT, dstt) in ((self.vTb, self.Vt), (self.BhT, self.Bh), (self.KhT, self.Kh)):
                ps = self.nextG()
                psb = ps.h.bitcast(BF16)
                for c in range(NCH):
                    A("pe", lambda e, psb=psb, c=c, srcT=srcT, c0=c0: e.transpose(out=psb[0:L, c * 128:(c + 1) * 128],
                                                                            in_=srcT[:, c, c0:c0 + L], identity=self.identb[:, :]),
                      r=[srcT, self.identb], w=[ps])
                A("act", lambda e, psb=psb, dstt=dstt: e.activation(out=dstt[0:L, :], in_=psb[0:L, :], func=AF.Copy),
                  r=[ps], w=[dstt])
            self.ck(11)
            self.wkv_chunk(L, c0, b)
            self.ck(12)
            if sample and not _os.environ.get("K_NOSTORE"):
                self.store_state(self.o_wkvs[l, b])
            self.ck(100 + b)
        self.ck(13)
        yc = F[1]
        for half in range(2):
            ps = self.nextG()
            psv = ps[:, :].rearrange("p (a b) -> p a b", b=128)
            sl = slice(half * 4, (half + 1) * 4)
            for q in range(4):
                c = half * 4 + q
                A("pe", lambda e, psv=psv, q=q, c=c: e.matmul(out=psv[:, q, :nt], lhsT=self.blkf[:, :], rhs=YT[:, c, :nt],
                                                              start=True, stop=True), r=[self.blkf, YT], w=[ps])
            A("dve", lambda e, psv=psv, sl=sl: e.scalar_tensor_tensor(out=yc[:, sl, :nt], in0=psv[:, :, :nt], scalar=-1.0 / 64,
                                                                      in1=YT[:, sl, :nt], op0=ALU.mult, op1=ALU.add),
              r=[ps, YT], w=[yc])
        ysq = F[3]
        A("act", lambda e: e.activation(out=ysq[:, :, :nt], in_=yc[:, :, :nt], func=AF.Square), r=[yc], w=[ysq])
        rstd = F[4]
        for half in range(2):
            ps = self.nextG()
            psv = ps[:, :].rearrange("p (a b) -> p a b", b=128)
            sl = slice(half * 4, (half + 1) * 4)
            for q in range(4):
                c = half * 4 + q
                A("pe", lambda e, psv=psv, q=q, c=c: e.matmul(out=psv[:, q, :nt], lhsT=self.blkf[:, :], rhs=ysq[:, c, :nt],
                                                              start=True, stop=True), r=[self.blkf, ysq], w=[ps])
            A("act", lambda e, psv=psv, sl=sl: e.activation(out=rstd[:, sl, :nt], in_=psv[:, :, :nt], func=AF.Ln,
                                                            scale=1.0 / 64, bias=GN_EPS), r=[ps], w=[rstd])
        A("act", lambda e: e.activation(out=rstd[:, :, :nt], in_=rstd[:, :, :nt], func=AF.Exp, scale=-0.5), r=[rstd], w=[rstd])
        A("dve", lambda e: e.tensor_tensor(out=yc[:, :, :nt], in0=yc[:, :, :nt], in1=rstd[:, :, :nt], op=ALU.mult),
          r=[yc, rstd], w=[yc])
        for c in range(NCH):
            A("act", lambda e, c=c: e.activation(out=yc[:, c, :nt], in_=yc[:, c, :nt], func=AF.Identity,
                                                 scale=V[:, V_GNW, c:c + 1], bias=V[:, V_GNB, c:c + 1]), r=[yc, V], w=[yc])
        rk = F[5]
        for c in range(NCH):
            eng = "dve"
            A(eng, lambda e, c=c: e.scalar_tensor_tensor(out=rk[:, c, :nt], in0=rT[:, c, :nt], scalar=V[:, V_RK, c:c + 1],
                                                         in1=kT[:, c, :nt], op0=ALU.mult, op1=ALU.mult), r=[rT, kT, V], w=[rk])
        for half in range(2):
            ps = self.nextG()
            psv = ps[:, :].rearrange("p (a b) -> p a b", b=128)
            sl = slice(half * 4, (half + 1) * 4)
            for q in range(4):
                c = half * 4 + q
                A("pe", lambda e, psv=psv, q=q, c=c: e.matmul(out=psv[:, q, :nt], lhsT=self.blkf[:, :], rhs=rk[:, c, :nt],
                                                              start=True, stop=True), r=[self.blkf, rk], w=[ps])
            A("dve", lambda e, psv=psv, sl=sl: e.tensor_tensor(out=ysq[:, sl, :nt], in0=psv[:, :, :nt], in1=vT[:, sl, :nt],
                                                               op=ALU.mult), r=[ps, vT], w=[ysq])
        A("dve", lambda e: e.tensor_tensor(out=yc[:, :, :nt], in0=yc[:, :, :nt], in1=ysq[:, :, :nt], op=ALU.add), r=[yc, ysq], w=[yc])
        OT = self.OT
        A("dve", lambda e: e.tensor_tensor(out=OT[:, :, :nt], in0=yc[:, :, :nt], in1=szT[:, :, :nt], op=ALU.mult), r=[yc, szT], w=[OT])
        self.ck(14)
        Wo = self.W[4]
        for half in range(2):
            ps = self.nextG()
            for c in range(NCH):
                A("pe", lambda e, ps=ps, c=c, half=half: e.matmul(out=ps[:nt, :], lhsT=OT[:, c, :nt],
                                                                  rhs=Wo[:, c, half * 512:(half + 1) * 512],
                                                                  start=(c == 0), stop=(c == NCH - 1)), r=[OT, Wo], w=[ps])
            A("dve", lambda e, ps=ps, half=half: e.tensor_tensor(out=Xt[:nt, half * 512:(half + 1) * 512], in0=ps[:nt, :],
                                                                 in1=Xt[:nt, half * 512:(half + 1) * 512], op=ALU.add),
              r=[ps, Xt], w=[Xt])
        self.DMA("sp", self.x_dram(ti), Xt[:nt, :], Xt, r=[Xt], w=[self.xbuf[ti]])
        self.ck(15)

    def wkv_chunk(self, L, c0, b):
        if not hasattr(self, "AsbG"):
            self.AsbG = [Al(f"AsbG{i}", self.Asb.h[:, 4 * i:4 * i + 4, :, :]) for i in range(2)]
            self.NsbG = [[Al(f"NsbG{j}{i}", self.Nsb[j].h[:, 4 * i:4 * i + 4, :, :]) for i in range(2)] for j in range(2)]
            self.XbG = [Al(f"XbG{i}", self.Xb.h[:, 4 * i:4 * i + 4, :]) for i in range(2)]
        A = self.A
        MM = self.MM
        AR, BK = self.AR, self.BK
        Vt, Bh, Kh, YT, WL = self.Vt, self.Bh, self.Kh, self.YT, self.WL
        nst = max(1, int(math.log2(L)))
        cs = slice(c0, c0 + L)
        psXs = [self.psX, self.psY]
        for pi in range(2):
            grps = [2 * pi, 2 * pi + 1]
            Ncur = {}
            for gi in grps:
                Asb = self.AsbG[gi % 2]
                for hl in range(4):
                    h = 4 * gi + hl
                    hp, pb = h // 2, 64 * (h % 2)
                    ps = self.psA[h % 2]
                    self.new_round(ps)
                    pv = ps[:, :].rearrange("p (a b) -> p a b", b=128)
                    for bk in range(2):
                        MM(ps, pv[0:L, 2 * bk:2 * bk + 2, 0:L], BK[pb:pb + 64, hp, bk, cs], AR[pb:pb + 64, hp, :, cs], pb, 64, 0, L,
                           r=[BK, AR])
                    A("dve", lambda e, pv=pv, hl=hl, Asb=Asb: e.tensor_tensor(out=Asb[0:L, hl, :, 0:L], in0=pv[0:L, :, 0:L],
                                                                             in1=self.maskA[0:L, :, 0:L], op=ALU.mult),
                      r=[ps, self.maskA], w=[Asb])
            for k2 in range(2):
                ps = self.psN[k2]
                self.new_round(ps)
                pv = ps[:, :].rearrange("p (a b) -> p a b", b=128)
                for gi in grps:
                    for j in range(2):
                        h = 4 * gi + 2 * j + k2
                        hp, pb = h // 2, 64 * (h % 2)
                        MM(ps, pv[0:L, 2 * (gi % 2) + j, 0:L], AR[pb:pb + 64, hp, 0, cs], BK[pb:pb + 64, hp, 0, cs], pb, 64, 0, L,
                           r=[BK, AR])
                for gi in grps:
                    N0 = self.NsbG[0][gi % 2]
                    sl0 = 2 * (gi % 2)
                    A("dve" if gi % 2 == 0 else "act", (lambda e, pv=pv, N0=N0, sl0=sl0, k2=k2: e.tensor_tensor(
                        out=N0[0:L, k2:4:2, 1, 0:L], in0=pv[0:L, sl0:sl0 + 2, 0:L],
                        in1=bc(self.maskAT[0:L, 0:L].unsqueeze(1), [L, 2, L]), op=ALU.mult)) if gi % 2 == 0 else
                      (lambda e, pv=pv, N0=N0, sl0=sl0, k2=k2: e.activation(out=N0[0:L, k2:4:2, 1, 0:L], in_=pv[0:L, sl0:sl0 + 2, 0:L],
                                                                            func=AF.Copy)),
                      r=[ps, self.maskAT], w=[N0])
            for gi in grps:
                N0 = self.NsbG[0][gi % 2]
                Asb = self.AsbG[gi % 2]
                if gi % 2 == 1:
                    A("pool", lambda e, N0=N0: e.tensor_tensor(out=N0[0:L, :, 1, 0:L], in0=N0[0:L, :, 1, 0:L],
                                                               in1=bc(self.maskAT[0:L, 0:L].unsqueeze(1), [L, 4, L]), op=ALU.mult),
                      r=[N0, self.maskAT], w=[N0])
                A("pool", lambda e, N0=N0, Asb=Asb: e.tensor_copy(out=N0[0:L, :, 0, 0:L], in_=Asb[0:L, :, 0, 0:L]), r=[Asb], w=[N0])
                Ncur[gi] = N0
            for gi in grps:
                Asb, Xb = self.AsbG[gi % 2], self.XbG[gi % 2]
                psX = psXs[gi % 2]
                pxv = psX[:, 0:256].rearrange("p (a b) -> p a b", b=64)
                self.new_round(psX)
                for hl in range(4):
                    h = 4 * gi + hl
                    hp, pb = h // 2, 64 * (h % 2)
                    STb = self.STbG[gi]
                    MM(psX, pxv[0:L, hl, :], AR[pb:pb + 64, hp, 0, cs], STb[pb:pb + 64, hp % 2, :], pb, 64, 0, L, r=[AR, STb])
                    MM(psX, pxv[0:L, hl, :], Asb[0:L, hl, 2, 0:L], Vt[0:L, h * 64:(h + 1) * 64], 0, L, 0, L, r=[Asb, Vt])
                A("act", lambda e, Xb=Xb, pxv=pxv: e.activation(out=Xb[0:L, :, :], in_=pxv[0:L, :, :], func=AF.Copy), r=[psX], w=[Xb])
            for k in range(nst):
                for gi in grps:
                    Xb = self.XbG[gi % 2]
                    psX = psXs[gi % 2]
                    pxv = psX[:, 0:256].rearrange("p (a b) -> p a b", b=64)
                    Nc = Ncur[gi]
                    for hl in range(4):
                        MM(psX, pxv[0:L, hl, :], Nc[0:L, hl, 0, 0:L], Xb[0:L, hl, :], 0, L, 0, L, r=[Nc, Xb])
                    (A("dve", lambda e, Xb=Xb, pxv=pxv: e.tensor_copy(out=Xb[0:L, :, :], in_=pxv[0:L, :, :]), r=[psX], w=[Xb])
                     if gi % 2 == 0 else
                     A("act", lambda e, Xb=Xb, pxv=pxv: e.activation(out=Xb[0:L, :, :], in_=pxv[0:L, :, :], func=AF.Copy), r=[psX], w=[Xb]))
                if k < nst - 1:
                    for gi in grps:
                        Nc = Ncur[gi]
                        Nn = self.NsbG[(k + 1) % 2][gi % 2]
                        for pr in range(2):
                            ps = self.psN[(2 * gi + pr) % 2]
                            self.new_round(ps)
                            pv = ps[:, :].rearrange("p (a b) -> p a b", b=128)
                            for k2 in range(2):
                                hl = 2 * pr + k2
                                MM(ps, pv[0:L, 2 * k2, 0:L], Nc[0:L, hl, 1, 0:L], Nc[0:L, hl, 0, 0:L], 0, L, 0, L, r=[Nc])
                                MM(ps, pv[0:L, 2 * k2 + 1, 0:L], Nc[0:L, hl, 0, 0:L], Nc[0:L, hl, 1, 0:L], 0, L, 0, L, r=[Nc])
                            if pr == 0:
                                A("dve", lambda e, pv=pv, pr=pr, Nn=Nn: e.tensor_copy(
                                    out=Nn[0:L, 2 * pr:2 * pr + 2, :, 0:L].rearrange("p a b c -> p (a b) c"), in_=pv[0:L, :, 0:L]),
                                  r=[ps], w=[Nn])
                            else:
                                A("act", lambda e, pv=pv, pr=pr, Nn=Nn: e.activation(
                                    out=Nn[0:L, 2 * pr:2 * pr + 2, :, 0:L].rearrange("p a b c -> p (a b) c"), in_=pv[0:L, :, 0:L],
                                    func=AF.Copy), r=[ps], w=[Nn])
                        Ncur[gi] = Nn
            for gi in grps:
                Asb, Xb = self.AsbG[gi % 2], self.XbG[gi % 2]
                STb, STf = self.STbG[gi], self.STfG[gi]
                psY = self.nextG()
                pyv = psY[:, 0:256].rearrange("p (a b) -> p a b", b=128)
                self.new_round(psY)
                for hl in range(4):
                    h = 4 * gi + hl
                    hp, pb = h // 2, 64 * (h % 2)
                    q = hl // 2
                    MM(psY, pyv[pb:pb + 64, q, 0:L], STb[pb:pb + 64, hp % 2, :], AR[pb:pb + 64, hp, 1, cs], pb, 64, pb, 64, r=[STb, AR])
                    MM(psY, pyv[pb:pb + 64, q, 0:L], Xb[0:L, hl, :], Asb[0:L, hl, 1, 0:L], 0, L, pb, 64, r=[Xb, Asb])
                    MM(psY, pyv[pb:pb + 64, q, 0:L], Vt[0:L, h * 64:(h + 1) * 64], Asb[0:L, hl, 3, 0:L], 0, L, pb, 64, r=[Vt, Asb])
                A("act", lambda e, gi=gi, pyv=pyv: e.activation(out=YT[:, 2 * gi:2 * gi + 2, cs], in_=pyv[:, :, 0:L], func=AF.Copy),
                  r=[psY], w=[YT])
                ps = self.nextG()
                psv = ps[:, 0:128].rearrange("p (a b) -> p a b", b=64)
                self.new_round(ps)
                for hl in range(4):
                    h = 4 * gi + hl
                    hp, pb = h // 2, 64 * (h % 2)
                    q = hl // 2
                    MM(ps, psv[pb:pb + 64, q, :], Bh[0:L, h * 64:(h + 1) * 64], Xb[0:L, hl, :], 0, L, pb, 64, r=[Bh, Xb])
                    MM(ps, psv[pb:pb + 64, q, :], Kh[0:L, h * 64:(h + 1) * 64], Vt[0:L, h * 64:(h + 1) * 64], 0, L, pb, 64, r=[Kh, Vt])
                sl = slice(2 * gi, 2 * gi + 2)
                A("dve", lambda e, sl=sl, STf=STf: e.tensor_tensor(out=STf[:, :, :], in0=STf[:, :, :],
                                                                   in1=bc(WL[:, sl, b:b + 1], [128, 2, 64]), op=ALU.mult),
                  r=[STf, WL], w=[STf])
                A("dve", lambda e, STf=STf, psv=psv: e.tensor_tensor(out=STf[:, :, :], in0=STf[:, :, :], in1=psv[:, :, :], op=ALU.add),
                  r=[STf, ps], w=[STf])
                A("act", lambda e, STf=STf, STb=STb: e.activation(out=STb[:, :, :], in_=STf[:, :, :], func=AF.Copy), r=[STf], w=[STb])

    def norm_T(self, ti, nt, vec_ap, dst_fn, load_from):
        A = self.A
        Xt, xr, st1 = self.Xt, self.xr, self.st1
        self.DMA("sp", Xt[:nt, :], load_from, Xt, r=[self.xbuf[ti]], w=[Xt])
        A("pool", lambda e: e.memset(st1[:], 0.0), w=[st1])
        A("act", lambda e: e.activation(out=self.junk[:nt, :], in_=Xt[:nt, :], func=AF.Square, accum_out=st1[:nt, 0:1]),
          r=[Xt, st1], w=[self.junk, st1])
        A("act", lambda e: e.activation(out=st1[:nt, 1:2], in_=st1[:nt, 0:1], func=AF.Ln, scale=1.0 / D, bias=RMS_EPS),
          r=[st1], w=[st1])
        A("act", lambda e: e.activation(out=st1[:nt, 2:3], in_=st1[:nt, 1:2], func=AF.Exp, scale=-0.5), r=[st1], w=[st1])
        A("act", lambda e: e.activation(out=xr[:nt, :], in_=Xt[:nt, :], func=AF.Identity, scale=st1[:nt, 2:3]),
          r=[Xt, st1], w=[xr])
        for half in range(2):
            ps = self.nextG()
            psv = ps[:, :].rearrange("p (a b) -> p a b", b=128)
            for q in range(4):
                c = half * 4 + q
                A("pe", lambda e, psv=psv, q=q, c=c: e.transpose(out=psv[:, q, :nt], in_=xr[:nt, c * 128:(c + 1) * 128],
                                                                 identity=self.identf[:nt, :nt]),
                  r=[xr, self.identf], w=[ps])
            for q in range(4):
                c = half * 4 + q
                dap, dT = dst_fn(c)
                if q % 2 == 0:
                    A("act", lambda e, psv=psv, q=q, c=c, dap=dap: e.activation(out=dap, in_=psv[:, q, :nt], func=AF.Identity,
                                                                                scale=vec_ap[:, c:c + 1]), r=[ps, self.BV], w=[dT])
                else:
                    A("dve", lambda e, psv=psv, q=q, c=c, dap=dap: e.tensor_scalar(out=dap, in0=psv[:, q, :nt], scalar1=vec_ap[:, c:c + 1],
                                                                                   scalar2=None, op0=ALU.mult), r=[ps, self.BV], w=[dT])

    def kv_phase(self):
        A = self.A
        self.s.fence()
        self.ropeP = Al("ropeP", self.F[1].h[:, :, :].rearrange("p a b -> p (a b)").rearrange("p (t k d) -> p t k d", k=2, d=32))
        self.DMA("sp", self.ropeP[:, :, :, :], self.c_ropeP[:, 0:NTILE, :, :], self.ropeP, w=[self.ropeP])
        wkv = self.w_kv.rearrange("(c p) e -> p c e", p=128)
        hnT = self.MX[0]
        KTt = self.MX[1]
        Kf = self.xr
        Vf = Vw(self.F[6], self.F[6].h[:, :, :].rearrange("p a b -> p (a b)"))
        tmpr = Vw(self.F[0], self.F[0].h[:, :, :].rearrange("p a b -> p (a b)"))
        Vb = Vw(self.szT, self.szT.h[:, :, :].rearrange("p a b -> p (a b)"))
        bf = lambda ap: ap.bitcast(BF16)
        self.Wv2 = []
        for c in range(NCH):
            Fk = self.F[2 + c // 2]
            self.Wv2.append(Al(f"Wv2_{c}", bf(Fk.h[:, :, :].rearrange("p a b -> p (a b)"))[:, (c % 2) * 1024:(c % 2 + 1) * 1024]))
        for g in range(3):
            Wk = self.W[g]
            for c in range(NCH):
                self.DMA("pool", Wk[:, c, :], wkv[:, c, g * D:(g + 1) * D], Wk, w=[Wk])
            for c in range(NCH):
                src = wkv[:, c, 3 * D + g * D:3 * D + (g + 1) * D]
                if g < 2:
                    self.DMA("pool", self.W[3 + g][:, c, :], src, self.W[3 + g], w=[self.W[3 + g]])
                else:
                    self.DMA("pool", self.Wv2[c][:, :], src, self.Wv2[c], w=[self.Wv2[c]])
        for ti in range(NTILE + 1):
            nt = 128 if ti < NTILE else NS
            self.kv_norm(ti, nt, hnT)
            for g in range(3):
                self.kv_tile(g, ti, nt, self.W[g], None, hnT, KTt, Kf, Vf, tmpr, Vb)

    def kv_norm(self, ti, nt, hnT):
        self.norm_T(ti, nt, self.BV[:, 0, :], lambda c: (hnT[:, c, :nt], hnT), self.x_dram(ti))

    def kv_tile(self, g, ti, nt, Wk, Wv, hnT, KTt, Kf, Vf, tmpr, Vb):
        A = self.A
        if True:
            if True:
                if ti < NTILE:
                    cosb = bc(self.ropeP[:nt, ti, 0, :].unsqueeze(1), [nt, 8, 32])
                    sinb = bc(self.ropeP[:nt, ti, 1, :].unsqueeze(1), [nt, 8, 32])
                    rpT = self.ropeP
                else:
                    cosb = bc(self.ropeS[:nt, 0, :].unsqueeze(1), [nt, 8, 32])
                    sinb = bc(self.ropeS[:nt, 1, :].unsqueeze(1), [nt, 8, 32])
                    rpT = self.ropeS
                for half in range(2):
                    ps = self.nextG()
                    for c in range(NCH):
                        A("pe", lambda e, ps=ps, c=c, half=half: e.matmul(out=ps[:nt, :], lhsT=hnT[:, c, :nt],
                                                                          rhs=Wk[:, c, half * 512:(half + 1) * 512],
                                                                          start=(c == 0), stop=(c == NCH - 1)), r=[hnT, Wk], w=[ps])
                    pv = ps[:nt, :].rearrange("p (h t d) -> p h t d", t=2, d=32)
                    ov = Kf[:nt, half * 512:(half + 1) * 512].rearrange("p (h t d) -> p h t d", t=2, d=32)
                    tv = tmpr[:nt, 0:512].rearrange("p (h t d) -> p h t d", t=2, d=32)
                    A("dve", lambda e, pv=pv, ov=ov: e.tensor_tensor(out=ov[:, :, 0, :], in0=pv[:, :, 0, :], in1=cosb, op=ALU.mult),
                      r=[ps, rpT], w=[Kf])
                    A("dve", lambda e, pv=pv, tv=tv: e.tensor_tensor(out=tv[:, :, 0, :], in0=pv[:, :, 1, :], in1=sinb, op=ALU.mult),
                      r=[ps, rpT], w=[tmpr])
                    A("dve", lambda e, pv=pv, ov=ov: e.tensor_tensor(out=ov[:, :, 1, :], in0=pv[:, :, 1, :], in1=cosb, op=ALU.mult),
                      r=[ps, rpT], w=[Kf])
                    A("dve", lambda e, pv=pv, tv=tv: e.tensor_tensor(out=tv[:, :, 1, :], in0=pv[:, :, 0, :], in1=sinb, op=ALU.mult),
                      r=[ps, rpT], w=[tmpr])
                    A("pool", lambda e, ov=ov, tv=tv: e.tensor_tensor(out=ov[:, :, 0, :], in0=ov[:, :, 0, :], in1=tv[:, :, 0, :],
                                                                      op=ALU.subtract), r=[Kf, tmpr], w=[Kf])
                    A("pool", lambda e, ov=ov, tv=tv: e.tensor_tensor(out=ov[:, :, 1, :], in0=ov[:, :, 1, :], in1=tv[:, :, 1, :],
                                                                      op=ALU.add), r=[Kf, tmpr], w=[Kf])
                for half in range(2):
                    ps = self.nextG()
                    for c in range(NCH):
                        if g < 2:
                            wv_ap, wv_t = self.W[3 + g][:, c, half * 512:(half + 1) * 512], self.W[3 + g]
                        else:
                            wv_ap, wv_t = self.Wv2[c][:, half * 512:(half + 1) * 512], self.Wv2[c]
                        A("pe", lambda e, ps=ps, c=c, wv_ap=wv_ap: e.matmul(out=ps[:nt, :], lhsT=hnT[:, c, :nt], rhs=wv_ap,
                                                                             start=(c == 0), stop=(c == NCH - 1)), r=[hnT, wv_t], w=[ps])
                    A("act", lambda e, ps=ps, half=half: e.activation(out=Vf[:nt, half * 512:(half + 1) * 512], in_=ps[:nt, :],
                                                                      func=AF.Copy), r=[ps], w=[Vf])
                if ti == NTILE:
                    dk, dv = self.o_kvs[g][:, 0, :], self.o_kvs[g][:, 1, :]
                else:
                    first = NTILE - (1, 4, 16)[g]
                    if ti >= first:
                        r0 = (ti - first) * 128
                        dk, dv = self.o_kvp[g][r0:r0 + 128, 0, :], self.o_kvp[g][r0:r0 + 128, 1, :]
                    else:
                        dk = dv = None
                if dk is not None:
                    self.DMA("sp", dk, Kf[:nt, :], Kf, r=[Kf])
                    self.DMA("sp", dv, Vf[:nt, :], Vf, r=[Vf])
                t0 = ti * 128
                A("pool", lambda e: e.tensor_copy(out=Vb[:nt, :], in_=Vf[:nt, :]), r=[Vf], w=[Vb])
                self.DMA("sp", self.d_v[g, t0:t0 + nt, :], Vb[:nt, :], Vb, r=[Vb], w=[self.vbuf[g]])
                A("act", lambda e: e.activation(out=self.junk[:nt, :], in_=Kf[:nt, :], func=AF.Copy), r=[Kf], w=[self.junk])
                ps = self.nextG()
                psb = ps.h.bitcast(BF16)
                for c in range(NCH):
                    A("pe", lambda e, psb=psb, c=c: e.transpose(out=psb[:, c * 128:c * 128 + nt], in_=self.junk[:nt, c * 128:(c + 1) * 128],
                                                                identity=self.identb[:nt, :nt]), r=[self.junk, self.identb], w=[ps])
                A("act", lambda e, psb=psb: e.activation(out=KTt[:, :, :nt], in_=psb[:, :].rearrange("p (a b) -> p a b", b=128)[:, :, :nt],
                                                         func=AF.Copy), r=[ps], w=[KTt])
                self.DMA("sp", self.d_kT[g, :, :, t0:t0 + nt].rearrange("hp p t -> p hp t"), KTt[:, :, :nt], KTt, r=[KTt],
                         w=[self.kTbuf[g]])
                if ti == NTILE:
                    A("pool", lambda e, g=g: e.tensor_copy(out=self.KTs[:, g, :, :], in_=KTt[:, :, :nt]), r=[KTt], w=[self.KTs])

    def setup_b(self):
        self.s.fence()
        bf = lambda ap: ap.bitcast(BF16)
        fl3 = lambda t: t.h[:, :, :].rearrange("p a b -> p (a b)")
        fl4 = lambda t: t.h[:, :, :, :].rearrange("p a b c -> p (a b c)")
        Wf = [fl3(t) for t in self.W]
        self.XNp = [Al(f"XNp{c}", Wf[c // 4][:, (c % 4) * 2048:(c % 4 + 1) * 2048]) for c in range(8)]
        self.OGp = [Al(f"OGp{c}", Wf[2 + c // 4][:, (c % 4) * 2048:(c % 4 + 1) * 2048]) for c in range(8)]
        self.Wout = Al("Wout", self.W[4].h[:, :, :])
        F = self.F
        self.QT = [Al(f"QT{g}", bf(fl3(F[g]))) for g in range(3)]
        self.KTp = [Al(f"KTp{g}", bf(fl3(F[3 + g]))) for g in range(3)]
        self.Vblk = [Al(f"Vblk{g}", bf(fl3(t)).rearrange("p (m e) -> p m e", e=128)) for g, t in enumerate((F[6], F[7], self.rT))]
        self.numT = Al("numT", fl4(self.Asb).bitcast(F32))
        self.denT = [Al(f"denT{i}", fl4(self.Nsb[i]).bitcast(F32)) for i in range(2)]
        self.Wqs = [Al(f"Wq{j}", t.h[:, :, :]) for j, t in enumerate((self.MX[0], self.MX[1], self.szT, self.vTb))]
        self.Wqr = Al("Wqr", self.BhT.h[:, :, :])
        self.zTp = Al("zTp", bf(fl3(self.kT)))
        self.cosT = Al("cosT", bf(fl3(self.vT)))
        self.sinT = Al("sinT", fl4(self.AR))
        self.tmpq = [Al("tmpq0", fl3(self.KhT).bitcast(F32)), Al("tmpq1", fl3(self.OT).bitcast(F32))]
        xb = self.Xb.h[:, :, :].rearrange("p a b -> p (a b)")
        self.Eb = [Al(f"Eb{i}", xb[:, i * 256:(i + 1) * 256].rearrange("p (a b) -> p a b", b=128)) for i in range(2)]
        self.Eb.append(Al("Eb2", self.WL.h[:, :, :].rearrange("p a b -> p (a b)").bitcast(BF16).rearrange("p (a b) -> p a b", b=128)))
        self.Eb.append(Al("Eb3", self.shiftT.h[:, :, :].rearrange("p a b -> p (a b)").bitcast(BF16).rearrange("p (a b) -> p a b", b=128)))
        wla = self.Wla.h[:, :, :].rearrange("p a b -> p (a b)")
        self.Eb.append(Al("Eb4", wla[:, 0:256].rearrange("p (a b) -> p a b", b=128)))
        self.Eb.append(Al("Eb5", wla[:, 256:512].rearrange("p (a b) -> p a b", b=128)))
        self.blk_par = 0
        self.flnb = Al("flnb", fl4(self.BK).bitcast(F32))
        self.DMA("pool", self.cosT[:, :], self.c_ropeT[:, 0, 0:SEQ], self.cosT, w=[self.cosT])
        self.DMA("pool", self.sinT[:, :], self.c_ropeT[:, 1, 0:SEQ], self.sinT, w=[self.sinT])
        self.DMA("pool", self.ropeTs[:], self.c_ropeT[:, :, SEQ:SEQ + NS], self.ropeTs, w=[self.ropeTs])
        self.DMA("sp", self.flnb[:, :], self.final_ln.partition_broadcast(128), self.flnb, w=[self.flnb])

    def layer_b(self, lb):
        if lb == 0:
            self.setup_b()
        for ti in range(NTILE + 1):
            self.b_norm_tile(lb, ti, 128 if ti < NTILE else NS)
        wo = self.b_w_out[lb].rearrange("(c p) e -> p c e", p=128)
        for c in range(NCH):
            self.DMA("pool", self.Wout[:, c, :], wo[:, c, :], self.Wout, w=[self.Wout])
        for hp in range(NCH):
            self.b_hp(lb, hp)
        self.b_samples(lb)
        for ti in range(NTILE + 1):
            self.b_out_tile(lb, ti, 128 if ti < NTILE else NS)

    def b_norm_tile(self, lb, ti, nt):
        if ti < NTILE:
            dst = lambda c: (self.XNp[c][:, ti * 128:(ti + 1) * 128], self.XNp[c])
        else:
            dst = lambda c: (self.XNs[:, c, :], self.XNs)
        self.norm_T(ti, nt, self.BV[:, 1 + lb, :], dst, self.x_dram(ti))

    def b_proj(self, w, hp, ch, evac):
        A = self.A
        ps = self.nextG()
        n = 512 if ch < 4 else NS
        for c in range(NCH):
            rhs = self.XNp[c][:, ch * 512:(ch + 1) * 512] if ch < 4 else self.XNs[:, c, :]
            rt = self.XNp[c] if ch < 4 else self.XNs
            A("pe", lambda e, ps=ps, c=c, rhs=rhs: e.matmul(out=ps[:, 0:n], lhsT=w[:, c, :], rhs=rhs, start=(c == 0), stop=(c == NCH - 1)),
              r=[w, rt], w=[ps])
        evac(ps, n)
        return ps

    def b_hp(self, lb, hp):
        A = self.A
        src = self.b_w_in[lb].rearrange("(c p) e -> p c e", p=128)
        for j in range(4):
            col0 = j * D + hp * 128
            self.DMA("pool", self.Wqs[j][:, :, :], src[:, :, col0:col0 + 128], self.Wqs[j], w=[self.Wqs[j]])
        tq0, tq1 = self.tmpq
        for g in range(3):
            w, wr = self.Wqs[g], self.Wqr
            for ch in range(5):
                self.b_q_chunk(g, hp, ch, w, wr)
        for ch in range(5):
            self.b_z_chunk(hp, ch)
        for g in range(3):
            self.DMA("sp", self.KTp[g][:, :], self.d_kT[g, hp, :, 0:SEQ], self.KTp[g], r=[self.kTbuf[g]], w=[self.KTp[g]])
            dv = self.d_v[g, 0:SEQ, hp * 128:(hp + 1) * 128]
            if g == 0:
                self.DMA("sp", self.Vblk[0][:, :, :], dv.rearrange("(m i) e -> i m e", i=128), self.Vblk[0], r=[self.vbuf[0]], w=[self.Vblk[0]])
            elif g == 1:
                dvv = dv.rearrange("(m i r) e -> i r m e", i=128, r=4)
                for r_ in range(4):
                    self.DMA("sp", self.Vblk[1][:, 4 * r_:4 * r_ + 4, :], dvv[:, r_, :, :], self.Vblk[1], r=[self.vbuf[1]], w=[self.Vblk[1]])
            else:
                self.DMA("sp", self.Vblk[2][:, :, :], dv.rearrange("(i r) e -> i r e", r=16), self.Vblk[2], r=[self.vbuf[2]], w=[self.Vblk[2]])
        for g in range(3):
            if g == 0:
                blocks = [((m * 128, (m + 1) * 128, 1), m, (m - 1) if m > 0 else None) for m in range(16)]
            elif g == 1:
                blocks = [((r_ + 512 * m, 512 * (m + 1), 4), 4 * r_ + m, (4 * r_ + m - 1) if m > 0 else None)
                          for r_ in range(4) for m in range(4)]
            else:
                blocks = [((r_, SEQ, 16), r_, None) for r_ in range(16)]
            for bi, (qs, own, prev) in enumerate(blocks):
                pq = None
                if prev is not None:
                    pq = blocks[bi - 1][0]
                self.b_block(g, hp, qs, own, prev, pq)
        numT, zTp = self.numT, self.zTp
        for i in range(2):
            A("act", lambda e, i=i: e.activation(out=self.denT[i][:, :], in_=self.denT[i][:, :], func=AF.Ln), r=[self.denT[i]], w=[self.denT[i]])
            A("act", lambda e, i=i: e.activation(out=self.denT[i][:, :], in_=self.denT[i][:, :], func=AF.Exp, scale=-1.0),
              r=[self.denT[i]], w=[self.denT[i]])
            A("dve", lambda e, i=i: e.tensor_tensor(out=numT[:, i * 1024:(i + 1) * 1024], in0=numT[:, i * 1024:(i + 1) * 1024],
                                                    in1=self.denT[i][:, :], op=ALU.mult), r=[numT, self.denT[i]], w=[numT])
        A("dve", lambda e: e.tensor_tensor(out=self.OGp[hp][:, :], in0=numT[:, :], in1=zTp[:, :], op=ALU.mult),
          r=[numT, zTp], w=[self.OGp[hp]])

    def b_q_chunk(self, g, hp, ch, w, wr):
        A = self.A
        tq0, tq1 = self.tmpq
        n = 512 if ch < 4 else NS
        if ch < 4:
            cs_, sn_ = self.cosT[:, ch * 512:(ch + 1) * 512], self.sinT[:, ch * 512:(ch + 1) * 512]
            ct, st_ = self.cosT, self.sinT
            dst, dT = self.QT[g][:, ch * 512:(ch + 1) * 512], self.QT[g]
        else:
            cs_, sn_ = self.ropeTs[:, 0, :], self.ropeTs[:, 1, :]
            ct = st_ = self.ropeTs
            dst, dT = self.QTs[:, hp, g, :], self.QTs

        Qb = self.Wqr
        Qbf = Qb[:, :, :].rearrange("p a b -> p (a b)")

        def ev0(ps, n):
            A("act", lambda e: e.activation(out=Qbf[:, 0:n], in_=ps[:, 0:n], func=AF.Copy), r=[ps], w=[Qb])
            A("dve", lambda e: e.tensor_tensor(out=tq0[:, 0:n], in0=ps[:, 0:n], in1=cs_, op=ALU.mult), r=[ps, ct], w=[tq0])
        self.b_proj(w, hp, ch, ev0)
        ps2 = self.nextG()
        A("pe", lambda e: e.matmul(out=ps2[:, 0:n], lhsT=self.prot[:, :], rhs=Qbf[:, 0:n], start=True, stop=True),
          r=[self.prot, Qb], w=[ps2])
        A("dve", lambda e: e.tensor_tensor(out=tq1[:, 0:n], in0=ps2[:, 0:n], in1=sn_, op=ALU.mult), r=[ps2, st_], w=[tq1])
        A("pool", lambda e: e.tensor_tensor(out=dst, in0=tq0[:, 0:n], in1=tq1[:, 0:n], op=ALU.add), r=[tq0, tq1], w=[dT])

    def b_z_chunk(self, hp, ch):
        A = self.A
        tq0 = self.tmpq[0]
        if ch < 4:
            dst, dT = self.zTp[:, ch * 512:(ch + 1) * 512], self.zTp
        else:
            dst, dT = self.zTs[:, hp, :], self.zTs

        def ev(ps, n):
            A("act", lambda e: e.activation(out=tq0[:, 0:n], in_=ps[:, 0:n], func=AF.Exp, scale=-1.0), r=[ps], w=[tq0])
            A("act", lambda e: e.activation(out=tq0[:, 0:n], in_=tq0[:, 0:n], func=AF.Ln, bias=1.0), r=[tq0], w=[tq0])
            A("act", lambda e: e.activation(out=tq0[:, 0:n], in_=tq0[:, 0:n], func=AF.Exp, scale=-1.0), r=[tq0], w=[tq0])
            A("dve", lambda e: e.tensor_tensor(out=dst, in0=ps[:, 0:n], in1=tq0[:, 0:n], op=ALU.mult), r=[ps, tq0], w=[dT])
        self.b_proj(self.Wqs[3], hp, ch, ev)

    def b_block(self, g, hp, qs, own, prev, pqs):
        A = self.A
        MM = self.MM
        KT, QT, Vb = self.KTp[g], self.QT[g], self.Vblk[g]
        qsl = slice(qs[0], qs[1], qs[2])
        kb0 = 0 if prev is not None else 1
        self.blk_par = (self.blk_par + 1) % 3
        par = self.blk_par
        sbanks = ((self.psA[0], self.psA[1]), (self.psX, self.psY), (self.psG[0], self.psG[1]))[par]
        Ebs = self.Eb[2 * par:2 * par + 2]
        for hh in range(2):
            pb = 64 * hh
            ps = sbanks[hh]
            pv = ps[:, 0:256].rearrange("p (a b) -> p a b", b=128)
            if prev is not None:
                psl = slice(pqs[0], pqs[1], pqs[2])
                A("pe", lambda e, pv=pv, pb=pb, psl=psl: e.matmul(out=pv[:, 0, :], lhsT=KT[pb:pb + 64, psl], rhs=QT[pb:pb + 64, qsl],
                                                                  start=True, stop=True), r=[KT, QT], w=[ps])
            A("pe", lambda e, pv=pv, pb=pb: e.matmul(out=pv[:, 1, :], lhsT=KT[pb:pb + 64, qsl], rhs=QT[pb:pb + 64, qsl],
                                                     start=True, stop=True), r=[KT, QT], w=[ps])
            Eb = Ebs[hh]
            A("act", lambda e, pv=pv, Eb=Eb: e.activation(out=Eb[:, kb0:2, :], in_=pv[:, kb0:2, :], func=AF.Exp, scale=0.125),
              r=[ps], w=[Eb])
            A("dve" if hh == 0 else "pool", lambda e, Eb=Eb: e.tensor_tensor(out=Eb[:, kb0:2, :], in0=Eb[:, kb0:2, :],
                                                                             in1=self.maskP[:, kb0:2, :], op=ALU.mult),
              r=[Eb, self.maskP], w=[Eb])
        psO = self.nextN()
        self.new_round(psO)
        po = psO[:, 0:256].rearrange("p (a b) -> p a b", b=128)
        for hh in range(2):
            pb = 64 * hh
            Eb = Ebs[hh]
            for kb in range(kb0, 2):
                bidx = prev if kb == 0 else own
                MM(psO, po[pb:pb + 64, 0, :], Vb[:, bidx, pb:pb + 64], Eb[:, kb, :], 0, 128, pb, 64, r=[Vb, Eb])
                MM(psO, po[pb:pb + 64, 1, :], self.ones_b[:, 0:64], Eb[:, kb, :], 0, 128, pb, 64, r=[self.ones_b, Eb])
        numT = self.numT
        if g == 0:
            A("act", lambda e: e.activation(out=numT[:, qsl], in_=po[:, 0, :], func=AF.Copy), r=[psO], w=[numT])
        else:
            A("dve", lambda e: e.tensor_tensor(out=numT[:, qsl], in0=po[:, 0, :], in1=numT[:, qsl], op=ALU.add), r=[psO, numT], w=[numT])
        pieces = []
        q0, q1, st = qs
        n_lo = len(range(q0, min(q1, 1024), st)) if q0 < 1024 else 0
        if n_lo > 0:
            pieces.append((0, slice(q0, min(q1, 1024), st), slice(0, n_lo)))
        if n_lo < 128:
            first_hi = q0 + n_lo * st
            pieces.append((1, slice(first_hi - 1024, q1 - 1024, st), slice(n_lo, 128)))
        for (hf, dsl, csl) in pieces:
            dT = self.denT[hf]
            if g == 0:
                A("act", lambda e, dT=dT, dsl=dsl, csl=csl: e.activation(out=dT[:, dsl], in_=po[:, 1, csl], func=AF.Copy), r=[psO], w=[dT])
            else:
                A("dve", lambda e, dT=dT, dsl=dsl, csl=csl: e.tensor_tensor(out=dT[:, dsl], in0=po[:, 1, csl], in1=dT[:, dsl], op=ALU.add),
                  r=[psO, dT], w=[dT])

    def setup_samples(self):
        bf = lambda ap: ap.bitcast(BF16)
        fl3 = lambda t: t.h[:, :, :].rearrange("p a b -> p (a b)")
        F = self.F
        f0, f1, f2, f3, f4, f5, f6 = [bf(fl3(F[k])) for k in range(7)]
        self.Kc = [Al("Kc0", f0[:, 0:1024]), Al("Kc1", f0[:, 1024:2048])]
        self.Vc = [Al("Vc0", f1[:, 0:1024]), Al("Vc1", f1[:, 1024:2048])]
        self.KTc = Al("KTc", f2[:, 0:1024].rearrange("p (a b) -> p a b", b=128))
        self.Vs = [Al("Vs0", f3[0:64, 0:1024]), Al("Vs1", f3[0:64, 1024:2048]), Al("Vs2", f4[0:64, 0:1024])]
        self.Vn = [Al("Vn0", f5[0:4, 0:1024]), Al("Vn1", f5[0:4, 1024:2048]), Al("Vn2", f4[0:4, 1024:2048])]
        self.Ec = Al("Ec", f6[:, 0:64].rearrange("p (a b c) -> p a b c", b=2, c=4))
        self.En = Al("En", f6[0:4, 64:256].rearrange("p (g a b c) -> p g a b c", a=8, b=2, c=4))
        self.Qbd = Al("Qbd", f6[:, 256:448].rearrange("p (a g b c) -> p a g b c", g=3, b=2, c=4))
        self.maskS = Al("maskS", f6[:, 448:496].rearrange("p (a c) -> p a c", c=4))
        f7 = fl3(F[7])
        self.numS = Al("numS", f7[:, 0:512].rearrange("p (a b) -> p a b", b=NS))
        self.denS = Al("denS", f7[:, 512:1024].rearrange("p (a b) -> p a b", b=NS))

    def b_samples(self, lb):
        A = self.A
        self.s.fence()
        if lb == 0:
            self.setup_samples()
        A("pool", lambda e: e.memset(self.Qbd[:, :, :, :, :], 0.0), w=[self.Qbd])
        self.DMA("pool", self.maskS[:, :, :], self.c_maskS[:, :, :], self.maskS, w=[self.maskS])
        for g in range(3):
            self.DMA("sp", self.Vs[g][:, :], self.d_v[g, SEQ:SEQ + NS, :], self.Vs[g], r=[self.vbuf[g]], w=[self.Vs[g]])
        for b in range(SB):
            self.b_sample_batch(lb, b)
        numS, denS = self.numS, self.denS
        A("dve", lambda e: e.reciprocal(out=denS[:, :, :], in_=denS[:, :, :]), r=[denS], w=[denS])
        A("dve", lambda e: e.tensor_tensor(out=numS[:, :, :], in0=numS[:, :, :], in1=denS[:, :, :], op=ALU.mult), r=[numS, denS], w=[numS])
        A("dve", lambda e: e.tensor_tensor(out=self.OGs[:], in0=numS[:, :, :], in1=self.zTs[:], op=ALU.mult), r=[numS, self.zTs], w=[self.OGs])
        self.s.fence()

    def b_sample_batch(self, lb, b):
        A = self.A
        MM = self.MM
        Qbd, QTs, Ec, En = self.Qbd, self.QTs, self.Ec, self.En
        bs = slice(ST_ * b, ST_ * b + ST_)
        for hh in range(2):
            pb = 64 * hh
            A("dve", lambda e, pb=pb, hh=hh: e.tensor_copy(out=Qbd[pb:pb + 64, :, :, hh, :], in_=QTs[pb:pb + 64, :, :, bs]),
              r=[QTs], w=[Qbd])
        for g in range(3):
            for half in range(2):
                ps = self.nextG()
                A("pe", lambda e, ps=ps, g=g, half=half: e.matmul(out=ps[0:ST_, :], lhsT=self.identb[0:NS, bs],
                                                                  rhs=self.Vs[g][0:NS, half * 512:(half + 1) * 512], start=True, stop=True),
                  r=[self.identb, self.Vs[g]], w=[ps])
                A("act", lambda e, ps=ps, g=g, half=half: e.activation(out=self.Vn[g][0:ST_, half * 512:(half + 1) * 512], in_=ps[0:ST_, :],
                                                                       func=AF.Copy), r=[ps], w=[self.Vn[g]])
        psX = self.psX
        self.new_round(psX)
        po = psX[:, 0:64].rearrange("p (s a c) -> p s a c", s=2, c=ST_)
        tiles = [(0, 0)] + [(1, r_) for r_ in range(4)] + [(2, r_) for r_ in range(4)]
        for tix, (g, r_) in enumerate(tiles):
            self.b_sample_tile(b, tix, g, r_, po)
        ps2 = self.psA[1]
        pv2 = ps2[:, 0:192].rearrange("p (g a c) -> p g a c", a=8, c=8)
        for g in range(3):
            for hp in range(NCH):
                A("pe", lambda e, g=g, hp=hp: e.matmul(out=pv2[0:ST_, g, hp, :], lhsT=self.KTs[:, g, hp, bs], rhs=Qbd[:, hp, g, :, :],
                                                       start=True, stop=True), r=[self.KTs, Qbd], w=[ps2])
        A("act", lambda e: e.activation(out=En[:, :, :, :, :].rearrange("p g a b c -> p g a (b c)"), in_=pv2[0:ST_, :, :, :], func=AF.Exp,
                                        scale=0.125), r=[ps2], w=[En])
        A("dve", lambda e: e.tensor_tensor(out=En[:, :, :, :, :].rearrange("p g a b c -> p g (a b) c"),
                                           in0=En[:, :, :, :, :].rearrange("p g a b c -> p g (a b) c"),
                                           in1=bc(self.maskS[0:ST_, 9:12, :].unsqueeze(2), [ST_, 3, 16, ST_]), op=ALU.mult),
          r=[En, self.maskS], w=[En])
        for g in range(3):
            for hp in range(NCH):
                for hh in range(2):
                    pb = 64 * hh
                    MM(psX, po[pb:pb + 64, 0, hp, :], self.Vn[g][0:ST_, hp * 128 + pb:hp * 128 + pb + 64], En[:, g, hp, hh, :], 0, ST_, pb, 64,
                       r=[self.Vn[g], En])
                    MM(psX, po[pb:pb + 64, 1, hp, :], self.ones_b[0:ST_, 0:64], En[:, g, hp, hh, :], 0, ST_, pb, 64, r=[self.ones_b, En])
        A("act", lambda e: e.activation(out=self.numS[:, :, bs], in_=po[:, 0, :, :], func=AF.Copy), r=[psX], w=[self.numS])
        A("dve", lambda e: e.tensor_copy(out=self.denS[:, :, bs], in_=po[:, 1, :, :]), r=[psX], w=[self.denS])

    def b_sample_tile(self, b, tix, g, r_, po):
        A = self.A
        MM = self.MM
        i = tix % 2
        Kc, Vc, KTc, Ec, Qbd = self.Kc[i], self.Vc[i], self.KTc, self.Ec, self.Qbd
        psX = self.psX
        if g == 0:
            src = self.ckv[0][b]
        elif g == 1:
            src = self.ckv[1][b].rearrange("(m r) k e -> m r k e", r=4)[:, r_]
        else:
            src = self.ckv[2][b].rearrange("(m r) k e -> m r k e", r=16)[:, r_]
        self.DMA("pool", Kc[:, :], src[:, 0, :], Kc, w=[Kc])
        self.DMA("pool", Vc[:, :], src[:, 1, :], Vc, w=[Vc])
        ps = self.nextG()
        psb = ps.h.bitcast(BF16)
        for c in range(NCH):
            A("pe", lambda e, psb=psb, c=c: e.transpose(out=psb[:, c * 128:(c + 1) * 128], in_=Kc[:, c * 128:(c + 1) * 128],
                                                        identity=self.identb[:, :]), r=[Kc, self.identb], w=[ps])
        A("act", lambda e, psb=psb: e.activation(out=KTc[:, :, :], in_=psb[:, :].rearrange("p (a b) -> p a b", b=128), func=AF.Copy),
          r=[ps], w=[KTc])
        ps1 = self.psA[0]
        pv = ps1[:, 0:64].rearrange("p (a c) -> p a c", c=8)
        for hp in range(NCH):
            A("pe", lambda e, hp=hp: e.matmul(out=pv[:, hp, :], lhsT=KTc[:, hp, :], rhs=Qbd[:, hp, g, :, :], start=True, stop=True),
              r=[KTc, Qbd], w=[ps1])
        A("act", lambda e: e.activation(out=Ec[:, :, :, :].rearrange("p a b c -> p a (b c)"), in_=pv[:, :, :], func=AF.Exp, scale=0.125),
          r=[ps1], w=[Ec])
        A("dve", lambda e: e.tensor_tensor(out=Ec[:, :, :, :].rearrange("p a b c -> p (a b) c"),
                                           in0=Ec[:, :, :, :].rearrange("p a b c -> p (a b) c"),
                                           in1=bc(self.maskS[:, tix, :].unsqueeze(1), [128, 16, ST_]), op=ALU.mult),
          r=[Ec, self.maskS], w=[Ec])
        for hp in range(NCH):
            for hh in range(2):
                pb = 64 * hh
                MM(psX, po[pb:pb + 64, 0, hp, :], Vc[:, hp * 128 + pb:hp * 128 + pb + 64], Ec[:, hp, hh, :], 0, 128, pb, 64, r=[Vc, Ec])
                MM(psX, po[pb:pb + 64, 1, hp, :], self.ones_b[:, 0:64], Ec[:, hp, hh, :], 0, 128, pb, 64, r=[self.ones_b, Ec])

    def b_out_tile(self, lb, ti, nt):
        A = self.A
        Xt = self.Xt
        self.DMA("sp", Xt[:nt, :], self.x_dram(ti), Xt, r=[self.xbuf[ti]], w=[Xt])
        Wo = self.Wout
        for half in range(2):
            ps = self.nextG()
            for c in range(NCH):
                if ti < NTILE:
                    lhsT, lt = self.OGp[c][:, ti * 128:(ti + 1) * 128], self.OGp[c]
                else:
                    lhsT, lt = self.OGs[:, c, :], self.OGs
                A("pe", lambda e, ps=ps, c=c, half=half, lhsT=lhsT: e.matmul(out=ps[:nt, :], lhsT=lhsT, rhs=Wo[:, c, half * 512:(half + 1) * 512],
                                                                             start=(c == 0), stop=(c == NCH - 1)), r=[lt, Wo], w=[ps])
            A("dve", lambda e, ps=ps, half=half: e.tensor_tensor(out=Xt[:nt, half * 512:(half + 1) * 512], in0=ps[:nt, :],
                                                                 in1=Xt[:nt, half * 512:(half + 1) * 512], op=ALU.add),
              r=[ps, Xt], w=[Xt])
        if lb == 1:
            st1 = self.st1
            A("pool", lambda e: e.memset(st1[:], 0.0), w=[st1])
            A("act", lambda e: e.activation(out=self.junk[:nt, :], in_=Xt[:nt, :], func=AF.Square, accum_out=st1[:nt, 0:1]),
              r=[Xt, st1], w=[self.junk, st1])
            A("act", lambda e: e.activation(out=st1[:nt, 1:2], in_=st1[:nt, 0:1], func=AF.Ln, scale=1.0 / D, bias=RMS_EPS), r=[st1], w=[st1])
            A("act", lambda e: e.activation(out=st1[:nt, 2:3], in_=st1[:nt, 1:2], func=AF.Exp, scale=-0.5), r=[st1], w=[st1])
            A("act", lambda e: e.activation(out=Xt[:nt, :], in_=Xt[:nt, :], func=AF.Identity, scale=st1[:nt, 2:3]), r=[Xt, st1], w=[Xt])
            A("dve", lambda e: e.tensor_tensor(out=Xt[:nt, :], in0=Xt[:nt, :], in1=self.flnb[:nt, :], op=ALU.mult), r=[Xt, self.flnb], w=[Xt])
        self.DMA("sp", self.x_dram(ti), Xt[:nt, :], Xt, r=[Xt], w=[self.xbuf[ti]])

    def _dump_x(self):
        pass


def _consts():
    c = {}
    c["c_ident"] = np.eye(128, dtype=np.float32)
    s = np.arange(128)[:, None]
    t = np.arange(128)[None, :]
    lt = (s < t).astype(np.float32)
    le = (s <= t).astype(np.float32)
    c["c_maskA"] = np.ascontiguousarray(np.stack([lt, le, lt, le], axis=1))
    c["c_maskAT"] = np.ascontiguousarray((t < s).astype(np.float32))
    blk = np.zeros((128, 128), np.float32)
    blk[:64, :64] = 1.0
    blk[64:, 64:] = 1.0
    c["c_blk"] = blk
    sc = np.ones((128, 2, 128), np.float32)
    sc[:, 0, 0] = 0.0
    sc[:, 1, 0::4] = 0.0
    c["c_scan"] = sc
    half = 32
    inv = (10000.0 ** (-np.arange(half, dtype=np.float32) / np.float32(half))).astype(np.float32)
    pos = np.concatenate([np.arange(SEQ, dtype=np.float32), np.tile(np.float32(2048) + np.arange(ST_, dtype=np.float32), SB)])
    ang = (pos[:, None] * inv[None, :]).astype(np.float32)
    cs = np.stack([np.cos(ang), np.sin(ang)], 1).astype(np.float32)
    rp = np.zeros((128, NTILE + 1, 2, 32), np.float32)
    rp[:, :NTILE] = cs[:SEQ].reshape(NTILE, 128, 2, 32).transpose(1, 0, 2, 3)
    rp[:NS, NTILE] = cs[SEQ:]
    c["c_ropeP"] = rp
    rt = np.zeros((128, 2, SEQ + NS), np.float32)
    pidx = np.arange(128) % 32
    rt[:, 0, :] = np.cos(ang).T[pidx]
    rt[:, 1, :] = np.sin(ang).T[pidx]
    c["c_ropeT"] = rt
    k = np.arange(128)[:, None]
    q = np.arange(128)[None, :]
    ms = np.zeros((128, 12, ST_), np.float32)
    kk_ = np.arange(128)[:, None]
    tt_ = np.arange(ST_)[None, :]
    ms[:, 0, :] = (kk_ > tt_)
    for r_ in range(4):
        ms[:, 1 + r_, :] = (tt_ == r_) & (kk_ >= 1)
        ms[:, 5 + r_, :] = (tt_ == r_) & (kk_ >= 1)
    ms[:ST_, 9, :] = (kk_[:ST_] <= tt_)
    ms[:ST_, 10, :] = (kk_[:ST_] == tt_)
    ms[:ST_, 11, :] = (kk_[:ST_] == tt_)
    c["c_maskS"] = ms
    pr_ = np.zeros((128, 128), np.float32)
    for m_ in range(128):
        if m_ % 64 < 32:
            pr_[m_ + 32, m_] = -1.0
        else:
            pr_[m_ - 32, m_] = 1.0
    c["c_prot"] = pr_
    c["c_maskP"] = np.ascontiguousarray(np.stack([(k > q), (k <= q)], 1).astype(np.float32))
    return c


def _fm(v):
    return np.ascontiguousarray(np.asarray(v, np.float32).reshape(NCH, 128).T)


def _prep_inputs(inp, ncores=NCORES):
    f = lambda k: np.ascontiguousarray(np.asarray(inp[k], dtype=np.float32))
    shared = {}
    a_vec = np.zeros((2, 128, NV, NCH), np.float32)
    for l in range(2):
        a_vec[l, :, V_LN] = _fm(inp["a_ln"][l])
        for p in range(6):
            a_vec[l, :, V_MU + p] = _fm(inp["a_mu"][l, p])
        a_vec[l, :, V_W0] = _fm(inp["a_w0"][l])
        a_vec[l, :, V_A0] = _fm(inp["a_a0"][l])
        if l == 1:
            a_vec[l, :, V_V0] = _fm(inp["a_v0"][0])
        a_vec[l, :, V_KK] = _fm(inp["a_k_k"][l])
        a_vec[l, :, V_KA] = _fm(inp["a_k_a"][l])
        a_vec[l, :, V_RK] = _fm(np.asarray(inp["a_r_k"][l]).reshape(-1))
        a_vec[l, :, V_GNW] = _fm(inp["a_gn_w"][l])
        a_vec[l, :, V_GNB] = _fm(inp["a_gn_b"][l])
    shared["a_vec"] = a_vec
    b_vec = np.zeros((128, 4, NCH), np.float32)
    b_vec[:, 0] = _fm(inp["kv_ln"])
    b_vec[:, 1] = _fm(inp["b_ln"][0])
    b_vec[:, 2] = _fm(inp["b_ln"][1])
    shared["b_vec"] = b_vec
    for k in ("w_kv", "b_w_in", "b_w_out", "final_ln"):
        shared[k] = f(k)
    for k in ("a_w_in", "a_w_out", "a_w_lora_a", "a_w_lora_b", "a_a_lora_a", "a_a_lora_b", "a_v_lora_a", "a_v_lora_b"):
        shared[k] = f(k)
    shared.update(_consts())
    xp = f("x_prompt")
    xs = f("x_sample")
    swkv = f("state_wkv")
    ssh = f("state_shift")
    maps = []
    for c in range(ncores):
        m = dict(shared)
        m["xp"] = xp[c]
        m["xs"] = np.ascontiguousarray(xs[c * SB:(c + 1) * SB].reshape(NS, D))
        m["swkv"] = np.ascontiguousarray(swkv[:, c * SB:(c + 1) * SB])
        m["sshift"] = np.ascontiguousarray(ssh[:, c * SB:(c + 1) * SB])
        for g in range(3):
            ck = np.asarray(inp[f"cache_kv_g{g}"])[c * SB:(c + 1) * SB]
            m[f"ckv{g}"] = np.ascontiguousarray(ck.reshape(SB, ck.shape[1], 2, D), dtype=np.float32)
        maps.append(m)
    return maps


_NC_CACHE = {}


def _get_nc(stage):
    if stage not in _NC_CACHE:
        _NC_CACHE[stage] = Builder(stage).build()
    return _NC_CACHE[stage]


def run_cores(inp, stage="full", cores=None):
    maps = _prep_inputs(inp, NCORES if cores is None else len(cores))
    nc = _get_nc(stage)
    res = run_bass_kernel_spmd(nc, maps, core_ids=list(range(len(maps))))
    return res.results


def kernel(**inputs):
    res = run_cores(inputs, "full")
    cat = lambda k, ax=0: np.concatenate([r[k] for r in res], axis=ax)
    y_prompt = np.stack([r["y_prompt"] for r in res], 0)
    y_sample = cat("y_sample").reshape(NCORES * SB, ST_, D)
    wkv_p = np.stack([r["wkv_p"] for r in res], 1)
    wkv_s = cat("wkv_s", 1)
    shift_p = np.stack([r["shift_p"] for r in res], 1)
    shift_s = cat("shift_s", 1)
    outs = [y_prompt, y_sample, wkv_p, wkv_s, shift_p, shift_s]
    for g, w in enumerate((128, 512, 2048)):
        outs.append(np.stack([r[f"kv{g}p"] for r in res], 0).reshape(NCORES, w, 2, 16, 64))
        outs.append(cat(f"kv{g}s").reshape(NCORES * SB, ST_, 2, 16, 64))
    return tuple(np.ascontiguousarray(o, dtype=np.float32) for o in outs)
```

```python
import contextlib
import math
import numpy as np
import concourse.bass as bass
import concourse.mybir as mybir
from concourse.bass_utils import run_bass_kernel_spmd

F32 = mybir.dt.float32
BF16 = mybir.dt.bfloat16
AF = mybir.ActivationFunctionType
ALU = mybir.AluOpType
AX = mybir.AxisListType

NCORES = 8
D = 1024
NCH = 8
SEQ = 2048
NTILE = 16
SB = 16
ST_ = 4
NS = SB * ST_
RMS_EPS = 1e-6
GN_EPS = 64e-5
CDEC = math.exp(-0.5)
SEM_LIMIT = 12000
import os as _os
TRACE = bool(_os.environ.get("K_TRACE"))


class Buf:
    __slots__ = ("name", "w", "r")

    def __init__(self, name):
        self.name = name
        self.w = None
        self.r = []


class Op:
    __slots__ = ("eng", "fn", "deps", "is_dma", "ms", "sem", "val", "line")

    def __init__(self, eng, fn, is_dma):
        self.eng = eng
        self.fn = fn
        self.is_dma = is_dma
        self.deps = []
        self.ms = False
        self.sem = None
        self.val = 0


class Sched:
    ENGS = ("pe", "act", "dve", "pool", "sp")

    def __init__(self, nc):
        self.nc = nc
        self.ops = {e: [] for e in self.ENGS}
        self.dma_cnt = {}
        self.all_dma = []

    def add(self, eng, fn, reads=(), writes=(), dma=False, key=None, extra=()):
        op = Op(eng, fn, dma)
        if TRACE:
            import sys as _sys
            f = _sys._getframe(1)
            ls = []
            while f is not None and len(ls) < 4:
                ls.append(f.f_lineno)
                f = f.f_back
            op.line = ls
        for d in extra:
            op.deps.append(d)
            d.ms = True
        deps = {}
        for b in reads:
            if b.w is not None:
                deps[id(b.w)] = (b.w, True)
        for b in writes:
            if b.w is not None and id(b.w) not in deps:
                deps[id(b.w)] = (b.w, False)
            for r in b.r:
                if id(r) not in deps:
                    deps[id(r)] = (r, False)
        for d, raw in deps.values():
            if d.is_dma:
                op.deps.append(d)
                d.ms = True
            elif d.eng == eng and not dma:
                if raw and eng != "pe":
                    op.deps.append(d)
                    d.ms = True
            else:
                op.deps.append(d)
                d.ms = True
        for b in reads:
            b.r.append(op)
        for b in writes:
            b.w = op
            b.r = []
        if dma:
            c = self.dma_cnt.get(id(key), (key, 0))[1] + 1
            self.dma_cnt[id(key)] = (key, c)
            op.sem = ("dma", id(key))
            op.val = 16 * c
            self.all_dma.append(op)
        self.ops[eng].append(op)
        return op

    def fence(self):
        lasts = []
        for e in self.ENGS:
            for op in reversed(self.ops[e]):
                if not op.is_dma:
                    lasts.append(op)
                    break
        dl = {}
        for op in self.all_dma:
            dl[op.sem] = op
        for e in self.ENGS:
            extra = [o for o in lasts if o.eng != e] + list(dl.values())
            self.add(e, lambda eng: eng.nop(), extra=extra)

    def emit(self):
        nc = self.nc
        nsem_eng = {}
        for e in self.ENGS:
            cnt = 0
            epoch = 0
            for op in self.ops[e]:
                if op.is_dma or not op.ms:
                    continue
                cnt += 1
                if cnt > SEM_LIMIT:
                    epoch += 1
                    cnt = 1
                op.sem = ("eng", e, epoch)
                op.val = cnt
            nsem_eng[e] = epoch + 1
        sems = {}
        with contextlib.ExitStack() as es:
            for e in self.ENGS:
                for k in range(nsem_eng[e]):
                    sems[("eng", e, k)] = es.enter_context(nc.semaphore(f"s_{e}{k}"))
            for kid, (key, c) in self.dma_cnt.items():
                sems[("dma", kid)] = es.enter_context(nc.semaphore(f"d_{key.name}"))
            last_dma = {}
            for op in self.all_dma:
                last_dma[op.sem] = op
            block = es.enter_context(nc.Block())

            def run(ename, eng):
                waited = {}
                nops = len(self.ops[ename])
                for io, op in enumerate(self.ops[ename]):
                    if TRACE and io >= nops - 25:
                        print("TR", ename, io, op.line, "inc", op.sem[1:] if op.sem else None, op.val, "ms", op.ms,
                              "waits", [(d.eng, d.sem[1:], d.val, d.line) for d in op.deps if waited.get(d.sem, 0) < d.val])
                    for d in op.deps:
                        if waited.get(d.sem, 0) >= d.val:
                            continue
                        waited[d.sem] = d.val
                        eng.wait_ge(sems[d.sem], d.val)
                    ins = op.fn(eng)
                    if op.is_dma:
                        ins.then_inc(sems[op.sem], 16)
                    elif op.ms:
                        ins.then_inc(sems[op.sem], 1)
                if ename == "sp":
                    for s, op in last_dma.items():
                        if waited.get(s, 0) < op.val:
                            eng.wait_ge(sems[s], op.val)

            @block.tensor
            def _(eng):
                run("pe", eng)

            @block.scalar
            def _(eng):
                run("act", eng)

            @block.vector
            def _(eng):
                run("dve", eng)

            @block.gpsimd
            def _(eng):
                run("pool", eng)

            @block.sync
            def _(eng):
                run("sp", eng)


class T:
    def __init__(self, h, name):
        self.h = h
        self.b = Buf(name)

    def __getitem__(self, k):
        return self.h[k]


class Vw:
    def __init__(self, t, ap):
        self.b = t.b
        self.ap = ap

    def __getitem__(self, k):
        return self.ap[k]


class Al:
    def __init__(self, name, ap):
        self.b = Buf(name)
        self.ap = ap

    def __getitem__(self, k):
        return self.ap[k]


def bc(ap, shape):
    return ap.broadcast_to(shape)


V_LN, V_MU, V_W0, V_A0, V_V0, V_KK, V_KA, V_RK, V_GNW, V_GNB = 0, 1, 7, 8, 9, 10, 11, 12, 13, 14
V_NW0, V_NA0, V_NV0, V_OMK = 15, 16, 17, 18
NV = 20


class _Stop(Exception):
    pass


class Builder:
    def __init__(self, stage="full"):
        import os
        self.stop = int(os.environ.get("K_STOP", "-1"))
        self.stop_at = tuple(int(v) for v in os.environ.get("K_AT", "0,0").split(","))
        self.cur = (0, 0)
        self.ckcnt = 0
        self.stop_cnt = int(os.environ.get("K_CNT", "1"))
        self.stage = stage
        self.nc = bass.Bass("TRN2", target_bir_lowering=False)
        self.s = Sched(self.nc)
        self.es = contextlib.ExitStack()
        self.gflip = 0
        self.aflip = 0
        self.nflip = 0
        self.uid = 0
        self.bank_last = {}
        self.bank_round = {}

    def sb(self, name, shape, dt):
        return T(self.es.enter_context(self.nc.sbuf_tensor(name, list(shape), dt)), name)

    def din(self, name, shape, dt=F32):
        return self.nc.dram_tensor(name, list(shape), dt, kind="ExternalInput").ap()

    def dout(self, name, shape, dt=F32):
        return self.nc.dram_tensor(name, list(shape), dt, kind="ExternalOutput").ap()

    def dint(self, name, shape, dt=F32):
        return self.nc.dram_tensor(name, list(shape), dt, kind="Internal").ap()

    def ck(self, n):
        if self.stop == n and self.cur == self.stop_at:
            self.ckcnt += 1
            if self.ckcnt == self.stop_cnt:
                raise _Stop()

    def A(self, eng, fn, r=(), w=()):
        w = list(w) + [x for x in r if getattr(x, "psum", False) and x not in w]
        self.s.add(eng, fn, reads=[x.b if isinstance(x, (T, Vw, Al)) else x for x in r],
                   writes=[x.b if isinstance(x, (T, Vw, Al)) else x for x in w])

    def DMA(self, eng, out, in_, key, r=(), w=()):
        self.s.add(eng, lambda e: e.dma_start(out=out, in_=in_),
                   reads=[x.b if isinstance(x, (T, Vw, Al)) else x for x in r],
                   writes=[x.b if isinstance(x, (T, Vw, Al)) else x for x in w], dma=True,
                   key=key.b if isinstance(key, (T, Vw, Al)) else key)

    def new_round(self, bank):
        self.bank_round[id(bank)] = set()

    def MM(self, bank, out, lhsT, rhs, kbase, ksize, qbase, qsize, r=()):
        rows = set(range(kbase // 32, (kbase + ksize + 31) // 32))
        quads = set(range(qbase // 32, (qbase + qsize + 31) // 32))
        cleared = self.bank_round.setdefault(id(bank), set())
        if quads <= cleared:
            start = False
        else:
            assert not (quads & cleared), (quads, cleared)
            start = True
            cleared |= quads
        last = self.bank_last.get(id(bank))
        extra = []
        if last is not None and not (last[1] & rows):
            extra.append(last[0])
        fn = lambda e: e.matmul(out=out, lhsT=lhsT, rhs=rhs, start=start, stop=True, skip_group_check=True)
        op = self.s.add("pe", fn, reads=[x.b for x in r], writes=[bank.b], extra=extra)
        self.bank_last[id(bank)] = (op, rows)
        return op

    def nextG(self):
        self.gflip ^= 1
        return self.psG[self.gflip]

    def nextA(self):
        self.aflip ^= 1
        return self.psA[self.aflip]

    def nextN(self):
        self.nflip ^= 1
        return self.psN[self.nflip]

    def build(self):
        nc = self.nc
        with self.es:
            self._declare_io()
            self._alloc()
            self._load_consts()
            try:
                for l in range(2):
                    self.layer_a(l)
                if self.stage != "A":
                    self.kv_phase()
                if self.stage not in ("A", "KV"):
                    for lb in range(2):
                        self.layer_b(lb)
            except _Stop:
                import os
                names = [n for n in os.environ.get("K_DUMP", "").split(",") if n]
                for i, n in enumerate(names):
                    t = getattr(self, n)
                    ap = t.h[:] if isinstance(t, T) else t.ap
                    shp = list(ap.shape)
                    dst = self.dout(f"dbg{i}", shp)
                    self.DMA("pool", dst, ap, t, r=[t])
            self.s.emit()
        return nc

    def _declare_io(self):
        I = self.din
        self.xp = I("xp", [SEQ, D])
        self.xs = I("xs", [NS, D])
        self.swkv = I("swkv", [2, SB, 16, 64, 64])
        self.sshift = I("sshift", [2, SB, D])
        self.a_vec = I("a_vec", [2, 128, NV, 8])
        self.a_w_in = I("a_w_in", [2, 4, D, D])
        self.a_w_out = I("a_w_out", [2, D, D])
        self.a_wla = I("a_w_lora_a", [2, D, 64])
        self.a_wlb = I("a_w_lora_b", [2, 64, D])
        self.a_ala = I("a_a_lora_a", [2, D, 64])
        self.a_alb = I("a_a_lora_b", [2, 64, D])
        self.a_vla = I("a_v_lora_a", [1, D, 32])
        self.a_vlb = I("a_v_lora_b", [1, 32, D])
        self.b_vec = I("b_vec", [128, 4, NCH])
        self.w_kv = I("w_kv", [D, 6 * D])
        self.b_w_in = I("b_w_in", [2, D, 4 * D])
        self.b_w_out = I("b_w_out", [2, D, D])
        self.final_ln = I("final_ln", [D])
        self.c_ropeP = I("c_ropeP", [128, NTILE + 1, 2, 32])
        self.c_ropeT = I("c_ropeT", [128, 2, SEQ + NS])
        self.c_maskP = I("c_maskP", [128, 2, 128])
        self.c_prot = I("c_prot", [128, 128])
        self.c_maskS = I("c_maskS", [128, 12, ST_])
        self.ckv = [I(f"ckv{g}", [SB, w, 2, D]) for g, w in enumerate((128, 512, 2048))]
        self.c_ident = I("c_ident", [128, 128])
        self.c_maskA = I("c_maskA", [128, 4, 128])
        self.c_maskAT = I("c_maskAT", [128, 128])
        self.c_blk = I("c_blk", [128, 128])
        self.c_scan = I("c_scan", [128, 2, 128])
        O = self.dout
        self.o_yp = O("y_prompt", [SEQ, D])
        self.o_ys = O("y_sample", [NS, D])
        self.o_wkvp = O("wkv_p", [2, 16, 64, 64])
        self.o_wkvs = O("wkv_s", [2, SB, 16, 64, 64])
        self.o_shp = O("shift_p", [2, D])
        self.o_shs = O("shift_s", [2, SB, D])
        self.o_kvp = [O(f"kv{g}p", [w, 2, D]) for g, w in enumerate((128, 512, 2048))]
        self.o_kvs = [O(f"kv{g}s", [NS, 2, D]) for g in range(3)]
        self.d_kT = self.dint("d_kT", [3, NCH, 128, SEQ + NS], BF16)
        self.d_v = self.dint("d_v", [3, SEQ + NS, D], BF16)
        self.kTbuf = [Buf(f"kTd{g}") for g in range(3)]
        self.vbuf = [Buf(f"vd{g}") for g in range(3)]
        self.d_vf = self.dint("d_vf", [NTILE + 1, 128, NCH, 128])
        self.xbuf = [Buf(f"xd{i}") for i in range(NTILE + 1)]
        self.vfbuf = [Buf(f"vfd{i}") for i in range(NTILE + 1)]

    def x_dram(self, ti):
        if ti < NTILE:
            return self.o_yp[ti * 128:(ti + 1) * 128, :]
        return self.o_ys[:, :]

    def x_in(self, ti):
        if ti < NTILE:
            return self.xp[ti * 128:(ti + 1) * 128, :]
        return self.xs[:, :]

    def _alloc(self):
        sb = self.sb
        nc = self.nc
        banks = [T(self.es.enter_context(nc.psum_tensor(f"ps{i}", [128, 512], F32)), f"ps{i}") for i in range(8)]
        for bk_ in banks:
            bk_.psum = True
        self.psG = banks[0:2]
        self.psA = banks[2:4]
        self.psN = banks[4:6]
        self.psX = banks[6]
        self.psY = banks[7]
        self.identf = sb("identf", [128, 128], F32)
        self.identb = sb("identb", [128, 128], BF16)
        self.maskA = sb("maskA", [128, 4, 128], BF16)
        self.maskAT = sb("maskAT", [128, 128], BF16)
        self.blkf = sb("blkf", [128, 128], F32)
        self.blkb = sb("blkb", [128, 128], BF16)
        self.scanm = sb("scanm", [128, 2, 128], F32)
        self.W = [sb(f"W{p}", [128, NCH, D], BF16) for p in range(5)]
        self.Wla = sb("Wla", [128, NCH, 64], BF16)
        self.Ala = sb("Ala", [128, NCH, 64], BF16)
        self.Vla = sb("Vla", [128, NCH, 32], BF16)
        self.Wlb = sb("Wlb", [64, D], BF16)
        self.Alb = sb("Alb", [64, D], BF16)
        self.Vlb = sb("Vlb", [32, D], BF16)
        self.VEC = sb("VEC", [128, NV, NCH], F32)
        self.BV = sb("BV", [128, 4, NCH], F32)
        self.ropeS = sb("ropeS", [128, 2, 32], F32)
        self.maskP = sb("maskP", [128, 2, 128], BF16)
        self.ones_b = sb("ones_b", [128, 64], BF16)
        self.prot = sb("prot", [128, 128], BF16)
        self.XNs = sb("XNs", [128, NCH, NS], BF16)
        self.OGs = sb("OGs", [128, NCH, NS], BF16)
        self.QTs = sb("QTs", [128, NCH, 3, NS], BF16)
        self.zTs = sb("zTs", [128, NCH, NS], BF16)
        self.KTs = sb("KTs", [128, 3, NCH, NS], BF16)
        self.ropeTs = sb("ropeTs", [128, 2, NS], BF16)
        self.Xt = sb("Xt", [128, D], F32)
        self.junk = sb("junk", [128, D], BF16)
        self.st1 = sb("st1", [128, 4], F32)
        FM = lambda n, dt=F32: sb(n, [128, NCH, 128], dt)
        self.F = [FM(f"F{i}") for i in range(8)]
        F = self.F
        flat = lambda t: Vw(t, t.h[:, :, :].rearrange("p a b -> p (a b)"))
        st4 = lambda t: Vw(t, t.h[0:64, :, :].rearrange("p a (b c) -> p a b c", c=64))
        self.xr = flat(F[7])
        self.xnT = F[4]
        self.dxT = F[5]
        self.carry = sb("carry", [128, NCH, 1], F32)
        self.shiftT = sb("shiftT", [128, NCH, SB], F32)
        self.shtok = flat(F[0])
        self.rowo = flat(F[0])
        self.MX = [FM(f"MX{i}", BF16) for i in range(2)]
        self.rT = FM("rT")
        self.kT = FM("kT")
        self.vT = FM("vT")
        self.vfT = F[6]
        self.szT = FM("szT", BF16)
        self.hid = sb("hid", [64, 128], F32)
        self.hidb = sb("hidb", [64, 128], BF16)
        self.hids = [(self.hid, self.hidb), (sb("hid2", [64, 128], F32), sb("hidb2", [64, 128], BF16))]
        self.AR = sb("AR", [128, NCH, 2, 128], BF16)
        self.BK = sb("BK", [128, NCH, 2, 128], BF16)
        self.vTb = FM("vTb", BF16)
        self.BhT = FM("BhT", BF16)
        self.KhT = FM("KhT", BF16)
        self.Vt = sb("Vt", [128, D], BF16)
        self.Bh = sb("Bh", [128, D], BF16)
        self.Kh = sb("Kh", [128, D], BF16)
        self.WL = sb("WL", [128, NCH, SB], F32)
        self.Asb = sb("Asb", [128, 8, 4, 128], BF16)
        self.Nsb = [sb(f"Nsb{i}", [128, 8, 2, 128], BF16) for i in range(2)]
        self.Xb = sb("Xb", [128, 8, 64], BF16)
        self.STf = sb("STf", [128, NCH, 64], F32)
        self.STb = sb("STb", [128, NCH, 64], BF16)
        self.STfG = [Al(f"STfG{i}", self.STf.h[:, 2 * i:2 * i + 2, :]) for i in range(4)]
        self.STbG = [Al(f"STbG{i}", self.STb.h[:, 2 * i:2 * i + 2, :]) for i in range(4)]
        self.SI = st4(F[3])
        self.SO = st4(F[1])
        self.YT = F[6]
        self.OT = FM("OT", BF16)

    def _load_consts(self):
        D_ = self.DMA
        D_("sp", self.identf[:], self.c_ident[:, :], self.identf, w=[self.identf])
        D_("pool", self.identb[:], self.c_ident[:, :], self.identb, w=[self.identb])
        D_("pool", self.maskA[:], self.c_maskA[:, :, :], self.maskA, w=[self.maskA])
        D_("pool", self.maskAT[:], self.c_maskAT[:, :], self.maskAT, w=[self.maskAT])
        D_("sp", self.blkf[:], self.c_blk[:, :], self.blkf, w=[self.blkf])
        D_("pool", self.blkb[:], self.c_blk[:, :], self.blkb, w=[self.blkb])
        D_("sp", self.scanm[:], self.c_scan[:, :, :], self.scanm, w=[self.scanm])
        D_("sp", self.BV[:], self.b_vec[:, :, :], self.BV, w=[self.BV])
        D_("sp", self.ropeS[:], self.c_ropeP[:, NTILE, :, :], self.ropeS, w=[self.ropeS])
        D_("pool", self.maskP[:], self.c_maskP[:, :, :], self.maskP, w=[self.maskP])
        D_("pool", self.prot[:], self.c_prot[:, :], self.prot, w=[self.prot])
        self.A("pool", lambda e: e.memset(self.ones_b[:], 1.0), w=[self.ones_b])

    def load_layer_a_weights(self, l):
        D_ = self.DMA
        for p in range(5):
            src = self.a_w_in[l, p] if p < 4 else self.a_w_out[l]
            srcv = src.rearrange("(c p) e -> p c e", p=128)
            for c in range(NCH):
                D_("pool", self.W[p][:, c, :], srcv[:, c, :], self.W[p], w=[self.W[p]])
        D_("pool", self.Wla[:], self.a_wla[l].rearrange("(c p) k -> p c k", p=128), self.Wla, w=[self.Wla])
        D_("pool", self.Ala[:], self.a_ala[l].rearrange("(c p) k -> p c k", p=128), self.Ala, w=[self.Ala])
        D_("pool", self.Wlb[:], self.a_wlb[l], self.Wlb, w=[self.Wlb])
        D_("pool", self.Alb[:], self.a_alb[l], self.Alb, w=[self.Alb])
        if l == 1:
            D_("pool", self.Vla[:], self.a_vla[0].rearrange("(c p) k -> p c k", p=128), self.Vla, w=[self.Vla])
            D_("pool", self.Vlb[:], self.a_vlb[0], self.Vlb, w=[self.Vlb])
        D_("sp", self.VEC[:], self.a_vec[l], self.VEC, w=[self.VEC])
        V = self.VEC
        A = self.A
        A("dve", lambda e: e.tensor_scalar(out=V[:, V_NW0:V_NW0 + 3, :], in0=V[:, V_W0:V_W0 + 3, :], scalar1=-1.0,
                                           scalar2=None, op0=ALU.mult), r=[V], w=[V])
        A("dve", lambda e: e.tensor_scalar(out=V[:, V_OMK, :], in0=V[:, V_KA, :], scalar1=-1.0, scalar2=1.0,
                                           op0=ALU.mult, op1=ALU.add), r=[V], w=[V])

    def layer_a(self, l):
        A = self.A
        self.load_layer_a_weights(l)
        A("pool", lambda e: e.memset(self.STf[:], 0.0), w=self.STfG)
        A("pool", lambda e: e.memset(self.STb[:], 0.0), w=self.STbG)
        A("pool", lambda e: e.memset(self.carry[:], 0.0), w=[self.carry])
        for ti in range(NTILE):
            self.tile_a(l, ti, 128, sample=False)
        self.store_state(self.o_wkvp[l])
        self.store_shift_prompt(l)
        self.load_shift_sample(l)
        self.tile_a(l, NTILE, NS, sample=True)

    def store_state(self, dst):
        A = self.A
        for half in range(2):
            ps = self.nextG()
            psv = ps[:, :].rearrange("p (a b) -> p a b", b=128)
            for q in range(4):
                hp = half * 4 + q
                A("pe", lambda e, psv=psv, q=q, hp=hp: e.transpose(out=psv[0:64, q, :], in_=self.STf[:, hp, :],
                                                                   identity=self.identf[:, :]),
                  r=self.STfG + [self.identf], w=[ps])
            A("act", lambda e, psv=psv, half=half: e.activation(
                out=self.SO[:, half * 4:(half + 1) * 4, :, :].rearrange("p a b c -> p a (b c)"),
                in_=psv[0:64, :, :], func=AF.Copy), r=[ps], w=[self.SO])
        self.DMA("sp", dst.rearrange("(hp hh) i j -> i hp hh j", hh=2), self.SO[:, :, :, :], self.SO, r=[self.SO])

    def load_state(self, src):
        A = self.A
        self.DMA("sp", self.SI[:, :, :, :], src.rearrange("(hp hh) i j -> i hp hh j", hh=2), self.SI, w=[self.SI])
        self.ck(111)
        ps = self.nextG()
        psv = ps[:, :].rearrange("p (a b) -> p a b", b=64)
        for hp in range(NCH):
            A("pe", lambda e, psv=psv, hp=hp: e.transpose(
                out=psv[:, hp, :], in_=self.SI[:, hp, :, :].rearrange("p a b -> p (a b)"),
                identity=self.identf[0:64, 0:64]), r=[self.SI, self.identf], w=[ps])
        self.ck(112)
        A("act", lambda e, psv=psv: e.activation(out=self.STf[:], in_=psv[:, :, :], func=AF.Copy), r=[ps], w=self.STfG)
        self.ck(113)
        A("dve", lambda e, psv=psv: e.tensor_copy(out=self.STb[:], in_=psv[:, :, :]), r=[ps], w=self.STbG)

    def store_shift_prompt(self, l):
        A = self.A
        ps = self.nextG()
        for half in range(2):
            if half == 1:
                ps2 = self.nextG()
            else:
                ps2 = ps
            for q in range(4):
                c = half * 4 + q
                A("pe", lambda e, ps2=ps2, q=q, c=c: e.transpose(out=ps2[0:1, q * 128:(q + 1) * 128],
                                                                  in_=self.carry[:, c, 0:1], identity=self.identf[:, :]),
                  r=[self.carry, self.identf], w=[ps2])
            A("act", lambda e, ps2=ps2, half=half: e.activation(out=self.rowo[0:1, half * 512:(half + 1) * 512],
                                                                in_=ps2[0:1, :], func=AF.Copy), r=[ps2], w=[self.rowo])
        self.DMA("sp", self.o_shp[l:l + 1, :], self.rowo[0:1, :], self.rowo, r=[self.rowo])

    def load_shift_sample(self, l):
        A = self.A
        self.DMA("sp", self.shtok[0:SB, :], self.sshift[l], self.shtok, w=[self.shtok])
        ps = self.nextG()
        psv = ps[:, 0:NCH * SB].rearrange("p (a b) -> p a b", b=SB)
        for c in range(NCH):
            A("pe", lambda e, psv=psv, c=c: e.transpose(out=psv[:, c, :], in_=self.shtok[0:SB, c * 128:(c + 1) * 128],
                                                        identity=self.identf[0:SB, 0:SB]),
              r=[self.shtok, self.identf], w=[ps])
        A("act", lambda e, psv=psv: e.activation(out=self.shiftT[:], in_=psv[:, :, :], func=AF.Copy), r=[ps], w=[self.shiftT])

    def tile_a(self, l, ti, nt, sample):
        self.cur = (l, ti)
        A = self.A
        V = self.VEC
        Xt, xr, xnT, dxT = self.Xt, self.xr, self.xnT, self.dxT
        F = self.F
        if l == 0:
            self.DMA("sp", Xt[:nt, :], self.x_in(ti), Xt, w=[Xt])
        else:
            self.DMA("sp", Xt[:nt, :], self.x_dram(ti), Xt, r=[self.xbuf[ti]], w=[Xt])
        st1 = self.st1
        A("pool", lambda e: e.memset(st1[:], 0.0), w=[st1])
        A("act", lambda e: e.activation(out=self.junk[:nt, :], in_=Xt[:nt, :], func=AF.Square, accum_out=st1[:nt, 0:1]),
          r=[Xt, st1], w=[self.junk, st1])
        A("act", lambda e: e.activation(out=st1[:nt, 1:2], in_=st1[:nt, 0:1], func=AF.Ln, scale=1.0 / D, bias=RMS_EPS),
          r=[st1], w=[st1])
        A("act", lambda e: e.activation(out=st1[:nt, 2:3], in_=st1[:nt, 1:2], func=AF.Exp, scale=-0.5), r=[st1], w=[st1])
        A("act", lambda e: e.activation(out=xr[:nt, :], in_=Xt[:nt, :], func=AF.Identity, scale=st1[:nt, 2:3]),
          r=[Xt, st1], w=[xr])
        self.ck(1)
        for half in range(2):
            ps = self.nextG()
            psv = ps[:, :].rearrange("p (a b) -> p a b", b=128)
            for q in range(4):
                c = half * 4 + q
                A("pe", lambda e, psv=psv, q=q, c=c: e.transpose(out=psv[:, q, :nt], in_=xr[:nt, c * 128:(c + 1) * 128],
                                                                 identity=self.identf[:nt, :nt]),
                  r=[xr, self.identf], w=[ps])
            A("dve", lambda e, psv=psv, half=half: e.tensor_tensor(
                out=xnT[:, half * 4:(half + 1) * 4, :nt], in0=psv[:, :, :nt],
                in1=bc(V[:, V_LN, half * 4:(half + 1) * 4].unsqueeze(2), [128, 4, nt]), op=ALU.mult),
              r=[ps, V], w=[xnT])
        self.ck(2)
        if not sample:
            A("dve", lambda e: e.tensor_tensor(out=dxT[:, :, 1:nt], in0=xnT[:, :, 0:nt - 1], in1=xnT[:, :, 1:nt],
                                               op=ALU.subtract), r=[xnT], w=[dxT])
            A("dve", lambda e: e.tensor_tensor(out=dxT[:, :, 0:1], in0=self.carry[:, :, 0:1], in1=xnT[:, :, 0:1],
                                               op=ALU.subtract), r=[xnT, self.carry], w=[dxT])
            A("dve", lambda e: e.tensor_copy(out=self.carry[:, :, 0:1], in_=xnT[:, :, nt - 1:nt]), r=[xnT], w=[self.carry])
        else:
            x4 = xnT[:, :, 0:NS].rearrange("p c (b t) -> p c b t", t=ST_)
            d4 = dxT[:, :, 0:NS].rearrange("p c (b t) -> p c b t", t=ST_)
            A("dve", lambda e: e.tensor_tensor(out=d4[:, :, :, 1:ST_], in0=x4[:, :, :, 0:ST_ - 1], in1=x4[:, :, :, 1:ST_],
                                               op=ALU.subtract), r=[xnT], w=[dxT])
            A("dve", lambda e: e.tensor_tensor(out=d4[:, :, :, 0:1], in0=self.shiftT[:].unsqueeze(3), in1=x4[:, :, :, 0:1],
                                               op=ALU.subtract), r=[xnT, self.shiftT], w=[dxT])
            for half in range(2):
                ps = self.nextG()
                for q in range(4):
                    c = half * 4 + q
                    A("pe", lambda e, ps=ps, q=q, c=c: e.transpose(out=ps[0:SB, q * 128:(q + 1) * 128],
                                                                    in_=x4[:, c, :, ST_ - 1], identity=self.identf[:, :]),
                      r=[xnT, self.identf], w=[ps])
                A("act", lambda e, ps=ps, half=half: e.activation(out=self.rowo[0:SB, half * 512:(half + 1) * 512],
                                                                  in_=ps[0:SB, :], func=AF.Copy), r=[ps], w=[self.rowo])
            self.DMA("sp", self.o_shs[l], self.rowo[0:SB, :], self.rowo, r=[self.rowo])

        def mix(p, dst):
            for c in range(NCH):
                eng = "dve"
                A(eng, lambda e, c=c: e.scalar_tensor_tensor(out=dst[:, c, :nt], in0=dxT[:, c, :nt],
                                                             scalar=V[:, V_MU + p, c:c + 1], in1=xnT[:, c, :nt],
                                                             op0=ALU.mult, op1=ALU.add), r=[dxT, xnT, V], w=[dst])

        def proj(Wt, src, evac):
            for half in range(2):
                ps = self.nextG()
                psv = ps[:, :].rearrange("p (a b) -> p a b", b=128)
                for q in range(4):
                    eo = half * 4 + q
                    for c in range(NCH):
                        A("pe", lambda e, psv=psv, q=q, eo=eo, c=c: e.matmul(
                            out=psv[:, q, :nt], lhsT=Wt[:, c, eo * 128:(eo + 1) * 128], rhs=src[:, c, :nt],
                            start=(c == 0), stop=(c == NCH - 1)), r=[Wt, src], w=[ps])
                evac(half, ps, psv)

        def evac_copy(dst, eng="act"):
            def f(half, ps, psv):
                if eng == "act":
                    A("act", lambda e: e.activation(out=dst[:, half * 4:(half + 1) * 4, :nt], in_=psv[:, :, :nt], func=AF.Copy),
                      r=[ps], w=[dst])
                else:
                    A("dve", lambda e: e.tensor_copy(out=dst[:, half * 4:(half + 1) * 4, :nt], in_=psv[:, :, :nt]),
                      r=[ps], w=[dst])
            return f

        def sigmoid_from_psum(dst, negbias_slot):
            def f(half, ps, psv):
                for q in range(4):
                    c = half * 4 + q
                    A("act", lambda e, q=q, c=c: e.activation(out=dst[:, c, :nt], in_=psv[:, q, :nt], func=AF.Exp,
                                                              scale=-1.0, bias=V[:, negbias_slot, c:c + 1]),
                      r=[ps, V], w=[dst])
                if half == 1:
                    A("act", lambda e: e.activation(out=dst[:, :, :nt], in_=dst[:, :, :nt], func=AF.Ln, bias=1.0), r=[dst], w=[dst])
                    A("act", lambda e: e.activation(out=dst[:, :, :nt], in_=dst[:, :, :nt], func=AF.Exp, scale=-1.0), r=[dst], w=[dst])
            return f

        def lora_hidden(Wa, src, nh, tanh, hset):
            ps = self.nextG()
            for c in range(NCH):
                A("pe", lambda e, ps=ps, c=c: e.matmul(out=ps[0:nh, 0:nt], lhsT=Wa[:, c, :], rhs=src[:, c, :nt],
                                                       start=(c == 0), stop=(c == NCH - 1)), r=[Wa, src], w=[ps])
            hid, hidb = self.hids[hset]
            if tanh:
                A("act", lambda e: e.activation(out=hid[0:nh, :nt], in_=ps[0:nh, 0:nt], func=AF.Exp, scale=2.0), r=[ps], w=[hid])
                A("dve", lambda e: e.tensor_scalar(out=hid[0:nh, :nt], in0=hid[0:nh, :nt], scalar1=1.0, scalar2=None, op0=ALU.add),
                  r=[hid], w=[hid])
                A("dve", lambda e: e.reciprocal(out=hid[0:nh, :nt], in_=hid[0:nh, :nt]), r=[hid], w=[hid])
                A("dve", lambda e: e.tensor_scalar(out=hidb[0:nh, :nt], in0=hid[0:nh, :nt], scalar1=-2.0, scalar2=1.0,
                                                   op0=ALU.mult, op1=ALU.add), r=[hid], w=[hidb])
            else:
                A("act", lambda e: e.activation(out=hidb[0:nh, :nt], in_=ps[0:nh, 0:nt], func=AF.Copy), r=[ps], w=[hidb])

        def lora_out(Wb, nh, evac, hset):
            hid, hidb = self.hids[hset]
            for half in range(2):
                ps2 = self.nextG()
                psv = ps2[:, :].rearrange("p (a b) -> p a b", b=128)
                for q in range(4):
                    eo = half * 4 + q
                    A("pe", lambda e, psv=psv, q=q, eo=eo: e.matmul(out=psv[:, q, :nt], lhsT=Wb[0:nh, eo * 128:(eo + 1) * 128],
                                                                    rhs=hidb[0:nh, :nt], start=True, stop=True),
                      r=[Wb, hidb], w=[ps2])
                evac(half, ps2, psv)

        def lora(Wa, Wb, src, nh, tanh, evac):
            lora_hidden(Wa, src, nh, tanh, 0)
            lora_out(Wb, nh, evac, 0)

        rT, kT, vT, szT = self.rT, self.kT, self.vT, self.szT
        MX = self.MX
        recw = F[1]
        aT = F[2]
        ez = F[0]
        mix(4, MX[0]); mix(5, MX[1])
        lora_hidden(self.Wla, MX[0], 64, True, 0)
        lora_hidden(self.Ala, MX[1], 64, False, 1)
        mix(0, MX[0]); proj(self.W[0], MX[0], evac_copy(rT, "act"))
        self.ck(4)
        lora_out(self.Wlb, 64, sigmoid_from_psum(recw, V_NW0), 0)
        mix(1, MX[1]); proj(self.W[1], MX[1], evac_copy(kT, "dve"))
        lora_out(self.Alb, 64, sigmoid_from_psum(aT, V_NA0), 1)
        mix(2, MX[0]); proj(self.W[2], MX[0], evac_copy(vT, "act"))
        if l == 1:
            gv = F[0]
            lora(self.Vla, self.Vlb, MX[0], 32, False, sigmoid_from_psum(gv, V_NV0))
            vf = self.vfT
            self.DMA("sp", vf[:, :, :nt], self.d_vf[ti, :, :, 0:nt], vf, r=[self.vfbuf[ti]], w=[vf])
            A("dve", lambda e: e.tensor_tensor(out=vf[:, :, :nt], in0=vf[:, :, :nt], in1=vT[:, :, :nt], op=ALU.subtract),
              r=[vf, vT], w=[vf])
            A("dve", lambda e: e.tensor_tensor(out=vf[:, :, :nt], in0=vf[:, :, :nt], in1=gv[:, :, :nt], op=ALU.mult),
              r=[vf, gv], w=[vf])
            A("dve", lambda e: e.tensor_tensor(out=vT[:, :, :nt], in0=vT[:, :, :nt], in1=vf[:, :, :nt], op=ALU.add),
              r=[vf, vT], w=[vT])
        else:
            self.DMA("sp", self.d_vf[ti, :, :, 0:nt], vT[:, :, :nt], vT, r=[vT], w=[self.vfbuf[ti]])
        self.ck(5)

        def evac_z(half, ps, psv):
            sl = slice(half * 4, (half + 1) * 4)
            A("act", lambda e: e.activation(out=ez[:, sl, :nt], in_=psv[:, :, :nt], func=AF.Exp, scale=-1.0), r=[ps], w=[ez])
            A("act", lambda e: e.activation(out=ez[:, sl, :nt], in_=ez[:, sl, :nt], func=AF.Ln, bias=1.0), r=[ez], w=[ez])
            A("act", lambda e: e.activation(out=ez[:, sl, :nt], in_=ez[:, sl, :nt], func=AF.Exp, scale=-1.0), r=[ez], w=[ez])
            A("dve", lambda e: e.tensor_tensor(out=szT[:, sl, :nt], in0=psv[:, :, :nt], in1=ez[:, sl, :nt], op=ALU.mult),
              r=[ps, ez], w=[szT])
        mix(3, MX[1]); proj(self.W[3], MX[1], evac_z)
        self.ck(6)
        self.ck(7)
        cum = F[3]
        sm = 1 if sample else 0
        for c in range(NCH):
            A("dve", lambda e, c=c: e.tensor_tensor_scan(out=cum[:, c, :nt], data0=self.scanm[:, sm, :nt], data1=recw[:, c, :nt],
                                                         initial=0.0, op0=ALU.mult, op1=ALU.add), r=[recw, self.scanm], w=[cum])
        A("pool", lambda e: e.tensor_tensor(out=recw[:, :, :nt], in0=cum[:, :, :nt], in1=recw[:, :, :nt], op=ALU.subtract),
          r=[cum, recw], w=[recw])
        Epos, Eneg, Eprev = F[4], F[5], F[6]
        A("act", lambda e: e.activation(out=Epos[:, :, :nt], in_=cum[:, :, :nt], func=AF.Exp, scale=-CDEC), r=[cum], w=[Epos])
        A("act", lambda e: e.activation(out=Eneg[:, :, :nt], in_=cum[:, :, :nt], func=AF.Exp, scale=CDEC), r=[cum], w=[Eneg])
        A("act", lambda e: e.activation(out=Eprev[:, :, :nt], in_=recw[:, :, :nt], func=AF.Exp, scale=-CDEC), r=[recw], w=[Eprev])
        nb = SB if sample else 1
        L = ST_ if sample else 128
        WL = self.WL
        cum4 = cum[:, :, 0:nb * L].rearrange("p c (b t) -> p c b t", t=L)
        A("act", lambda e: e.activation(out=WL[:, :, 0:nb], in_=cum4[:, :, :, L - 1], func=AF.Exp, scale=-CDEC), r=[cum], w=[WL])
        self.ck(8)
        kk = F[7]
        sq = self.junk
        sqv = sq[:, :].rearrange("p (a b) -> p a b", b=128)
        for c in range(NCH):
            A("act", lambda e, c=c: e.activation(out=sqv[:, c, :nt], in_=kT[:, c, :nt], func=AF.Square, scale=V[:, V_KK, c:c + 1]),
              r=[kT, V], w=[sq])
        rs = F[0]
        for half in range(2):
            ps = self.nextG()
            psv = ps[:, :].rearrange("p (a b) -> p a b", b=128)
            for q in range(4):
                c = half * 4 + q
                A("pe", lambda e, psv=psv, q=q, c=c: e.matmul(out=psv[:, q, :nt], lhsT=self.blkb[:, :], rhs=sqv[:, c, :nt],
                                                              start=True, stop=True), r=[self.blkb, sq], w=[ps])
            sl = slice(half * 4, (half + 1) * 4)
            A("act", lambda e, psv=psv, sl=sl: e.activation(out=rs[:, sl, :nt], in_=psv[:, :, :nt], func=AF.Ln, bias=1e-24),
              r=[ps], w=[rs])
        A("act", lambda e: e.activation(out=rs[:, :, :nt], in_=rs[:, :, :nt], func=AF.Exp, scale=-0.5), r=[rs], w=[rs])
        for c in range(NCH):
            eng = "dve"
            A(eng, lambda e, c=c: e.scalar_tensor_tensor(out=kk[:, c, :nt], in0=kT[:, c, :nt], scalar=V[:, V_KK, c:c + 1],
                                                         in1=rs[:, c, :nt], op0=ALU.mult, op1=ALU.mult), r=[kT, rs, V], w=[kk])
        tmp = F[0]
        for c in range(NCH):
            A("act", lambda e, c=c: e.activation(out=tmp[:, c, :nt], in_=aT[:, c, :nt], func=AF.Identity,
                                                 scale=V[:, V_KA, c:c + 1], bias=V[:, V_OMK, c:c + 1]), r=[aT, V], w=[tmp])
        A("dve", lambda e: e.tensor_tensor(out=kT[:, :, :nt], in0=kT[:, :, :nt], in1=tmp[:, :, :nt], op=ALU.mult),
          r=[kT, tmp], w=[kT])
        self.ck(9)
        AR, BK = self.AR, self.BK
        A("dve", lambda e: e.scalar_tensor_tensor(out=AR[:, :, 0, :nt], in0=kk[:, :, :nt], scalar=-1.0, in1=Eprev[:, :, :nt],
                                                  op0=ALU.mult, op1=ALU.mult), r=[kk, Eprev], w=[AR])
        A("pool", lambda e: e.tensor_tensor(out=AR[:, :, 1, :nt], in0=rT[:, :, :nt], in1=Epos[:, :, :nt], op=ALU.mult),
          r=[rT, Epos], w=[AR])
        A("dve", lambda e: e.tensor_tensor(out=aT[:, :, :nt], in0=aT[:, :, :nt], in1=kk[:, :, :nt], op=ALU.mult), r=[aT, kk], w=[aT])
        A("dve", lambda e: e.tensor_tensor(out=aT[:, :, :nt], in0=aT[:, :, :nt], in1=Eneg[:, :, :nt], op=ALU.mult), r=[aT, Eneg], w=[aT])
        A("pool", lambda e: e.tensor_tensor(out=Eneg[:, :, :nt], in0=Eneg[:, :, :nt], in1=kT[:, :, :nt], op=ALU.mult),
          r=[kT, Eneg], w=[Eneg])
        A("act", lambda e: e.activation(out=BK[:, :, 0, :nt], in_=aT[:, :, :nt], func=AF.Copy), r=[aT], w=[BK])
        A("act", lambda e: e.activation(out=BK[:, :, 1, :nt], in_=Eneg[:, :, :nt], func=AF.Copy), r=[Eneg], w=[BK])
        WLb = bc(WL[:, :, 0:nb].unsqueeze(3), [128, NCH, nb, L])
        v4 = lambda t: t[:, :, 0:nb * L].rearrange("p c (b t) -> p c b t", t=L)
        A("dve", lambda e: e.tensor_tensor(out=v4(self.BhT), in0=v4(aT), in1=WLb, op=ALU.mult), r=[aT, WL], w=[self.BhT])
        A("pool", lambda e: e.tensor_tensor(out=v4(self.KhT), in0=v4(Eneg), in1=WLb, op=ALU.mult), r=[Eneg, WL], w=[self.KhT])
        A("act", lambda e: e.activation(out=self.vTb[:, :, :nt], in_=vT[:, :, :nt], func=AF.Copy), r=[vT], w=[self.vTb])
        self.ck(10)
        YT = self.YT
        for b in range(nb):
            c0 = b * L
            if sample and not (_os.environ.get("K_NOLOAD") and b > 0):
                self.load_state(self.swkv[l, b])
            self.ck(110)
            for (srcT, dstt) in ((self.vTb, self.Vt), (self.BhT, self.Bh), (self.KhT, self.Kh)):
                ps = self.nextG()
                psb = ps.h.bitcast(BF16)
                for c in range(NCH):
                    A("pe", lambda e, psb=psb, c=c, srcT=srcT, c0=c0: e.transpose(out=psb[0:L, c * 128:(c + 1) * 128],
                                                                            in_=srcT[:, c, c0:c0 + L], identity=self.identb[:, :]),
                      r=[srcT, self.identb], w=[ps])
                A("act", lambda e, psb=psb, dstt=dstt: e.activation(out=dstt[0:L, :], in_=psb[0:L, :], func=AF.Copy),
                  r=[ps], w=[dstt])
            self.ck(11)
            self.wkv_chunk(L, c0, b)
            self.ck(12)
            if sample and not _os.environ.get("K_NOSTORE"):
                self.store_state(self.o_wkvs[l, b])
            self.ck(100 + b)
        self.ck(13)
        yc = F[1]
        for half in range(2):
            ps = self.nextG()
            psv = ps[:, :].rearrange("p (a b) -> p a b", b=128)
            sl = slice(half * 4, (half + 1) * 4)
            for q in range(4):
                c = half * 4 + q
                A("pe", lambda e, psv=psv, q=q, c=c: e.matmul(out=psv[:, q, :nt], lhsT=self.blkf[:, :], rhs=YT[:, c, :nt],
                                                              start=True, stop=True), r=[self.blkf, YT], w=[ps])
            A("dve", lambda e, psv=psv, sl=sl: e.scalar_tensor_tensor(out=yc[:, sl, :nt], in0=psv[:, :, :nt], scalar=-1.0 / 64,
                                                                      in1=YT[:, sl, :nt], op0=ALU.mult, op1=ALU.add),
              r=[ps, YT], w=[yc])
        ysq = F[3]
        A("act", lambda e: e.activation(out=ysq[:, :, :nt], in_=yc[:, :, :nt], func=AF.Square), r=[yc], w=[ysq])
        rstd = F[4]
        for half in range(2):
            ps = self.nextG()
            psv = ps[:, :].rearrange("p (a b) -> p a b", b=128)
            sl = slice(half * 4, (half + 1) * 4)
            for q in range(4):
                c = half * 4 + q
                A("pe", lambda e, psv=psv, q=q, c=c: e.matmul(out=psv[:, q, :nt], lhsT=self.blkf[:, :], rhs=ysq[:, c, :nt],
                                                              start=True, stop=True), r=[self.blkf, ysq], w=[ps])
            A("act", lambda e, psv=psv, sl=sl: e.activation(out=rstd[:, sl, :nt], in_=psv[:, :, :nt], func=AF.Ln,
                                                            scale=1.0 / 64, bias=GN_EPS), r=[ps], w=[rstd])
        A("act", lambda e: e.activation(out=rstd[:, :, :nt], in_=rstd[:, :, :nt], func=AF.Exp, scale=-0.5), r=[rstd], w=[rstd])
        A("dve", lambda e: e.tensor_tensor(out=yc[:, :, :nt], in0=yc[:, :, :nt], in1=rstd[:, :, :nt], op=ALU.mult),
          r=[yc, rstd], w=[yc])
        for c in range(NCH):
            A("act", lambda e, c=c: e.activation(out=yc[:, c, :nt], in_=yc[:, c, :nt], func=AF.Identity,
                                                 scale=V[:, V_GNW, c:c + 1], bias=V[:, V_GNB, c:c + 1]), r=[yc, V], w=[yc])
        rk = F[5]
        for c in range(NCH):
            eng = "dve"
            A(eng, lambda e, c=c: e.scalar_tensor_tensor(out=rk[:, c, :nt], in0=rT[:, c, :nt], scalar=V[:, V_RK, c:c + 1],
                                                         in1=kT[:, c, :nt], op0=ALU.mult, op1=ALU.mult), r=[rT, kT, V], w=[rk])
        for half in range(2):
            ps = self.nextG()
            psv = ps[:, :].rearrange("p (a b) -> p a b", b=128)
            sl = slice(half * 4, (half + 1) * 4)
            for q in range(4):
                c = half * 4 + q
                A("pe", lambda e, psv=psv, q=q, c=c: e.matmul(out=psv[:, q, :nt], lhsT=self.blkf[:, :], rhs=rk[:, c, :nt],
                                                              start=True, stop=True), r=[self.blkf, rk], w=[ps])
            A("dve", lambda e, psv=psv, sl=sl: e.tensor_tensor(out=ysq[:, sl, :nt], in0=psv[:, :, :nt], in1=vT[:, sl, :nt],
                                                               op=ALU.mult), r=[ps, vT], w=[ysq])
        A("dve", lambda e: e.tensor_tensor(out=yc[:, :, :nt], in0=yc[:, :, :nt], in1=ysq[:, :, :nt], op=ALU.add), r=[yc, ysq], w=[yc])
        OT = self.OT
        A("dve", lambda e: e.tensor_tensor(out=OT[:, :, :nt], in0=yc[:, :, :nt], in1=szT[:, :, :nt], op=ALU.mult), r=[yc, szT], w=[OT])
        self.ck(14)
        Wo = self.W[4]
        for half in range(2):
            ps = self.nextG()
            for c in range(NCH):
                A("pe", lambda e, ps=ps, c=c, half=half: e.matmul(out=ps[:nt, :], lhsT=OT[:, c, :nt],
                                                                  rhs=Wo[:, c, half * 512:(half + 1) * 512],
                                                                  start=(c == 0), stop=(c == NCH - 1)), r=[OT, Wo], w=[ps])
            A("dve", lambda e, ps=ps, half=half: e.tensor_tensor(out=Xt[:nt, half * 512:(half + 1) * 512], in0=ps[:nt, :],
                                                                 in1=Xt[:nt, half * 512:(half + 1) * 512], op=ALU.add),
              r=[ps, Xt], w=[Xt])
        self.DMA("sp", self.x_dram(ti), Xt[:nt, :], Xt, r=[Xt], w=[self.xbuf[ti]])
        self.ck(15)

    def wkv_chunk(self, L, c0, b):
        if not hasattr(self, "AsbG"):
            self.AsbG = [Al(f"AsbG{i}", self.Asb.h[:, 4 * i:4 * i + 4, :, :]) for i in range(2)]
            self.NsbG = [[Al(f"NsbG{j}{i}", self.Nsb[j].h[:, 4 * i:4 * i + 4, :, :]) for i in range(2)] for j in range(2)]
            self.XbG = [Al(f"XbG{i}", self.Xb.h[:, 4 * i:4 * i + 4, :]) for i in range(2)]
        A = self.A
        MM = self.MM
        AR, BK = self.AR, self.BK
        Vt, Bh, Kh, YT, WL = self.Vt, self.Bh, self.Kh, self.YT, self.WL
        nst = max(1, int(math.log2(L)))
        cs = slice(c0, c0 + L)
        psXs = [self.psX, self.psY]
        for pi in range(2):
            grps = [2 * pi, 2 * pi + 1]
            Ncur = {}
            for gi in grps:
                Asb = self.AsbG[gi % 2]
                for hl in range(4):
                    h = 4 * gi + hl
                    hp, pb = h // 2, 64 * (h % 2)
                    ps = self.psA[h % 2]
                    self.new_round(ps)
                    pv = ps[:, :].rearrange("p (a b) -> p a b", b=128)
                    for bk in range(2):
                        MM(ps, pv[0:L, 2 * bk:2 * bk + 2, 0:L], BK[pb:pb + 64, hp, bk, cs], AR[pb:pb + 64, hp, :, cs], pb, 64, 0, L,
                           r=[BK, AR])
                    A("dve", lambda e, pv=pv, hl=hl, Asb=Asb: e.tensor_tensor(out=Asb[0:L, hl, :, 0:L], in0=pv[0:L, :, 0:L],
                                                                             in1=self.maskA[0:L, :, 0:L], op=ALU.mult),
                      r=[ps, self.maskA], w=[Asb])
            for k2 in range(2):
                ps = self.psN[k2]
                self.new_round(ps)
                pv = ps[:, :].rearrange("p (a b) -> p a b", b=128)
                for gi in grps:
                    for j in range(2):
                        h = 4 * gi + 2 * j + k2
                        hp, pb = h // 2, 64 * (h % 2)
                        MM(ps, pv[0:L, 2 * (gi % 2) + j, 0:L], AR[pb:pb + 64, hp, 0, cs], BK[pb:pb + 64, hp, 0, cs], pb, 64, 0, L,
                           r=[BK, AR])
                for gi in grps:
                    N0 = self.NsbG[0][gi % 2]
                    sl0 = 2 * (gi % 2)
                    A("dve" if gi % 2 == 0 else "act", (lambda e, pv=pv, N0=N0, sl0=sl0, k2=k2: e.tensor_tensor(
                        out=N0[0:L, k2:4:2, 1, 0:L], in0=pv[0:L, sl0:sl0 + 2, 0:L],
                        in1=bc(self.maskAT[0:L, 0:L].unsqueeze(1), [L, 2, L]), op=ALU.mult)) if gi % 2 == 0 else
                      (lambda e, pv=pv, N0=N0, sl0=sl0, k2=k2: e.activation(out=N0[0:L, k2:4:2, 1, 0:L], in_=pv[0:L, sl0:sl0 + 2, 0:L],
                                                                            func=AF.Copy)),
                      r=[ps, self.maskAT], w=[N0])
            for gi in grps:
                N0 = self.NsbG[0][gi % 2]
                Asb = self.AsbG[gi % 2]
                if gi % 2 == 1:
                    A("pool", lambda e, N0=N0: e.tensor_tensor(out=N0[0:L, :, 1, 0:L], in0=N0[0:L, :, 1, 0:L],
                                                               in1=bc(self.maskAT[0:L, 0:L].unsqueeze(1), [L, 4, L]), op=ALU.mult),
                      r=[N0, self.maskAT], w=[N0])
                A("pool", lambda e, N0=N0, Asb=Asb: e.tensor_copy(out=N0[0:L, :, 0, 0:L], in_=Asb[0:L, :, 0, 0:L]), r=[Asb], w=[N0])
                Ncur[gi] = N0
            for gi in grps:
                Asb, Xb = self.AsbG[gi % 2], self.XbG[gi % 2]
                psX = psXs[gi % 2]
                pxv = psX[:, 0:256].rearrange("p (a b) -> p a b", b=64)
                self.new_round(psX)
                for hl in range(4):
                    h = 4 * gi + hl
                    hp, pb = h // 2, 64 * (h % 2)
                    STb = self.STbG[gi]
                    MM(psX, pxv[0:L, hl, :], AR[pb:pb + 64, hp, 0, cs], STb[pb:pb + 64, hp % 2, :], pb, 64, 0, L, r=[AR, STb])
                    MM(psX, pxv[0:L, hl, :], Asb[0:L, hl, 2, 0:L], Vt[0:L, h * 64:(h + 1) * 64], 0, L, 0, L, r=[Asb, Vt])
                A("act", lambda e, Xb=Xb, pxv=pxv: e.activation(out=Xb[0:L, :, :], in_=pxv[0:L, :, :], func=AF.Copy), r=[psX], w=[Xb])
            for k in range(nst):
                for gi in grps:
                    Xb = self.XbG[gi % 2]
                    psX = psXs[gi % 2]
                    pxv = psX[:, 0:256].rearrange("p (a b) -> p a b", b=64)
                    Nc = Ncur[gi]
                    for hl in range(4):
                        MM(psX, pxv[0:L, hl, :], Nc[0:L, hl, 0, 0:L], Xb[0:L, hl, :], 0, L, 0, L, r=[Nc, Xb])
                    (A("dve", lambda e, Xb=Xb, pxv=pxv: e.tensor_copy(out=Xb[0:L, :, :], in_=pxv[0:L, :, :]), r=[psX], w=[Xb])
                     if gi % 2 == 0 else
                     A("act", lambda e, Xb=Xb, pxv=pxv: e.activation(out=Xb[0:L, :, :], in_=pxv[0:L, :, :], func=AF.Copy), r=[psX], w=[Xb]))
                if k < nst - 1:
                    for gi in grps:
                        Nc = Ncur[gi]
                        Nn = self.NsbG[(k + 1) % 2][gi % 2]
                        for pr in range(2):
                            ps = self.psN[(2 * gi + pr) % 2]
                            self.new_round(ps)
                            pv = ps[:, :].rearrange("p (a b) -> p a b", b=128)
                            for k2 in range(2):
                                hl = 2 * pr + k2
                                MM(ps, pv[0:L, 2 * k2, 0:L], Nc[0:L, hl, 1, 0:L], Nc[0:L, hl, 0, 0:L], 0, L, 0, L, r=[Nc])
                                MM(ps, pv[0:L, 2 * k2 + 1, 0:L], Nc[0:L, hl, 0, 0:L], Nc[0:L, hl, 1, 0:L], 0, L, 0, L, r=[Nc])
                            if pr == 0:
                                A("dve", lambda e, pv=pv, pr=pr, Nn=Nn: e.tensor_copy(
                                    out=Nn[0:L, 2 * pr:2 * pr + 2, :, 0:L].rearrange("p a b c -> p (a b) c"), in_=pv[0:L, :, 0:L]),
                                  r=[ps], w=[Nn])
                            else:
                                A("act", lambda e, pv=pv, pr=pr, Nn=Nn: e.activation(
                                    out=Nn[0:L, 2 * pr:2 * pr + 2, :, 0:L].rearrange("p a b c -> p (a b) c"), in_=pv[0:L, :, 0:L],
                                    func=AF.Copy), r=[ps], w=[Nn])
                        Ncur[gi] = Nn
            for gi in grps:
                Asb, Xb = self.AsbG[gi % 2], self.XbG[gi % 2]
                STb, STf = self.STbG[gi], self.STfG[gi]
                psY = self.nextG()
                pyv = psY[:, 0:256].rearrange("p (a b) -> p a b", b=128)
                self.new_round(psY)
                for hl in range(4):
                    h = 4 * gi + hl
                    hp, pb = h // 2, 64 * (h % 2)
                    q = hl // 2
                    MM(psY, pyv[pb:pb + 64, q, 0:L], STb[pb:pb + 64, hp % 2, :], AR[pb:pb + 64, hp, 1, cs], pb, 64, pb, 64, r=[STb, AR])
                    MM(psY, pyv[pb:pb + 64, q, 0:L], Xb[0:L, hl, :], Asb[0:L, hl, 1, 0:L], 0, L, pb, 64, r=[Xb, Asb])
                    MM(psY, pyv[pb:pb + 64, q, 0:L], Vt[0:L, h * 64:(h + 1) * 64], Asb[0:L, hl, 3, 0:L], 0, L, pb, 64, r=[Vt, Asb])
                A("act", lambda e, gi=gi, pyv=pyv: e.activation(out=YT[:, 2 * gi:2 * gi + 2, cs], in_=pyv[:, :, 0:L], func=AF.Copy),
                  r=[psY], w=[YT])
                ps = self.nextG()
                psv = ps[:, 0:128].rearrange("p (a b) -> p a b", b=64)
                self.new_round(ps)
                for hl in range(4):
                    h = 4 * gi + hl
                    hp, pb = h // 2, 64 * (h % 2)
                    q = hl // 2
                    MM(ps, psv[pb:pb + 64, q, :], Bh[0:L, h * 64:(h + 1) * 64], Xb[0:L, hl, :], 0, L, pb, 64, r=[Bh, Xb])
                    MM(ps, psv[pb:pb + 64, q, :], Kh[0:L, h * 64:(h + 1) * 64], Vt[0:L, h * 64:(h + 1) * 64], 0, L, pb, 64, r=[Kh, Vt])
                sl = slice(2 * gi, 2 * gi + 2)
                A("dve", lambda e, sl=sl, STf=STf: e.tensor_tensor(out=STf[:, :, :], in0=STf[:, :, :],
                                                                   in1=bc(WL[:, sl, b:b + 1], [128, 2, 64]), op=ALU.mult),
                  r=[STf, WL], w=[STf])
                A("dve", lambda e, STf=STf, psv=psv: e.tensor_tensor(out=STf[:, :, :], in0=STf[:, :, :], in1=psv[:, :, :], op=ALU.add),
                  r=[STf, ps], w=[STf])
                A("act", lambda e, STf=STf, STb=STb: e.activation(out=STb[:, :, :], in_=STf[:, :, :], func=AF.Copy), r=[STf], w=[STb])

    def norm_T(self, ti, nt, vec_ap, dst_fn, load_from):
        A = self.A
        Xt, xr, st1 = self.Xt, self.xr, self.st1
        self.DMA("sp", Xt[:nt, :], load_from, Xt, r=[self.xbuf[ti]], w=[Xt])
        A("pool", lambda e: e.memset(st1[:], 0.0), w=[st1])
        A("act", lambda e: e.activation(out=self.junk[:nt, :], in_=Xt[:nt, :], func=AF.Square, accum_out=st1[:nt, 0:1]),
          r=[Xt, st1], w=[self.junk, st1])
        A("act", lambda e: e.activation(out=st1[:nt, 1:2], in_=st1[:nt, 0:1], func=AF.Ln, scale=1.0 / D, bias=RMS_EPS),
          r=[st1], w=[st1])
        A("act", lambda e: e.activation(out=st1[:nt, 2:3], in_=st1[:nt, 1:2], func=AF.Exp, scale=-0.5), r=[st1], w=[st1])
        A("act", lambda e: e.activation(out=xr[:nt, :], in_=Xt[:nt, :], func=AF.Identity, scale=st1[:nt, 2:3]),
          r=[Xt, st1], w=[xr])
        for half in range(2):
            ps = self.nextG()
            psv = ps[:, :].rearrange("p (a b) -> p a b", b=128)
            for q in range(4):
                c = half * 4 + q
                A("pe", lambda e, psv=psv, q=q, c=c: e.transpose(out=psv[:, q, :nt], in_=xr[:nt, c * 128:(c + 1) * 128],
                                                                 identity=self.identf[:nt, :nt]),
                  r=[xr, self.identf], w=[ps])
            for q in range(4):
                c = half * 4 + q
                dap, dT = dst_fn(c)
                if q % 2 == 0:
                    A("act", lambda e, psv=psv, q=q, c=c, dap=dap: e.activation(out=dap, in_=psv[:, q, :nt], func=AF.Identity,
                                                                                scale=vec_ap[:, c:c + 1]), r=[ps, self.BV], w=[dT])
                else:
                    A("dve", lambda e, psv=psv, q=q, c=c, dap=dap: e.tensor_scalar(out=dap, in0=psv[:, q, :nt], scalar1=vec_ap[:, c:c + 1],
                                                                                   scalar2=None, op0=ALU.mult), r=[ps, self.BV], w=[dT])

    def kv_phase(self):
        A = self.A
        self.s.fence()
        self.ropeP = Al("ropeP", self.F[1].h[:, :, :].rearrange("p a b -> p (a b)").rearrange("p (t k d) -> p t k d", k=2, d=32))
        self.DMA("sp", self.ropeP[:, :, :, :], self.c_ropeP[:, 0:NTILE, :, :], self.ropeP, w=[self.ropeP])
        wkv = self.w_kv.rearrange("(c p) e -> p c e", p=128)
        hnT = self.MX[0]
        KTt = self.MX[1]
        Kf = self.xr
        Vf = Vw(self.F[6], self.F[6].h[:, :, :].rearrange("p a b -> p (a b)"))
        tmpr = Vw(self.F[0], self.F[0].h[:, :, :].rearrange("p a b -> p (a b)"))
        Vb = Vw(self.szT, self.szT.h[:, :, :].rearrange("p a b -> p (a b)"))
        bf = lambda ap: ap.bitcast(BF16)
        self.Wv2 = []
        for c in range(NCH):
            Fk = self.F[2 + c // 2]
            self.Wv2.append(Al(f"Wv2_{c}", bf(Fk.h[:, :, :].rearrange("p a b -> p (a b)"))[:, (c % 2) * 1024:(c % 2 + 1) * 1024]))
        for g in range(3):
            Wk = self.W[g]
            for c in range(NCH):
                self.DMA("pool", Wk[:, c, :], wkv[:, c, g * D:(g + 1) * D], Wk, w=[Wk])
            for c in range(NCH):
                src = wkv[:, c, 3 * D + g * D:3 * D + (g + 1) * D]
                if g < 2:
                    self.DMA("pool", self.W[3 + g][:, c, :], src, self.W[3 + g], w=[self.W[3 + g]])
                else:
                    self.DMA("pool", self.Wv2[c][:, :], src, self.Wv2[c], w=[self.Wv2[c]])
        for ti in range(NTILE + 1):
            nt = 128 if ti < NTILE else NS
            self.kv_norm(ti, nt, hnT)
            for g in range(3):
                self.kv_tile(g, ti, nt, self.W[g], None, hnT, KTt, Kf, Vf, tmpr, Vb)

    def kv_norm(self, ti, nt, hnT):
        self.norm_T(ti, nt, self.BV[:, 0, :], lambda c: (hnT[:, c, :nt], hnT), self.x_dram(ti))

    def kv_tile(self, g, ti, nt, Wk, Wv, hnT, KTt, Kf, Vf, tmpr, Vb):
        A = self.A
        if True:
            if True:
                if ti < NTILE:
                    cosb = bc(self.ropeP[:nt, ti, 0, :].unsqueeze(1), [nt, 8, 32])
                    sinb = bc(self.ropeP[:nt, ti, 1, :].unsqueeze(1), [nt, 8, 32])
                    rpT = self.ropeP
                else:
                    cosb = bc(self.ropeS[:nt, 0, :].unsqueeze(1), [nt, 8, 32])
                    sinb = bc(self.ropeS[:nt, 1, :].unsqueeze(1), [nt, 8, 32])
                    rpT = self.ropeS
                for half in range(2):
                    ps = self.nextG()
                    for c in range(NCH):
                        A("pe", lambda e, ps=ps, c=c, half=half: e.matmul(out=ps[:nt, :], lhsT=hnT[:, c, :nt],
                                                                          rhs=Wk[:, c, half * 512:(half + 1) * 512],
                                                                          start=(c == 0), stop=(c == NCH - 1)), r=[hnT, Wk], w=[ps])
                    pv = ps[:nt, :].rearrange("p (h t d) -> p h t d", t=2, d=32)
                    ov = Kf[:nt, half * 512:(half + 1) * 512].rearrange("p (h t d) -> p h t d", t=2, d=32)
                    tv = tmpr[:nt, 0:512].rearrange("p (h t d) -> p h t d", t=2, d=32)
                    A("dve", lambda e, pv=pv, ov=ov: e.tensor_tensor(out=ov[:, :, 0, :], in0=pv[:, :, 0, :], in1=cosb, op=ALU.mult),
                      r=[ps, rpT], w=[Kf])
                    A("dve", lambda e, pv=pv, tv=tv: e.tensor_tensor(out=tv[:, :, 0, :], in0=pv[:, :, 1, :], in1=sinb, op=ALU.mult),
                      r=[ps, rpT], w=[tmpr])
                    A("dve", lambda e, pv=pv, ov=ov: e.tensor_tensor(out=ov[:, :, 1, :], in0=pv[:, :, 1, :], in1=cosb, op=ALU.mult),
                      r=[ps, rpT], w=[Kf])
                    A("dve", lambda e, pv=pv, tv=tv: e.tensor_tensor(out=tv[:, :, 1, :], in0=pv[:, :, 0, :], in1=sinb, op=ALU.mult),
                      r=[ps, rpT], w=[tmpr])
                    A("pool", lambda e, ov=ov, tv=tv: e.tensor_tensor(out=ov[:, :, 0, :], in0=ov[:, :, 0, :], in1=tv[:, :, 0, :],
                                                                      op=ALU.subtract), r=[Kf, tmpr], w=[Kf])
                    A("pool", lambda e, ov=ov, tv=tv: e.tensor_tensor(out=ov[:, :, 1, :], in0=ov[:, :, 1, :], in1=tv[:, :, 1, :],
                                                                      op=ALU.add), r=[Kf, tmpr], w=[Kf])
                for half in range(2):
                    ps = self.nextG()
                    for c in range(NCH):
                        if g < 2:
                            wv_ap, wv_t = self.W[3 + g][:, c, half * 512:(half + 1) * 512], self.W[3 + g]
                        else:
                            wv_ap, wv_t = self.Wv2[c][:, half * 512:(half + 1) * 512], self.Wv2[c]
                        A("pe", lambda e, ps=ps, c=c, wv_ap=wv_ap: e.matmul(out=ps[:nt, :], lhsT=hnT[:, c, :nt], rhs=wv_ap,
                                                                             start=(c == 0), stop=(c == NCH - 1)), r=[hnT, wv_t], w=[ps])
                    A("act", lambda e, ps=ps, half=half: e.activation(out=Vf[:nt, half * 512:(half + 1) * 512], in_=ps[:nt, :],
                                                                      func=AF.Copy), r=[ps], w=[Vf])
                if ti == NTILE:
                    dk, dv = self.o_kvs[g][:, 0, :], self.o_kvs[g][:, 1, :]
                else:
                    first = NTILE - (1, 4, 16)[g]
                    if ti >= first:
                        r0 = (ti - first) * 128
                        dk, dv = self.o_kvp[g][r0:r0 + 128, 0, :], self.o_kvp[g][r0:r0 + 128, 1, :]
                    else:
                        dk = dv = None
                if dk is not None:
                    self.DMA("sp", dk, Kf[:nt, :], Kf, r=[Kf])
                    self.DMA("sp", dv, Vf[:nt, :], Vf, r=[Vf])
                t0 = ti * 128
                A("pool", lambda e: e.tensor_copy(out=Vb[:nt, :], in_=Vf[:nt, :]), r=[Vf], w=[Vb])
                self.DMA("sp", self.d_v[g, t0:t0 + nt, :], Vb[:nt, :], Vb, r=[Vb], w=[self.vbuf[g]])
                A("act", lambda e: e.activation(out=self.junk[:nt, :], in_=Kf[:nt, :], func=AF.Copy), r=[Kf], w=[self.junk])
                ps = self.nextG()
                psb = ps.h.bitcast(BF16)
                for c in range(NCH):
                    A("pe", lambda e, psb=psb, c=c: e.transpose(out=psb[:, c * 128:c * 128 + nt], in_=self.junk[:nt, c * 128:(c + 1) * 128],
                                                                identity=self.identb[:nt, :nt]), r=[self.junk, self.identb], w=[ps])
                A("act", lambda e, psb=psb: e.activation(out=KTt[:, :, :nt], in_=psb[:, :].rearrange("p (a b) -> p a b", b=128)[:, :, :nt],
                                                         func=AF.Copy), r=[ps], w=[KTt])
                self.DMA("sp", self.d_kT[g, :, :, t0:t0 + nt].rearrange("hp p t -> p hp t"), KTt[:, :, :nt], KTt, r=[KTt],
                         w=[self.kTbuf[g]])
                if ti == NTILE:
                    A("pool", lambda e, g=g: e.tensor_copy(out=self.KTs[:, g, :, :], in_=KTt[:, :, :nt]), r=[KTt], w=[self.KTs])

    def setup_b(self):
        self.s.fence()
        bf = lambda ap: ap.bitcast(BF16)
        fl3 = lambda t: t.h[:, :, :].rearrange("p a b -> p (a b)")
        fl4 = lambda t: t.h[:, :, :, :].rearrange("p a b c -> p (a b c)")
        Wf = [fl3(t) for t in self.W]
        self.XNp = [Al(f"XNp{c}", Wf[c // 4][:, (c % 4) * 2048:(c % 4 + 1) * 2048]) for c in range(8)]
        self.OGp = [Al(f"OGp{c}", Wf[2 + c // 4][:, (c % 4) * 2048:(c % 4 + 1) * 2048]) for c in range(8)]
        self.Wout = Al("Wout", self.W[4].h[:, :, :])
        F = self.F
        self.QT = [Al(f"QT{g}", bf(fl3(F[g]))) for g in range(3)]
        self.KTp = [Al(f"KTp{g}", bf(fl3(F[3 + g]))) for g in range(3)]
        self.Vblk = [Al(f"Vblk{g}", bf(fl3(t)).rearrange("p (m e) -> p m e", e=128)) for g, t in enumerate((F[6], F[7], self.rT))]
        self.numT = Al("numT", fl4(self.Asb).bitcast(F32))
        self.denT = [Al(f"denT{i}", fl4(self.Nsb[i]).bitcast(F32)) for i in range(2)]
        self.Wqs = [Al(f"Wq{j}", t.h[:, :, :]) for j, t in enumerate((self.MX[0], self.MX[1], self.szT, self.vTb))]
        self.Wqr = Al("Wqr", self.BhT.h[:, :, :])
        self.zTp = Al("zTp", bf(fl3(self.kT)))
        self.cosT = Al("cosT", bf(fl3(self.vT)))
        self.sinT = Al("sinT", fl4(self.AR))
        self.tmpq = [Al("tmpq0", fl3(self.KhT).bitcast(F32)), Al("tmpq1", fl3(self.OT).bitcast(F32))]
        xb = self.Xb.h[:, :, :].rearrange("p a b -> p (a b)")
        self.Eb = [Al(f"Eb{i}", xb[:, i * 256:(i + 1) * 256].rearrange("p (a b) -> p a b", b=128)) for i in range(2)]
        self.Eb.append(Al("Eb2", self.WL.h[:, :, :].rearrange("p a b -> p (a b)").bitcast(BF16).rearrange("p (a b) -> p a b", b=128)))
        self.Eb.append(Al("Eb3", self.shiftT.h[:, :, :].rearrange("p a b -> p (a b)").bitcast(BF16).rearrange("p (a b) -> p a b", b=128)))
        wla = self.Wla.h[:, :, :].rearrange("p a b -> p (a b)")
        self.Eb.append(Al("Eb4", wla[:, 0:256].rearrange("p (a b) -> p a b", b=128)))
        self.Eb.append(Al("Eb5", wla[:, 256:512].rearrange("p (a b) -> p a b", b=128)))
        self.blk_par = 0
        self.pend_ctx = None
        self.flnb = Al("flnb", fl4(self.BK).bitcast(F32))
        self.DMA("pool", self.cosT[:, :], self.c_ropeT[:, 0, 0:SEQ], self.cosT, w=[self.cosT])
        self.DMA("pool", self.sinT[:, :], self.c_ropeT[:, 1, 0:SEQ], self.sinT, w=[self.sinT])
        self.DMA("pool", self.ropeTs[:], self.c_ropeT[:, :, SEQ:SEQ + NS], self.ropeTs, w=[self.ropeTs])
        self.DMA("sp", self.flnb[:, :], self.final_ln.partition_broadcast(128), self.flnb, w=[self.flnb])

    def layer_b(self, lb):
        if lb == 0:
            self.setup_b()
        for ti in range(NTILE + 1):
            self.b_norm_tile(lb, ti, 128 if ti < NTILE else NS)
        wo = self.b_w_out[lb].rearrange("(c p) e -> p c e", p=128)
        for c in range(NCH):
            self.DMA("pool", self.Wout[:, c, :], wo[:, c, :], self.Wout, w=[self.Wout])
        for hp in range(NCH):
            self.b_hp(lb, hp)
        self.b_samples(lb)
        for ti in range(NTILE + 1):
            self.b_out_tile(lb, ti, 128 if ti < NTILE else NS)

    def b_norm_tile(self, lb, ti, nt):
        if ti < NTILE:
            dst = lambda c: (self.XNp[c][:, ti * 128:(ti + 1) * 128], self.XNp[c])
        else:
            dst = lambda c: (self.XNs[:, c, :], self.XNs)
        self.norm_T(ti, nt, self.BV[:, 1 + lb, :], dst, self.x_dram(ti))

    def b_proj(self, w, hp, ch, evac):
        A = self.A
        ps = self.nextG()
        n = 512 if ch < 4 else NS
        for c in range(NCH):
            rhs = self.XNp[c][:, ch * 512:(ch + 1) * 512] if ch < 4 else self.XNs[:, c, :]
            rt = self.XNp[c] if ch < 4 else self.XNs
            A("pe", lambda e, ps=ps, c=c, rhs=rhs: e.matmul(out=ps[:, 0:n], lhsT=w[:, c, :], rhs=rhs, start=(c == 0), stop=(c == NCH - 1)),
              r=[w, rt], w=[ps])
        evac(ps, n)
        return ps

    def b_hp(self, lb, hp):
        A = self.A
        src = self.b_w_in[lb].rearrange("(c p) e -> p c e", p=128)
        for j in range(4):
            col0 = j * D + hp * 128
            self.DMA("pool", self.Wqs[j][:, :, :], src[:, :, col0:col0 + 128], self.Wqs[j], w=[self.Wqs[j]])
        tq0, tq1 = self.tmpq
        for g in range(3):
            w, wr = self.Wqs[g], self.Wqr
            for ch in range(5):
                self.b_q_chunk(g, hp, ch, w, wr)
        for ch in range(5):
            self.b_z_chunk(hp, ch)
        for g in range(3):
            self.DMA("sp", self.KTp[g][:, :], self.d_kT[g, hp, :, 0:SEQ], self.KTp[g], r=[self.kTbuf[g]], w=[self.KTp[g]])
            dv = self.d_v[g, 0:SEQ, hp * 128:(hp + 1) * 128]
            if g == 0:
                self.DMA("sp", self.Vblk[0][:, :, :], dv.rearrange("(m i) e -> i m e", i=128), self.Vblk[0], r=[self.vbuf[0]], w=[self.Vblk[0]])
            elif g == 1:
                dvv = dv.rearrange("(m i r) e -> i r m e", i=128, r=4)
                for r_ in range(4):
                    self.DMA("sp", self.Vblk[1][:, 4 * r_:4 * r_ + 4, :], dvv[:, r_, :, :], self.Vblk[1], r=[self.vbuf[1]], w=[self.Vblk[1]])
            else:
                self.DMA("sp", self.Vblk[2][:, :, :], dv.rearrange("(i r) e -> i r e", r=16), self.Vblk[2], r=[self.vbuf[2]], w=[self.Vblk[2]])
        for g in range(3):
            if g == 0:
                blocks = [((m * 128, (m + 1) * 128, 1), m, (m - 1) if m > 0 else None) for m in range(16)]
            elif g == 1:
                blocks = [((r_ + 512 * m, 512 * (m + 1), 4), 4 * r_ + m, (4 * r_ + m - 1) if m > 0 else None)
                          for r_ in range(4) for m in range(4)]
            else:
                blocks = [((r_, SEQ, 16), r_, None) for r_ in range(16)]
            for bi, (qs, own, prev) in enumerate(blocks):
                pq = None
                if prev is not None:
                    pq = blocks[bi - 1][0]
                ctx = self.b_block(g, hp, qs, own, prev, pq)
                if self.pend_ctx is not None:
                    self.b_block_pv(self.pend_ctx)
                self.pend_ctx = ctx
        self.b_block_pv(self.pend_ctx)
        self.pend_ctx = None
        numT, zTp = self.numT, self.zTp
        for i in range(2):
            A("act", lambda e, i=i: e.activation(out=self.denT[i][:, :], in_=self.denT[i][:, :], func=AF.Ln), r=[self.denT[i]], w=[self.denT[i]])
            A("act", lambda e, i=i: e.activation(out=self.denT[i][:, :], in_=self.denT[i][:, :], func=AF.Exp, scale=-1.0),
              r=[self.denT[i]], w=[self.denT[i]])
            A("dve", lambda e, i=i: e.tensor_tensor(out=numT[:, i * 1024:(i + 1) * 1024], in0=numT[:, i * 1024:(i + 1) * 1024],
                                                    in1=self.denT[i][:, :], op=ALU.mult), r=[numT, self.denT[i]], w=[numT])
        A("dve", lambda e: e.tensor_tensor(out=self.OGp[hp][:, :], in0=numT[:, :], in1=zTp[:, :], op=ALU.mult),
          r=[numT, zTp], w=[self.OGp[hp]])

    def b_q_chunk(self, g, hp, ch, w, wr):
        A = self.A
        tq0, tq1 = self.tmpq
        n = 512 if ch < 4 else NS
        if ch < 4:
            cs_, sn_ = self.cosT[:, ch * 512:(ch + 1) * 512], self.sinT[:, ch * 512:(ch + 1) * 512]
            ct, st_ = self.cosT, self.sinT
            dst, dT = self.QT[g][:, ch * 512:(ch + 1) * 512], self.QT[g]
        else:
            cs_, sn_ = self.ropeTs[:, 0, :], self.ropeTs[:, 1, :]
            ct = st_ = self.ropeTs
            dst, dT = self.QTs[:, hp, g, :], self.QTs

        Qb = self.Wqr
        Qbf = Qb[:, :, :].rearrange("p a b -> p (a b)")

        def ev0(ps, n):
            A("act", lambda e: e.activation(out=Qbf[:, 0:n], in_=ps[:, 0:n], func=AF.Copy), r=[ps], w=[Qb])
            A("dve", lambda e: e.tensor_tensor(out=tq0[:, 0:n], in0=ps[:, 0:n], in1=cs_, op=ALU.mult), r=[ps, ct], w=[tq0])
        self.b_proj(w, hp, ch, ev0)
        ps2 = self.nextG()
        A("pe", lambda e: e.matmul(out=ps2[:, 0:n], lhsT=self.prot[:, :], rhs=Qbf[:, 0:n], start=True, stop=True),
          r=[self.prot, Qb], w=[ps2])
        A("dve", lambda e: e.tensor_tensor(out=tq1[:, 0:n], in0=ps2[:, 0:n], in1=sn_, op=ALU.mult), r=[ps2, st_], w=[tq1])
        A("pool", lambda e: e.tensor_tensor(out=dst, in0=tq0[:, 0:n], in1=tq1[:, 0:n], op=ALU.add), r=[tq0, tq1], w=[dT])

    def b_z_chunk(self, hp, ch):
        A = self.A
        tq0 = self.tmpq[0]
        if ch < 4:
            dst, dT = self.zTp[:, ch * 512:(ch + 1) * 512], self.zTp
        else:
            dst, dT = self.zTs[:, hp, :], self.zTs

        def ev(ps, n):
            A("act", lambda e: e.activation(out=tq0[:, 0:n], in_=ps[:, 0:n], func=AF.Exp, scale=-1.0), r=[ps], w=[tq0])
            A("act", lambda e: e.activation(out=tq0[:, 0:n], in_=tq0[:, 0:n], func=AF.Ln, bias=1.0), r=[tq0], w=[tq0])
            A("act", lambda e: e.activation(out=tq0[:, 0:n], in_=tq0[:, 0:n], func=AF.Exp, scale=-1.0), r=[tq0], w=[tq0])
            A("dve", lambda e: e.tensor_tensor(out=dst, in0=ps[:, 0:n], in1=tq0[:, 0:n], op=ALU.mult), r=[ps, tq0], w=[dT])
        self.b_proj(self.Wqs[3], hp, ch, ev)

    def b_block(self, g, hp, qs, own, prev, pqs):
        A = self.A
        MM = self.MM
        KT, QT, Vb = self.KTp[g], self.QT[g], self.Vblk[g]
        qsl = slice(qs[0], qs[1], qs[2])
        kb0 = 0 if prev is not None else 1
        self.blk_par = (self.blk_par + 1) % 3
        par = self.blk_par
        sbanks = ((self.psA[0], self.psA[1]), (self.psX, self.psY), (self.psG[0], self.psG[1]))[par]
        Ebs = self.Eb[2 * par:2 * par + 2]
        for hh in range(2):
            pb = 64 * hh
            ps = sbanks[hh]
            pv = ps[:, 0:256].rearrange("p (a b) -> p a b", b=128)
            if prev is not None:
                psl = slice(pqs[0], pqs[1], pqs[2])
                A("pe", lambda e, pv=pv, pb=pb, psl=psl: e.matmul(out=pv[:, 0, :], lhsT=KT[pb:pb + 64, psl], rhs=QT[pb:pb + 64, qsl],
                                                                  start=True, stop=True), r=[KT, QT], w=[ps])
            A("pe", lambda e, pv=pv, pb=pb: e.matmul(out=pv[:, 1, :], lhsT=KT[pb:pb + 64, qsl], rhs=QT[pb:pb + 64, qsl],
                                                     start=True, stop=True), r=[KT, QT], w=[ps])
            Eb = Ebs[hh]
            A("act", lambda e, pv=pv, Eb=Eb: e.activation(out=Eb[:, kb0:2, :], in_=pv[:, kb0:2, :], func=AF.Exp, scale=0.125),
              r=[ps], w=[Eb])
            A("dve" if hh == 0 else "pool", lambda e, Eb=Eb: e.tensor_tensor(out=Eb[:, kb0:2, :], in0=Eb[:, kb0:2, :],
                                                                             in1=self.maskP[:, kb0:2, :], op=ALU.mult),
              r=[Eb, self.maskP], w=[Eb])
        return (g, qs, own, prev, kb0, Ebs, qsl)

    def b_block_pv(self, ctx):
        A = self.A
        MM = self.MM
        g, qs, own, prev, kb0, Ebs, qsl = ctx
        Vb = self.Vblk[g]
        psO = self.nextN()
        self.new_round(psO)
        po = psO[:, 0:256].rearrange("p (a b) -> p a b", b=128)
        for hh in range(2):
            pb = 64 * hh
            Eb = Ebs[hh]
            for kb in range(kb0, 2):
                bidx = prev if kb == 0 else own
                MM(psO, po[pb:pb + 64, 0, :], Vb[:, bidx, pb:pb + 64], Eb[:, kb, :], 0, 128, pb, 64, r=[Vb, Eb])
                MM(psO, po[pb:pb + 64, 1, :], self.ones_b[:, 0:64], Eb[:, kb, :], 0, 128, pb, 64, r=[self.ones_b, Eb])
        numT = self.numT
        if g == 0:
            A("act", lambda e: e.activation(out=numT[:, qsl], in_=po[:, 0, :], func=AF.Copy), r=[psO], w=[numT])
        else:
            A("dve", lambda e: e.tensor_tensor(out=numT[:, qsl], in0=po[:, 0, :], in1=numT[:, qsl], op=ALU.add), r=[psO, numT], w=[numT])
        pieces = []
        q0, q1, st = qs
        n_lo = len(range(q0, min(q1, 1024), st)) if q0 < 1024 else 0
        if n_lo > 0:
            pieces.append((0, slice(q0, min(q1, 1024), st), slice(0, n_lo)))
        if n_lo < 128:
            first_hi = q0 + n_lo * st
            pieces.append((1, slice(first_hi - 1024, q1 - 1024, st), slice(n_lo, 128)))
        for (hf, dsl, csl) in pieces:
            dT = self.denT[hf]
            if g == 0:
                A("act", lambda e, dT=dT, dsl=dsl, csl=csl: e.activation(out=dT[:, dsl], in_=po[:, 1, csl], func=AF.Copy), r=[psO], w=[dT])
            else:
                A("dve", lambda e, dT=dT, dsl=dsl, csl=csl: e.tensor_tensor(out=dT[:, dsl], in0=po[:, 1, csl], in1=dT[:, dsl], op=ALU.add),
                  r=[psO, dT], w=[dT])

    def setup_samples(self):
        bf = lambda ap: ap.bitcast(BF16)
        fl3 = lambda t: t.h[:, :, :].rearrange("p a b -> p (a b)")
        F = self.F
        f0, f1, f2, f3, f4, f5, f6 = [bf(fl3(F[k])) for k in range(7)]
        self.Kc = [Al("Kc0", f0[:, 0:1024]), Al("Kc1", f0[:, 1024:2048])]
        self.Vc = [Al("Vc0", f1[:, 0:1024]), Al("Vc1", f1[:, 1024:2048])]
        self.KTc = Al("KTc", f2[:, 0:1024].rearrange("p (a b) -> p a b", b=128))
        self.Vs = [Al("Vs0", f3[0:64, 0:1024]), Al("Vs1", f3[0:64, 1024:2048]), Al("Vs2", f4[0:64, 0:1024])]
        self.Vn = [Al("Vn0", f5[0:4, 0:1024]), Al("Vn1", f5[0:4, 1024:2048]), Al("Vn2", f4[0:4, 1024:2048])]
        self.Ec = Al("Ec", f6[:, 0:64].rearrange("p (a b c) -> p a b c", b=2, c=4))
        self.En = Al("En", f6[0:4, 64:256].rearrange("p (g a b c) -> p g a b c", a=8, b=2, c=4))
        self.Qbd = Al("Qbd", f6[:, 256:448].rearrange("p (a g b c) -> p a g b c", g=3, b=2, c=4))
        self.maskS = Al("maskS", f6[:, 448:496].rearrange("p (a c) -> p a c", c=4))
        f7 = fl3(F[7])
        self.numS = Al("numS", f7[:, 0:512].rearrange("p (a b) -> p a b", b=NS))
        self.denS = Al("denS", f7[:, 512:1024].rearrange("p (a b) -> p a b", b=NS))

    def b_samples(self, lb):
        A = self.A
        self.s.fence()
        if lb == 0:
            self.setup_samples()
        A("pool", lambda e: e.memset(self.Qbd[:, :, :, :, :], 0.0), w=[self.Qbd])
        self.DMA("pool", self.maskS[:, :, :], self.c_maskS[:, :, :], self.maskS, w=[self.maskS])
        for g in range(3):
            self.DMA("sp", self.Vs[g][:, :], self.d_v[g, SEQ:SEQ + NS, :], self.Vs[g], r=[self.vbuf[g]], w=[self.Vs[g]])
        for b in range(SB):
            self.b_sample_batch(lb, b)
        numS, denS = self.numS, self.denS
        A("dve", lambda e: e.reciprocal(out=denS[:, :, :], in_=denS[:, :, :]), r=[denS], w=[denS])
        A("dve", lambda e: e.tensor_tensor(out=numS[:, :, :], in0=numS[:, :, :], in1=denS[:, :, :], op=ALU.mult), r=[numS, denS], w=[numS])
        A("dve", lambda e: e.tensor_tensor(out=self.OGs[:], in0=numS[:, :, :], in1=self.zTs[:], op=ALU.mult), r=[numS, self.zTs], w=[self.OGs])
        self.s.fence()

    def b_sample_batch(self, lb, b):
        A = self.A
        MM = self.MM
        Qbd, QTs, Ec, En = self.Qbd, self.QTs, self.Ec, self.En
        bs = slice(ST_ * b, ST_ * b + ST_)
        for hh in range(2):
            pb = 64 * hh
            A("dve", lambda e, pb=pb, hh=hh: e.tensor_copy(out=Qbd[pb:pb + 64, :, :, hh, :], in_=QTs[pb:pb + 64, :, :, bs]),
              r=[QTs], w=[Qbd])
        for g in range(3):
            for half in range(2):
                ps = self.nextG()
                A("pe", lambda e, ps=ps, g=g, half=half: e.matmul(out=ps[0:ST_, :], lhsT=self.identb[0:NS, bs],
                                                                  rhs=self.Vs[g][0:NS, half * 512:(half + 1) * 512], start=True, stop=True),
                  r=[self.identb, self.Vs[g]], w=[ps])
                A("act", lambda e, ps=ps, g=g, half=half: e.activation(out=self.Vn[g][0:ST_, half * 512:(half + 1) * 512], in_=ps[0:ST_, :],
                                                                       func=AF.Copy), r=[ps], w=[self.Vn[g]])
        psX = self.psX
        self.new_round(psX)
        po = psX[:, 0:64].rearrange("p (s a c) -> p s a c", s=2, c=ST_)
        tiles = [(0, 0)] + [(1, r_) for r_ in range(4)] + [(2, r_) for r_ in range(4)]
        for tix, (g, r_) in enumerate(tiles):
            self.b_sample_tile(b, tix, g, r_, po)
        ps2 = self.psA[1]
        pv2 = ps2[:, 0:192].rearrange("p (g a c) -> p g a c", a=8, c=8)
        for g in range(3):
            for hp in range(NCH):
                A("pe", lambda e, g=g, hp=hp: e.matmul(out=pv2[0:ST_, g, hp, :], lhsT=self.KTs[:, g, hp, bs], rhs=Qbd[:, hp, g, :, :],
                                                       start=True, stop=True), r=[self.KTs, Qbd], w=[ps2])
        A("act", lambda e: e.activation(out=En[:, :, :, :, :].rearrange("p g a b c -> p g a (b c)"), in_=pv2[0:ST_, :, :, :], func=AF.Exp,
                                        scale=0.125), r=[ps2], w=[En])
        A("dve", lambda e: e.tensor_tensor(out=En[:, :, :, :, :].rearrange("p g a b c -> p g (a b) c"),
                                           in0=En[:, :, :, :, :].rearrange("p g a b c -> p g (a b) c"),
                                           in1=bc(self.maskS[0:ST_, 9:12, :].unsqueeze(2), [ST_, 3, 16, ST_]), op=ALU.mult),
          r=[En, self.maskS], w=[En])
        for g in range(3):
            for hp in range(NCH):
                for hh in range(2):
                    pb = 64 * hh
                    MM(psX, po[pb:pb + 64, 0, hp, :], self.Vn[g][0:ST_, hp * 128 + pb:hp * 128 + pb + 64], En[:, g, hp, hh, :], 0, ST_, pb, 64,
                       r=[self.Vn[g], En])
                    MM(psX, po[pb:pb + 64, 1, hp, :], self.ones_b[0:ST_, 0:64], En[:, g, hp, hh, :], 0, ST_, pb, 64, r=[self.ones_b, En])
        A("act", lambda e: e.activation(out=self.numS[:, :, bs], in_=po[:, 0, :, :], func=AF.Copy), r=[psX], w=[self.numS])
        A("dve", lambda e: e.tensor_copy(out=self.denS[:, :, bs], in_=po[:, 1, :, :]), r=[psX], w=[self.denS])

    def b_sample_tile(self, b, tix, g, r_, po):
        A = self.A
        MM = self.MM
        i = tix % 2
        Kc, Vc, KTc, Ec, Qbd = self.Kc[i], self.Vc[i], self.KTc, self.Ec, self.Qbd
        psX = self.psX
        if g == 0:
            src = self.ckv[0][b]
        elif g == 1:
            src = self.ckv[1][b].rearrange("(m r) k e -> m r k e", r=4)[:, r_]
        else:
            src = self.ckv[2][b].rearrange("(m r) k e -> m r k e", r=16)[:, r_]
        self.DMA("pool", Kc[:, :], src[:, 0, :], Kc, w=[Kc])
        self.DMA("pool", Vc[:, :], src[:, 1, :], Vc, w=[Vc])
        ps = self.nextG()
        psb = ps.h.bitcast(BF16)
        for c in range(NCH):
            A("pe", lambda e, psb=psb, c=c: e.transpose(out=psb[:, c * 128:(c + 1) * 128], in_=Kc[:, c * 128:(c + 1) * 128],
                                                        identity=self.identb[:, :]), r=[Kc, self.identb], w=[ps])
        A("act", lambda e, psb=psb: e.activation(out=KTc[:, :, :], in_=psb[:, :].rearrange("p (a b) -> p a b", b=128), func=AF.Copy),
          r=[ps], w=[KTc])
        ps1 = self.psA[0]
        pv = ps1[:, 0:64].rearrange("p (a c) -> p a c", c=8)
        for hp in range(NCH):
            A("pe", lambda e, hp=hp: e.matmul(out=pv[:, hp, :], lhsT=KTc[:, hp, :], rhs=Qbd[:, hp, g, :, :], start=True, stop=True),
              r=[KTc, Qbd], w=[ps1])
        A("act", lambda e: e.activation(out=Ec[:, :, :, :].rearrange("p a b c -> p a (b c)"), in_=pv[:, :, :], func=AF.Exp, scale=0.125),
          r=[ps1], w=[Ec])
        A("dve", lambda e: e.tensor_tensor(out=Ec[:, :, :, :].rearrange("p a b c -> p (a b) c"),
                                           in0=Ec[:, :, :, :].rearrange("p a b c -> p (a b) c"),
                                           in1=bc(self.maskS[:, tix, :].unsqueeze(1), [128, 16, ST_]), op=ALU.mult),
          r=[Ec, self.maskS], w=[Ec])
        for hp in range(NCH):
            for hh in range(2):
                pb = 64 * hh
                MM(psX, po[pb:pb + 64, 0, hp, :], Vc[:, hp * 128 + pb:hp * 128 + pb + 64], Ec[:, hp, hh, :], 0, 128, pb, 64, r=[Vc, Ec])
                MM(psX, po[pb:pb + 64, 1, hp, :], self.ones_b[:, 0:64], Ec[:, hp, hh, :], 0, 128, pb, 64, r=[self.ones_b, Ec])

    def b_out_tile(self, lb, ti, nt):
        A = self.A
        Xt = self.Xt
        self.DMA("sp", Xt[:nt, :], self.x_dram(ti), Xt, r=[self.xbuf[ti]], w=[Xt])
        Wo = self.Wout
        for half in range(2):
            ps = self.nextG()
            for c in range(NCH):
                if ti < NTILE:
                    lhsT, lt = self.OGp[c][:, ti * 128:(ti + 1) * 128], self.OGp[c]
                else:
                    lhsT, lt = self.OGs[:, c, :], self.OGs
                A("pe", lambda e, ps=ps, c=c, half=half, lhsT=lhsT: e.matmul(out=ps[:nt, :], lhsT=lhsT, rhs=Wo[:, c, half * 512:(half + 1) * 512],
                                                                             start=(c == 0), stop=(c == NCH - 1)), r=[lt, Wo], w=[ps])
            A("dve", lambda e, ps=ps, half=half: e.tensor_tensor(out=Xt[:nt, half * 512:(half + 1) * 512], in0=ps[:nt, :],
                                                                 in1=Xt[:nt, half * 512:(half + 1) * 512], op=ALU.add),
              r=[ps, Xt], w=[Xt])
        if lb == 1:
            st1 = self.st1
            A("pool", lambda e: e.memset(st1[:], 0.0), w=[st1])
            A("act", lambda e: e.activation(out=self.junk[:nt, :], in_=Xt[:nt, :], func=AF.Square, accum_out=st1[:nt, 0:1]),
              r=[Xt, st1], w=[self.junk, st1])
            A("act", lambda e: e.activation(out=st1[:nt, 1:2], in_=st1[:nt, 0:1], func=AF.Ln, scale=1.0 / D, bias=RMS_EPS), r=[st1], w=[st1])
            A("act", lambda e: e.activation(out=st1[:nt, 2:3], in_=st1[:nt, 1:2], func=AF.Exp, scale=-0.5), r=[st1], w=[st1])
            A("act", lambda e: e.activation(out=Xt[:nt, :], in_=Xt[:nt, :], func=AF.Identity, scale=st1[:nt, 2:3]), r=[Xt, st1], w=[Xt])
            A("dve", lambda e: e.tensor_tensor(out=Xt[:nt, :], in0=Xt[:nt, :], in1=self.flnb[:nt, :], op=ALU.mult), r=[Xt, self.flnb], w=[Xt])
        self.DMA("sp", self.x_dram(ti), Xt[:nt, :], Xt, r=[Xt], w=[self.xbuf[ti]])

    def _dump_x(self):
        pass


def _consts():
    c = {}
    c["c_ident"] = np.eye(128, dtype=np.float32)
    s = np.arange(128)[:, None]
    t = np.arange(128)[None, :]
    lt = (s < t).astype(np.float32)
    le = (s <= t).astype(np.float32)
    c["c_maskA"] = np.ascontiguousarray(np.stack([lt, le, lt, le], axis=1))
    c["c_maskAT"] = np.ascontiguousarray((t < s).astype(np.float32))
    blk = np.zeros((128, 128), np.float32)
    blk[:64, :64] = 1.0
    blk[64:, 64:] = 1.0
    c["c_blk"] = blk
    sc = np.ones((128, 2, 128), np.float32)
    sc[:, 0, 0] = 0.0
    sc[:, 1, 0::4] = 0.0
    c["c_scan"] = sc
    half = 32
    inv = (10000.0 ** (-np.arange(half, dtype=np.float32) / np.float32(half))).astype(np.float32)
    pos = np.concatenate([np.arange(SEQ, dtype=np.float32), np.tile(np.float32(2048) + np.arange(ST_, dtype=np.float32), SB)])
    ang = (pos[:, None] * inv[None, :]).astype(np.float32)
    cs = np.stack([np.cos(ang), np.sin(ang)], 1).astype(np.float32)
    rp = np.zeros((128, NTILE + 1, 2, 32), np.float32)
    rp[:, :NTILE] = cs[:SEQ].reshape(NTILE, 128, 2, 32).transpose(1, 0, 2, 3)
    rp[:NS, NTILE] = cs[SEQ:]
    c["c_ropeP"] = rp
    rt = np.zeros((128, 2, SEQ + NS), np.float32)
    pidx = np.arange(128) % 32
    rt[:, 0, :] = np.cos(ang).T[pidx]
    rt[:, 1, :] = np.sin(ang).T[pidx]
    c["c_ropeT"] = rt
    k = np.arange(128)[:, None]
    q = np.arange(128)[None, :]
    ms = np.zeros((128, 12, ST_), np.float32)
    kk_ = np.arange(128)[:, None]
    tt_ = np.arange(ST_)[None, :]
    ms[:, 0, :] = (kk_ > tt_)
    for r_ in range(4):
        ms[:, 1 + r_, :] = (tt_ == r_) & (kk_ >= 1)
        ms[:, 5 + r_, :] = (tt_ == r_) & (kk_ >= 1)
    ms[:ST_, 9, :] = (kk_[:ST_] <= tt_)
    ms[:ST_, 10, :] = (kk_[:ST_] == tt_)
    ms[:ST_, 11, :] = (kk_[:ST_] == tt_)
    c["c_maskS"] = ms
    pr_ = np.zeros((128, 128), np.float32)
    for m_ in range(128):
        if m_ % 64 < 32:
            pr_[m_ + 32, m_] = -1.0
        else:
            pr_[m_ - 32, m_] = 1.0
    c["c_prot"] = pr_
    c["c_maskP"] = np.ascontiguousarray(np.stack([(k > q), (k <= q)], 1).astype(np.float32))
    return c


def _fm(v):
    return np.ascontiguousarray(np.asarray(v, np.float32).reshape(NCH, 128).T)


def _prep_inputs(inp, ncores=NCORES):
    f = lambda k: np.ascontiguousarray(np.asarray(inp[k], dtype=np.float32))
    shared = {}
    a_vec = np.zeros((2, 128, NV, NCH), np.float32)
    for l in range(2):
        a_vec[l, :, V_LN] = _fm(inp["a_ln"][l])
        for p in range(6):
            a_vec[l, :, V_MU + p] = _fm(inp["a_mu"][l, p])
        a_vec[l, :, V_W0] = _fm(inp["a_w0"][l])
        a_vec[l, :, V_A0] = _fm(inp["a_a0"][l])
        if l == 1:
            a_vec[l, :, V_V0] = _fm(inp["a_v0"][0])
        a_vec[l, :, V_KK] = _fm(inp["a_k_k"][l])
        a_vec[l, :, V_KA] = _fm(inp["a_k_a"][l])
        a_vec[l, :, V_RK] = _fm(np.asarray(inp["a_r_k"][l]).reshape(-1))
        a_vec[l, :, V_GNW] = _fm(inp["a_gn_w"][l])
        a_vec[l, :, V_GNB] = _fm(inp["a_gn_b"][l])
    shared["a_vec"] = a_vec
    b_vec = np.zeros((128, 4, NCH), np.float32)
    b_vec[:, 0] = _fm(inp["kv_ln"])
    b_vec[:, 1] = _fm(inp["b_ln"][0])
    b_vec[:, 2] = _fm(inp["b_ln"][1])
    shared["b_vec"] = b_vec
    for k in ("w_kv", "b_w_in", "b_w_out", "final_ln"):
        shared[k] = f(k)
    for k in ("a_w_in", "a_w_out", "a_w_lora_a", "a_w_lora_b", "a_a_lora_a", "a_a_lora_b", "a_v_lora_a", "a_v_lora_b"):
        shared[k] = f(k)
    shared.update(_consts())
    xp = f("x_prompt")
    xs = f("x_sample")
    swkv = f("state_wkv")
    ssh = f("state_shift")
    maps = []
    for c in range(ncores):
        m = dict(shared)
        m["xp"] = xp[c]
        m["xs"] = np.ascontiguousarray(xs[c * SB:(c + 1) * SB].reshape(NS, D))
        m["swkv"] = np.ascontiguousarray(swkv[:, c * SB:(c + 1) * SB])
        m["sshift"] = np.ascontiguousarray(ssh[:, c * SB:(c + 1) * SB])
        for g in range(3):
            ck = np.asarray(inp[f"cache_kv_g{g}"])[c * SB:(c + 1) * SB]
            m[f"ckv{g}"] = np.ascontiguousarray(ck.reshape(SB, ck.shape[1], 2, D), dtype=np.float32)
        maps.append(m)
    return maps


_NC_CACHE = {}


def _get_nc(stage):
    if stage not in _NC_CACHE:
        _NC_CACHE[stage] = Builder(stage).build()
    return _NC_CACHE[stage]


def run_cores(inp, stage="full", cores=None):
    maps = _prep_inputs(inp, NCORES if cores is None else len(cores))
    nc = _get_nc(stage)
    res = run_bass_kernel_spmd(nc, maps, core_ids=list(range(len(maps))))
    return res.results


def kernel(**inputs):
    res = run_cores(inputs, "full")
    cat = lambda k, ax=0: np.concatenate([r[k] for r in res], axis=ax)
    y_prompt = np.stack([r["y_prompt"] for r in res], 0)
    y_sample = cat("y_sample").reshape(NCORES * SB, ST_, D)
    wkv_p = np.stack([r["wkv_p"] for r in res], 1)
    wkv_s = cat("wkv_s", 1)
    shift_p = np.stack([r["shift_p"] for r in res], 1)
    shift_s = cat("shift_s", 1)
    outs = [y_prompt, y_sample, wkv_p, wkv_s, shift_p, shift_s]
    for g, w in enumerate((128, 512, 2048)):
        outs.append(np.stack([r[f"kv{g}p"] for r in res], 0).reshape(NCORES, w, 2, 16, 64))
        outs.append(cat(f"kv{g}s").reshape(NCORES * SB, ST_, 2, 16, 64))
    return tuple(np.ascontiguousarray(o, dtype=np.float32) for o in outs)
```
